# Optimizing a Trainium2 kernel written in Bass

```python
import jax, jax.numpy as jnp
from jax import lax
import numpy as np

D_MODEL = 1024
BATCH = 4
SEQ = 8192
DEPTH = 4

CHUNK = 64
PE_DIM = 256
EPS = 1e-6
M_HEADS = 4
M_DK = 256
M_DV = 256
M_QK = M_HEADS * M_DK
M_V = M_HEADS * M_DV
CONV_K = 4
R_HEADS = 4
R_DK = 256
R_DV = 512
R_QK = R_HEADS * R_DK
R_V = R_HEADS * R_DV
ROPE_BASE = 10000.0
D_FF = 4 * D_MODEL
IN_SIZES = (M_QK, M_QK, M_V, M_V, M_HEADS, M_HEADS, R_QK, R_QK, R_V, R_V, D_MODEL, D_MODEL)
N_IN = M_QK * 2 + M_V * 2 + M_HEADS * 2 + R_QK * 2 + R_V * 2 + D_MODEL * 2
F_GATE_OFFSET = M_QK * 2 + M_V * 2 + M_HEADS

kernel_name = "hybrid_mlstm_retention_griffin_block"


def split_cols(z):
    outs = []
    off = 0
    for s in IN_SIZES:
        outs.append(z[..., off:off + s])
        off += s
    return outs


def rmsnorm(x, g):
    xf = x.astype(jnp.float32)
    y = xf * lax.rsqrt(jnp.mean(xf * xf, axis=-1, keepdims=True) + EPS)
    return (y * g.astype(jnp.float32)).astype(x.dtype)


def head_layernorm(y, g, n_heads):
    B, S, W = y.shape
    yf = y.astype(jnp.float32).reshape(B, S, n_heads, W // n_heads)
    mu = jnp.mean(yf, axis=-1, keepdims=True)
    var = jnp.mean(jnp.square(yf - mu), axis=-1, keepdims=True)
    yn = ((yf - mu) * lax.rsqrt(var + EPS)).reshape(B, S, W)
    return yn * g.astype(jnp.float32)


def causal_conv(u, w, b):
    S = u.shape[1]
    up = jnp.pad(u, ((0, 0), (CONV_K - 1, 0), (0, 0)))
    out = b
    for j in range(CONV_K):
        out = out + up[:, j:j + S, :] * w[j]
    return out


def rope(x, cos, sin):
    half = x.shape[-1] // 2
    x1, x2 = x[..., :half], x[..., half:]
    return jnp.concatenate([x1 * cos - x2 * sin, x1 * sin + x2 * cos], axis=-1)


def to_chunks(x, n_heads):
    B, S, W = x.shape
    return x.reshape(B, S // CHUNK, CHUNK, n_heads, W // n_heads).transpose(1, 0, 3, 2, 4)


def gates_to_chunks(g):
    B, S, H = g.shape
    return g.reshape(B, S // CHUNK, CHUNK, H).transpose(1, 0, 3, 2)


def from_chunks(y):
    NC, B, H, L, d = y.shape
    return y.transpose(1, 0, 3, 2, 4).reshape(B, NC * L, H * d)


def mlstm_chunkwise(q, k, v, li, lf):
    NC, B, H, L, dk = q.shape
    dv = v.shape[-1]
    causal = jnp.tril(jnp.ones((L, L), dtype=bool))

    def step(carry, inp):
        C, n, m = carry
        qc, kc, vc, lic, lfc = inp
        b = jnp.cumsum(lfc, axis=-1)
        log_intra = b[..., :, None] - b[..., None, :] + lic[..., None, :]
        log_intra = jnp.where(causal, log_intra, -jnp.inf)
        log_inter = b + m[..., None]
        m_t = jnp.maximum(log_inter, jnp.max(log_intra, axis=-1))
        d_intra = jnp.exp(log_intra - m_t[..., None])
        d_inter = jnp.exp(log_inter - m_t)
        s = jnp.einsum('bhtd,bhsd->bhts', qc, kc) * d_intra
        num = jnp.einsum('bhts,bhsv->bhtv', s, vc) + d_inter[..., None] * jnp.einsum('bhtd,bhdv->bhtv', qc, C)
        den = jnp.sum(s, axis=-1) + d_inter * jnp.einsum('bhtd,bhd->bht', qc, n)
        h = num / jnp.maximum(jnp.abs(den), jnp.exp(-m_t))[..., None]
        b_end = b[..., -1]
        log_w = b_end[..., None] - b + lic
        m_new = jnp.maximum(b_end + m, jnp.max(log_w, axis=-1))
        w = jnp.exp(log_w - m_new[..., None])
        decay = jnp.exp(b_end + m - m_new)
        C = decay[..., None, None] * C + jnp.einsum('bhsd,bhsv->bhdv', kc * w[..., None], vc)
        n = decay[..., None] * n + jnp.einsum('bhs,bhsd->bhd', w, kc)
        return (C, n, m_new), h

    init = (jnp.zeros((B, H, dk, dv), jnp.float32), jnp.zeros((B, H, dk), jnp.float32),
            jnp.zeros((B, H), jnp.float32))
    _, hs = lax.scan(step, init, (q, k, v, li, lf))
    return hs


def retention_chunkwise(q, k, v, log_gamma):
    NC, B, H, L, dk = q.shape
    dv = v.shape[-1]
    pos = jnp.arange(L, dtype=jnp.float32)
    diff = pos[:, None] - pos[None, :]
    lg = log_gamma[:, None, None]
    intra = jnp.where(diff >= 0, jnp.exp(jnp.maximum(diff, 0.0) * lg), 0.0)
    inter = jnp.exp((pos + 1.0) * log_gamma[:, None])
    kdec = jnp.exp((L - 1.0 - pos) * log_gamma[:, None])
    cdec = jnp.exp(L * log_gamma)

    def step(R, inp):
        qc, kc, vc = inp
        s = jnp.einsum('bhtd,bhsd->bhts', qc, kc) * intra
        y = jnp.einsum('bhts,bhsv->bhtv', s, vc) + jnp.einsum('bhtd,bhdv->bhtv', qc, R) * inter[..., None]
        R = cdec[:, None, None] * R + jnp.einsum('bhsd,bhsv->bhdv', kc * kdec[..., None], vc)
        return R, y

    _, ys = lax.scan(step, jnp.zeros((B, H, dk, dv), jnp.float32), (q, k, v))
    return ys


def setup_inputs(seed: int = 0) -> dict:
    key = jax.random.key(seed)
    ks = jax.random.split(key, 24)
    f32 = jnp.float32

    def nrm(k, shape, scale):
        return jax.random.normal(k, shape, f32) * scale

    def gain(k, shape):
        return 1.0 + 0.05 * jax.random.normal(k, shape, f32)

    b_in = nrm(ks[4], (DEPTH, N_IN), 0.02)
    f_bias = jnp.linspace(3.0, 6.0, M_HEADS, dtype=f32)[None, :] + nrm(ks[5], (DEPTH, M_HEADS), 0.1)
    b_in = b_in.at[:, F_GATE_OFFSET:F_GATE_OFFSET + M_HEADS].set(f_bias)
    return {
        "x": nrm(ks[0], (BATCH, SEQ, D_MODEL), 1.0),
        "p": nrm(ks[1], (DEPTH, BATCH, SEQ, PE_DIM), 1.0),
        "norm1_g": gain(ks[2], (DEPTH, D_MODEL)),
        "w_in": nrm(ks[3], (DEPTH, D_MODEL, N_IN), D_MODEL ** -0.5),
        "b_in": b_in,
        "conv_w": nrm(ks[6], (DEPTH, CONV_K, 2 * M_QK), CONV_K ** -0.5),
        "conv_b": nrm(ks[7], (DEPTH, 2 * M_QK), 0.02),
        "m_norm_g": gain(ks[8], (DEPTH, M_V)),
        "r_norm_g": gain(ks[9], (DEPTH, R_V)),
        "w_bm": nrm(ks[10], (DEPTH, M_V, D_MODEL), M_V ** -0.5),
        "w_br": nrm(ks[11], (DEPTH, R_V, D_MODEL), R_V ** -0.5),
        "w_out": nrm(ks[12], (DEPTH, D_MODEL, D_MODEL), D_MODEL ** -0.5),
        "norm2_g": gain(ks[13], (DEPTH, D_MODEL)),
        "w_ff1": nrm(ks[14], (DEPTH, D_MODEL, D_FF), D_MODEL ** -0.5),
        "b_ff1": nrm(ks[15], (DEPTH, D_FF), 0.02),
        "w_ff2": nrm(ks[16], (DEPTH, D_FF, D_MODEL), D_FF ** -0.5),
        "b_ff2": nrm(ks[17], (DEPTH, D_MODEL), 0.02),
        "norm3_g": gain(ks[18], (DEPTH, D_MODEL)),
        "w_pe_gate": nrm(ks[19], (DEPTH, D_MODEL, D_MODEL), D_MODEL ** -0.5),
        "w_pe": nrm(ks[20], (DEPTH, PE_DIM, D_MODEL), PE_DIM ** -0.5),
        "final_g": gain(ks[21], (D_MODEL,)),
    }


def reference(x, p, norm1_g, w_in, b_in, conv_w, conv_b, m_norm_g, r_norm_g, w_bm, w_br,
              w_out, norm2_g, w_ff1, b_ff1, w_ff2, b_ff2, norm3_g, w_pe_gate, w_pe, final_g):
    B, S, _ = x.shape
    dt = x.dtype
    f32 = jnp.float32
    pos = jnp.arange(S, dtype=f32)
    inv_freq = ROPE_BASE ** (-jnp.arange(0, R_DK, 2, dtype=f32) / R_DK)
    ang = pos[:, None] * inv_freq[None, :]
    cos = jnp.cos(ang)[:, None, :]
    sin = jnp.sin(ang)[:, None, :]
    log_gamma = jnp.log(1.0 - jnp.exp2(-5.0 - jnp.arange(R_HEADS, dtype=f32)))

    for i in range(DEPTH):
        h = rmsnorm(x, norm1_g[i])
        z = h @ w_in[i] + b_in[i]
        _, _, mv, mo, mi, mf, rq, rk, rv, rg, gm, gr = split_cols(z)
        qk = jax.nn.silu(causal_conv(z[..., :2 * M_QK], conv_w[i], conv_b[i]))
        mq = qk[..., :M_QK] * (M_DK ** -0.5)
        mk = qk[..., M_QK:]

        hm = mlstm_chunkwise(
            to_chunks(mq.astype(f32), M_HEADS), to_chunks(mk.astype(f32), M_HEADS),
            to_chunks(mv.astype(f32), M_HEADS),
            gates_to_chunks(mi.astype(f32)), gates_to_chunks(jax.nn.log_sigmoid(mf.astype(f32))))
        hm = from_chunks(hm)
        y_m = (jax.nn.sigmoid(mo.astype(f32)) * head_layernorm(hm, m_norm_g[i], M_HEADS)).astype(dt)

        rq4 = rope(rq.astype(f32).reshape(B, S, R_HEADS, R_DK), cos, sin)
        rk4 = rope(rk.astype(f32).reshape(B, S, R_HEADS, R_DK), cos, sin) * (R_DK ** -0.5)
        hr = retention_chunkwise(
            to_chunks(rq4.reshape(B, S, R_QK), R_HEADS), to_chunks(rk4.reshape(B, S, R_QK), R_HEADS),
            to_chunks(rv.astype(f32), R_HEADS), log_gamma)
        hr = from_chunks(hr)
        y_r = (jax.nn.silu(rg.astype(f32)) * head_layernorm(hr, r_norm_g[i], R_HEADS)).astype(dt)

        merged = jax.nn.sigmoid(gm) * (y_m @ w_bm[i]) + jax.nn.sigmoid(gr) * (y_r @ w_br[i])
        x = x + merged @ w_out[i]

        h2 = rmsnorm(x, norm2_g[i])
        x = x + jnp.square(jax.nn.relu(h2 @ w_ff1[i] + b_ff1[i])) @ w_ff2[i] + b_ff2[i]

        h3 = rmsnorm(x, norm3_g[i])
        x = x + jax.nn.sigmoid(h3 @ w_pe_gate[i]) * (p[i] @ w_pe[i])

    return rmsnorm(x, final_g)
```

```python
import contextlib
import math
import numpy as np
import concourse.bass as bass
import concourse.mybir as mybir
from concourse.bass_utils import run_bass_kernel_spmd

F32 = mybir.dt.float32
BF16 = mybir.dt.bfloat16
AF = mybir.ActivationFunctionType
ALU = mybir.AluOpType
AX = mybir.AxisListType

P = 128
D = 1024
KC = 8
EPS = 1e-6
N_IN = 12296
MQ, MK, MV, MO, MI, MF, RQ, RK, RV, RG, GM, GR = 0, 1024, 2048, 3072, 4096, 4100, 4104, 5128, 6152, 8200, 10248, 11272
DFF = 4096
WBLK = 4608
BIASOFF = 4096
LN16 = math.log(16.0)

SM = {}
_o = 0
for _n, _w in (("bqk", 16), ("brqk", 16), ("bg", 16), ("cw", 64), ("cb", 16), ("bf1", 32), ("bf2", 8),
               ("g1", 8), ("gmn", 8), ("grn", 16), ("g2", 8), ("g3", 8), ("bgate", 8), ("fg", 8)):
    SM[_n] = _o
    _o += _w
NSM = _o
C_ID, C_TRIU, C_ONES, C_MASKM, C_MASKR, C_RSC, C_KDEC = 0, 128, 256, 384, 512, 1024, 1028
NCST = 1032


class Buf:
    __slots__ = ("name", "w", "r", "excl")

    def __init__(self, name, excl=False):
        self.name = name
        self.w = None
        self.r = {}
        self.excl = excl


class DSem:
    def __init__(self, sem):
        self.sem = sem
        self.count = 0


class _Eng:
    def __init__(self, name, sem, same_sync):
        self.name = name
        self.sem = sem
        self.count = 0
        self.ops = []
        self.waited = {}
        self.same_sync = same_sync


class _Rec:
    def __init__(self):
        self.call = None

    def __getattr__(self, name):
        def f(*a, **k):
            self.call = (name, a, k)
            return self
        return f


class Sched:
    def __init__(self, nc, sems, same_sync=True):
        self.nc = nc
        self.engs = {}
        for name, same in (("pe", False), ("act", same_sync), ("dve", same_sync), ("pool", same_sync), ("sp", False)):
            self.engs[name] = _Eng(name, sems[name], same)
        self.nops = 0

    def add(self, eng, fn, reads=(), writes=(), dsem=None):
        E = self.engs[eng]
        need = {}

        def dep(t):
            s, v = t
            k = id(s)
            if k not in need or need[k][1] < v:
                need[k] = (s, v)

        for b in reads:
            if b.w is not None:
                dep(b.w)
            if b.excl:
                for k_, t in b.r.items():
                    if k_ != id(E.sem):
                        dep(t)
        for b in writes:
            if b.w is not None:
                dep(b.w)
            for t in b.r.values():
                dep(t)
        waits = []
        for k, (s, v) in need.items():
            if s is E.sem and not E.same_sync:
                continue
            if E.waited.get(k, 0) >= v:
                continue
            E.waited[k] = v
            waits.append((s, v))
        if dsem is None:
            E.count += 1
            tok = (E.sem, E.count)
            inc = (E.sem, 1)
        else:
            dsem.count += 16
            tok = (dsem.sem, dsem.count)
            inc = (dsem.sem, 16)
        rec = _Rec()
        fn(rec)
        E.ops.append((waits, rec.call, inc))
        for b in reads:
            k = id(tok[0])
            if k not in b.r or b.r[k][1] < tok[1]:
                b.r[k] = tok
        for b in writes:
            b.w = tok
            b.r = {}
        self.nops += 1
        return tok

    def wait_only(self, eng, toks):
        E = self.engs[eng]
        waits = []
        for (s, v) in toks:
            if E.waited.get(id(s), 0) >= v:
                continue
            E.waited[id(s)] = v
            waits.append((s, v))
        E.ops.append((waits, None, None))

    def emit(self):
        nc = self.nc

        def run(h, name):
            for waits, fn, inc in self.engs[name].ops:
                for (s, v) in waits:
                    h.wait_ge(s, v)
                if fn is not None:
                    getattr(h, fn[0])(*fn[1], **fn[2]).then_inc(inc[0], inc[1])

        with nc.Block() as block:
            @block.tensor
            def _(h):
                run(h, "pe")

            @block.scalar
            def _(h):
                run(h, "act")

            @block.vector
            def _(h):
                run(h, "dve")

            @block.gpsimd
            def _(h):
                run(h, "pool")

            @block.sync
            def _(h):
                run(h, "sp")


def alias_barrier(new_bufs, old_bufs):
    for nb in new_bufs:
        for ob in old_bufs:
            if ob.w is not None:
                k = id(ob.w[0])
                if k not in nb.r or nb.r[k][1] < ob.w[1]:
                    nb.r[k] = ob.w
            for k, t in ob.r.items():
                if k not in nb.r or nb.r[k][1] < t[1]:
                    nb.r[k] = t


def block_table():
    B = []

    def blk(name, kcb, ncb, pieces, bias=()):
        B.append(dict(name=name, kcb=kcb, ncb=ncb, pieces=pieces, bias=bias))

    for j in range(4):
        blk(f"QK{j}", 8, 512, [("w_in", j * 512, 512, 0, "g1")])
    blk("G", 8, 8, [("w_in", MI, 8, 0, "g1")])
    for h in range(4):
        blk(f"M{h}", 8, 512, [("w_in", MV + h * 256, 256, 0, "g1"), ("w_in", MO + h * 256, 256, 256, "g1")],
            bias=[(MV + h * 256, 256, 0), (MO + h * 256, 256, 256)])
    for h in range(4):
        blk(f"RQK{h}", 8, 512, [("w_in", RQ + h * 256, 256, 0, "g1"), ("w_in", RK + h * 256, 256, 256, "g1")])
        blk(f"RV{h}", 8, 512, [("w_in", RV + h * 512, 512, 0, "g1")], bias=[(RV + h * 512, 512, 0)])
        blk(f"RG{h}", 8, 512, [("w_in", RG + h * 512, 512, 0, "g1")], bias=[(RG + h * 512, 512, 0)])
    for j in range(2):
        blk(f"GM{j}", 8, 512, [("w_in", GM + j * 512, 512, 0, "g1")])
    for j in range(2):
        blk(f"GR{j}", 8, 512, [("w_in", GR + j * 512, 512, 0, "g1")])
    for j in range(2):
        blk(f"BM{j}", 8, 512, [("w_bm", j * 512, 512, 0, "gmn")])
    for j in range(4):
        blk(f"BR{j}", 16, 256, [("w_br", j * 256, 256, 0, "grn")])
    for j in range(2):
        blk(f"O{j}", 8, 512, [("w_out", j * 512, 512, 0, None)])
    for j in range(8):
        blk(f"F1{j}", 8, 512, [("w_ff1", j * 512, 512, 0, "g2")])
    for j in range(8):
        blk(f"F2{j}", 32, 128, [("w_ff2", j * 128, 128, 0, None)])
    for j in range(2):
        blk(f"PG{j}", 8, 512, [("w_pe_gate", j * 512, 512, 0, "g3")])
    blk("PE", 2, 1024, [("w_pe", 0, 1024, 0, None)])
    return B


BLOCKS = block_table()
NBLK = len(BLOCKS)
BIDX = {b["name"]: i for i, b in enumerate(BLOCKS)}


def build(NL, S, TT=256, NLW=4, same_sync=True, NSLOT=3, NSTAGE=1, NXT=2, max_steps=None, skip_conv=False, dbg=0):
    NSUB = TT // 128
    NT = S // TT
    assert S % TT == 0
    nc = bass.Bass("TRN2", target_bir_lowering=False)

    def din(name, shape):
        return nc.dram_tensor(name, list(shape), F32, kind="ExternalInput").ap()

    x_d = din("x", [S, D])
    p_d = din("p", [NLW, S, 256])
    wsrc = {
        "w_in": din("w_in", [NLW, D, N_IN]),
        "w_bm": din("w_bm", [NLW, 1024, D]),
        "w_br": din("w_br", [NLW, 2048, D]),
        "w_out": din("w_out", [NLW, D, D]),
        "w_ff1": din("w_ff1", [NLW, D, DFF]),
        "w_ff2": din("w_ff2", [NLW, DFF, D]),
        "w_pe_gate": din("w_pe_gate", [NLW, D, D]),
        "w_pe": din("w_pe", [NLW, 256, D]),
    }
    b_in_d = din("b_in", [NLW, N_IN])
    smalls_d = din("smalls", [NLW, P, NSM])
    consts_d = din("consts", [P, NCST])
    cos_d = din("cos_t", [P, S])
    sin_d = din("sin_t", [P, S])
    out_d = nc.dram_tensor("out", [S, D], F32, kind="ExternalOutput").ap()
    wscr = nc.dram_tensor("wscr", [NL * NBLK, P, WBLK], BF16).ap()
    xscr = nc.dram_tensor("xscr", [NT, P, KC * TT], F32).ap()
    hscr = nc.dram_tensor("hscr", [NT, P, KC * TT], BF16).ap()

    st = contextlib.ExitStack()
    with st:
        sems = {n: st.enter_context(nc.semaphore("s_" + n)) for n in ("pe", "act", "dve", "pool", "sp")}
        S_ = Sched(nc, sems, same_sync=same_sync)

        def newdsem(name):
            return DSem(st.enter_context(nc.semaphore("d_" + name)))

        def sb(name, shape, dt):
            return nc.alloc_sbuf_tensor("sb_" + name, list(shape), dt)

        xT = sb("xT", [P, KC, TT], F32)
        hT = sb("hT", [P, KC, TT], BF16)
        ZW = TT + 3
        BIGN = max(16 * ZW + 16 * TT, 32 * TT)
        BIGA = sb("BIGA", [P, BIGN], BF16)
        zqk = BIGA[:, 0:16 * ZW].rearrange("p (c t) -> p c t", t=ZW)
        qkT = BIGA[:, 16 * ZW:16 * ZW + 16 * TT].rearrange("p (c t) -> p c t", t=TT)
        hidT = BIGA[:, 0:32 * TT].rearrange("p (c t) -> p c t", t=TT)
        sgm = BIGA[:, 0:8 * TT].rearrange("p (c t) -> p c t", t=TT)
        sgr = BIGA[:, 8 * TT:16 * TT].rearrange("p (c t) -> p c t", t=TT)
        mrgT = BIGA[:, 16 * TT:24 * TT].rearrange("p (c t) -> p c t", t=TT)
        sgT = BIGA[:, 0:8 * TT].rearrange("p (c t) -> p c t", t=TT)
        RETB = sb("RETB", [P, 16 * TT], BF16)
        rpre = RETB[:, 0:4 * TT].rearrange("p (c t) -> p c t", t=TT)
        rqkT = RETB[:, 4 * TT:8 * TT].rearrange("p (c t) -> p c t", t=TT)
        rv = RETB[:, 8 * TT:12 * TT].rearrange("p (c n) -> p c n", n=512)
        Vext = RETB[:, 12 * TT:12 * TT + NSUB * 258].rearrange("p (c n) -> p c n", n=258)
        yrT = RETB[:, :].rearrange("p (c t) -> p c t", t=TT)
        srg = sb("srg", [P, NSUB, 4, 512], BF16)
        Ur = sb("Ur", [P, NSUB, 4, 512], BF16)
        sigo = srg[:].rearrange("p c h n -> p (c h n)")[:, 0:NSUB * 4 * 256].rearrange("p (c h n) -> p c h n", c=NSUB, h=4)
        Um = Ur[:].rearrange("p c h n -> p (c h n)")[:, 0:NSUB * 4 * 258].rearrange("p (c h n) -> p c h n", c=NSUB, h=4)
        STm = sb("STm", [P, NSUB, 4, 6], F32)
        STr = sb("STr", [P, NSUB, 4, 6], F32)
        MVb = sb("MVb", [P, NSUB, 4, 2], F32)
        XS = sb("XS", [P, 6, NSUB, 4], F32)
        ymT = sb("ymT", [P, 8, TT], BF16)
        wsl = [sb(f"wsl{i}", [P, WBLK], BF16) for i in range(NSLOT)]
        stage32 = [sb(f"st32_{i}", [P, 2048], F32) for i in range(NSTAGE)]
        stage16 = [sb(f"st16_{i}", [P, 2048], BF16) for i in range(NSTAGE)]
        Cbf = sb("Cbf", [P, 2, 4, 258], BF16)
        Rbf = sb("Rbf", [P, 2, 4, 512], BF16)
        cosb = sb("cosb", [P, TT], F32)
        sinb = sb("sinb", [P, TT], F32)
        NT32 = 3
        T32 = [sb(f"T32_{i}", [P, TT], F32) for i in range(NT32)]
        rstd = sb("rstd", [P, TT], F32)
        Spp = [sb(f"Spp{i}", [P, 128], BF16) for i in range(2)]
        kp = [sb(f"kp{i}", [P, 256], BF16) for i in range(2)]
        ytmp = [sb(f"ytmp{i}", [P, 512], BF16) for i in range(2)]
        ytm2 = [sb(f"ytm2{i}", [P, 512], BF16) for i in range(2)]
        xtok = [sb(f"xtok{i}", [P, D], F32) for i in range(NXT)]
        ptok = [sb(f"ptok{i}", [P, 256], F32) for i in range(2)]
        pbf = [sb(f"pbf{i}", [P, 256], BF16) for i in range(2)]
        pT = sb("pT", [P, 2, TT], BF16)
        rel = [sb(f"rel{i}", [P, TT], BF16) for i in range(2)]
        gif = sb("gif", [P, NSUB, 8], F32)
        spl = sb("spl", [P, NSUB, 4], F32)
        G1 = sb("G1", [P, NSUB, 16], F32)
        GE = sb("GE", [P, NSUB, 16], F32)
        stats = [sb(f"stats{i}", [P, 6], F32) for i in range(2)]
        mv_ = [sb(f"mv{i}", [P, 2], F32) for i in range(2)]
        sc = [sb(f"sc{i}", [P, 8], F32) for i in range(2)]
        smalls = sb("smalls", [P, NSM], F32)
        consts = sb("consts", [P, NCST], F32)
        identb = sb("identb", [P, 128], BF16)
        onesb = sb("onesb", [P, 128], BF16)
        onerow = sb("onerow", [P, 128], BF16)
        halo = sb("halo", [P, 16, 3], BF16)

        PSD = [nc.alloc_psum_tensor(f"psd{i}", [P, 512], F32) for i in range(2)]
        PSS = nc.alloc_psum_tensor("pss", [P, 512], F32)
        PSO = [nc.alloc_psum_tensor(f"pso{i}", [P, 512], F32) for i in range(2)]
        PSD = PSD + PSO
        PSU = nc.alloc_psum_tensor("psu", [P, 2, 512], F32)
        PST = nc.alloc_psum_tensor("pst", [P, 1024], BF16)

        def bl(name, n):
            return [Buf(f"{name}{i}") for i in range(n)]

        B_xT = bl("xT", KC)
        B_hT = bl("hT", KC)
        B_zqk = bl("zqk", 16)
        B_qkT = bl("qkT", 16)
        B_hid = bl("hid", 32)
        B_sgm = bl("sgm", 8)
        B_sgr = bl("sgr", 8)
        B_mrg = bl("mrg", 8)
        B_sgT = bl("sgT", 8)
        B_rpre = bl("rpre", 4)
        B_rqkT = bl("rqkT", 4)
        B_Vext = bl("Vext", NSUB)
        B_sigo = bl("sigo", NSUB * 4)
        B_rv = bl("rv", NSUB)
        B_srg = bl("srg", NSUB * 4)
        B_Um = bl("Um", NSUB * 4)
        B_Ur = bl("Ur", NSUB * 4)
        B_STm = bl("STm", NSUB * 4)
        B_STr = bl("STr", NSUB * 4)
        B_MVb = Buf("MVb")
        B_XS = Buf("XS")
        B_ymT = bl("ymT", 8)
        B_yrT = bl("yrT", 16)
        B_wsl = bl("wsl", NSLOT)
        B_st32 = bl("st32", NSTAGE)
        B_st16 = bl("st16", NSTAGE)
        B_C32 = bl("C32", 4)
        B_Cbf = bl("Cbf", 4)
        B_R32 = bl("R32", 4)
        B_Rbf = bl("Rbf", 4)
        B_cos = Buf("cos")
        B_sin = Buf("sin")
        B_T32 = bl("T32", NT32)
        B_rstd = Buf("rstd")
        B_Spp = bl("Spp", 2)
        B_kp = bl("kp", 2)
        B_ytmp = bl("ytmp", 2)
        B_ytm2 = bl("ytm2", 2)
        B_xtok = bl("xtok", NXT)
        B_ptok = bl("ptok", 2)
        B_pbf = bl("pbf", 2)
        B_pT = Buf("pT")
        B_rel = bl("rel", 2)
        B_gif = Buf("gif")
        B_spl = Buf("spl")
        B_G1 = Buf("G1")
        B_GE = Buf("GE")
        B_stats = bl("stats", 2)
        B_mv = bl("mv", 2)
        B_sc = bl("sc", 2)
        B_smalls = Buf("smalls")
        B_consts = Buf("consts")
        B_cb = Buf("constb")
        B_halo = Buf("halo")
        B_PSD = [Buf(f"psd{i}", excl=True) for i in range(2)]
        B_PSS = Buf("pss", excl=True)
        B_PSO = [Buf(f"pso{i}", excl=True) for i in range(2)]
        B_PSD = B_PSD + B_PSO
        B_PSU = Buf("psu", excl=True)
        B_PSTy = Buf("psty", excl=True)
        B_PSTk = B_PSTy
        B_wscr = [Buf(f"wscr{i}") for i in range(NL * NBLK)]
        B_xscr = bl("xscr", NT)
        B_hscr = bl("hscr", NT)
        B_out = Buf("out")

        D_wsl = [newdsem(f"wsl{i}") for i in range(NSLOT)]
        D_stin = [newdsem(f"stin{i}") for i in range(NSTAGE)]
        D_stout = [newdsem(f"stout{i}") for i in range(NSTAGE)]
        D_x = newdsem("x")
        D_xs = newdsem("xs")
        D_h = newdsem("h")
        D_hs = newdsem("hs")
        D_misc = newdsem("misc")
        D_cos = newdsem("cos")
        D_sin = newdsem("sin")
        D_xtok = [newdsem(f"xtok{i}") for i in range(NXT)]
        D_ost = [newdsem(f"ost{i}") for i in range(NXT)]
        D_ptok = [newdsem(f"ptok{i}") for i in range(2)]
        D_sm = newdsem("sm")

        A = S_.add
        rr = {"psd": 0, "pso": 0, "t32": 0, "spp": 0, "kp": 0, "yt": 0, "yt2": 0, "st": 0, "rel": 0,
              "stats": 0, "ew": 0, "xtok": 0, "ptok": 0}

        def nxt(k, n):
            v = rr[k]
            rr[k] = (v + 1) % n
            return v

        def ew_eng():
            return "dve" if nxt("ew", 2) == 0 else "pool"

        A("sp", lambda h: h.dma_start(out=consts[:], in_=consts_d), writes=[B_consts], dsem=D_misc)
        A("dve", lambda h: h.tensor_copy(out=identb[:], in_=consts[:, C_ID:C_ID + 128]), reads=[B_consts], writes=[B_cb])
        A("dve", lambda h: h.memset(onesb[:], 1.0 / D), writes=[B_cb])
        A("dve", lambda h: h.memset(onerow[:], 0.0), writes=[B_cb])
        A("dve", lambda h: h.memset(onerow[0:1, :], 1.0), writes=[B_cb])
        for i_ in range(NSLOT):
            A("dve", lambda h, i_=i_: h.memset(wsl[i_][:, BIASOFF:WBLK], 0.0), writes=[B_wsl[i_]])
        ident32 = consts[:, C_ID:C_ID + 128]
        triu = consts[:, C_TRIU:C_TRIU + 128]
        ones32 = consts[:, C_ONES:C_ONES + 128]
        maskm = consts[:, C_MASKM:C_MASKM + 128]

        def smc(name, j, n=1, smt=None):
            o = SM[name] + j
            return (smalls if smt is None else smt)[:, o:o + n]

        def conv_jobs(l):
            jobs = []
            for bi, b in enumerate(BLOCKS):
                kcb, ncb = b["kcb"], b["ncb"]
                for (src, c0, ncols, dc, gname) in b["pieces"]:
                    per = max(1, 2048 // ncols)
                    for k0 in range(0, kcb, per):
                        k1 = min(kcb, k0 + per)
                        jobs.append(("w", l, bi, src, c0, ncols, dc, gname, k0, k1))
                for (c0, ncols, dc) in b["bias"]:
                    jobs.append(("b", l, bi, c0, ncols, dc))
            return jobs

        def do_conv_job(job, smt, smb):
            s = nxt("st", NSTAGE)
            if job[0] == "w":
                _, l, bi, src, c0, ncols, dc, gname, k0, k1 = job
                b = BLOCKS[bi]
                kcb, ncb = b["kcb"], b["ncb"]
                nk = k1 - k0
                srcap = wsrc[src][l].rearrange("(k p) n -> p k n", p=P)[:, k0:k1, c0:c0 + ncols]
                s32 = stage32[s][:, 0:nk * ncols].rearrange("p (k n) -> p k n", n=ncols)
                s16 = stage16[s][:, 0:nk * ncols].rearrange("p (k n) -> p k n", n=ncols)
                A("sp", lambda h: h.dma_start(out=s32, in_=srcap), writes=[B_st32[s]], dsem=D_stin[s])
                if gname is None:
                    A("pool", lambda h: h.tensor_copy(out=stage16[s][:, 0:nk * ncols], in_=stage32[s][:, 0:nk * ncols]),
                      reads=[B_st32[s]], writes=[B_st16[s]])
                else:
                    for k in range(nk):
                        g = smc(gname, k0 + k, smt=smt)
                        eng = "pool" if k % 2 == 0 else "act"
                        if eng == "pool":
                            A("pool", lambda h, k=k, g=g: h.tensor_scalar(out=s16[:, k, :], in0=s32[:, k, :], scalar1=g,
                                                                          scalar2=None, op0=ALU.mult),
                              reads=[B_st32[s], smb], writes=[B_st16[s]])
                        else:
                            A("act", lambda h, k=k, g=g: h.activation(out=s16[:, k, :], in_=s32[:, k, :], func=AF.Copy, scale=g),
                              reads=[B_st32[s], smb], writes=[B_st16[s]])
                dst = wscr[l * NBLK + bi][:, 0:kcb * ncb].rearrange("p (k n) -> p k n", n=ncb)[:, k0:k1, dc:dc + ncols]
                A("sp", lambda h: h.dma_start(out=dst, in_=s16), reads=[B_st16[s]], writes=[B_wscr[l * NBLK + bi]], dsem=D_stout[s])
            else:
                _, l, bi, c0, ncols, dc = job
                srcap = b_in_d[l:l + 1, c0:c0 + ncols]
                A("sp", lambda h: h.dma_start(out=stage32[s][0:1, 0:ncols], in_=srcap), writes=[B_st32[s]], dsem=D_stin[s])
                A("pool", lambda h: h.tensor_copy(out=stage16[s][0:1, 0:ncols], in_=stage32[s][0:1, 0:ncols]),
                  reads=[B_st32[s]], writes=[B_st16[s]])
                dst = wscr[l * NBLK + bi][0:1, BIASOFF + dc:BIASOFF + dc + ncols]
                A("sp", lambda h: h.dma_start(out=dst, in_=stage16[s][0:1, 0:ncols]), reads=[B_st16[s]],
                  writes=[B_wscr[l * NBLK + bi]], dsem=D_stout[s])

        def rms_accum(n):
            A("act", lambda h: h.activation(out=hT[:, n, :], in_=xT[:, n, :], func=AF.Square), reads=[B_xT[n]], writes=[B_hT[n]])
            A("pe", lambda h: h.matmul(PSS[:, 0:TT], lhsT=onesb[:], rhs=hT[:, n, :], start=(n == 0), stop=(n == KC - 1)),
              reads=[B_hT[n], B_cb], writes=[B_PSS])

        def rmsnorm_to_hT(pre=False):
            if pre:
                src, bsrc = PSS, B_PSS
            else:
                for kc in range(KC):
                    A("act", lambda h, kc=kc: h.activation(out=hT[:, kc, :], in_=xT[:, kc, :], func=AF.Square),
                      reads=[B_xT[kc]], writes=[B_hT[kc]])
                pi = nxt("psd", 4)
                for kc in range(KC):
                    A("pe", lambda h, kc=kc: h.matmul(PSD[pi][:, 0:TT], lhsT=onesb[:], rhs=hT[:, kc, :], start=(kc == 0), stop=(kc == KC - 1)),
                      reads=[B_hT[kc], B_cb], writes=[B_PSD[pi]])
                src, bsrc = PSD[pi], B_PSD[pi]
            A("act", lambda h: h.activation(out=rstd[:], in_=src[:, 0:TT], func=AF.Ln, bias=EPS), reads=[bsrc], writes=[B_rstd])
            A("act", lambda h: h.activation(out=rstd[:], in_=rstd[:], func=AF.Exp, scale=-0.5), reads=[B_rstd], writes=[B_rstd])
            for kc in range(KC):
                e = "pool" if kc % 4 == 3 else "dve"
                A(e, lambda h, kc=kc: h.tensor_tensor(out=hT[:, kc, :], in0=xT[:, kc, :], in1=rstd[:], op=ALU.mult),
                  reads=[B_xT[kc], B_rstd], writes=[B_hT[kc]])

        def fm_group(w, bw, nchunks, col0, evac):
            wv = w[:, 0:KC * 512].rearrange("p (k n) -> p k n", n=512)
            for j in range(nchunks):
                pi = nxt("psd", 4)
                for kc in range(KC):
                    A("pe", lambda h, kc=kc, j=j, pi=pi: h.matmul(PSD[pi][:, 0:TT], lhsT=wv[:, kc, col0 + j * 128:col0 + (j + 1) * 128],
                                                                   rhs=hT[:, kc, :], start=(kc == 0), stop=(kc == KC - 1)),
                      reads=[bw, B_hT[kc]], writes=[B_PSD[pi]])
                evac(j, pi)

        def tm_group(w, bw, ncols, evac, bias=True):
            wv = w[:, 0:KC * ncols].rearrange("p (k n) -> p k n", n=ncols)
            for c in range(NSUB):
                pi = nxt("psd", 4)
                if bias:
                    A("pe", lambda h, pi=pi: h.matmul(PSD[pi][:, 0:ncols], lhsT=onerow[:], rhs=w[:, BIASOFF:BIASOFF + ncols],
                                                     start=True, stop=False), reads=[bw, B_cb], writes=[B_PSD[pi]])
                for kc in range(KC):
                    A("pe", lambda h, kc=kc, c=c, pi=pi: h.matmul(PSD[pi][:, 0:ncols], lhsT=hT[:, kc, c * 128:(c + 1) * 128], rhs=wv[:, kc, :],
                                                                   start=(kc == 0 and not bias), stop=(kc == KC - 1)),
                      reads=[bw, B_hT[kc]], writes=[B_PSD[pi]])
                evac(c, pi)

        def ln_scalars(pso_ap, nfeat, r_ap, Bpso, extra_reads):
            si = nxt("stats", 2)
            A("dve", lambda h: h.bn_stats(out=stats[si][:], in_=pso_ap), reads=[Bpso], writes=[B_stats[si]])
            A("dve", lambda h: h.bn_aggr(out=mv_[si][:], in_=stats[si][:]), reads=[B_stats[si]], writes=[B_mv[si]])
            s_ = sc[si]
            A("dve", lambda h: h.tensor_tensor(out=s_[:, 0:1], in0=r_ap, in1=r_ap, op=ALU.mult), reads=extra_reads, writes=[B_sc[si]])
            A("dve", lambda h: h.tensor_tensor(out=s_[:, 1:2], in0=s_[:, 0:1], in1=mv_[si][:, 1:2], op=ALU.mult),
              reads=[B_sc[si], B_mv[si]], writes=[B_sc[si]])
            A("act", lambda h: h.activation(out=s_[:, 2:3], in_=s_[:, 1:2], func=AF.Ln, bias=EPS), reads=[B_sc[si]], writes=[B_sc[si]])
            A("act", lambda h: h.activation(out=s_[:, 2:3], in_=s_[:, 2:3], func=AF.Exp, scale=-0.5), reads=[B_sc[si]], writes=[B_sc[si]])
            A("dve", lambda h: h.tensor_tensor(out=s_[:, 3:4], in0=s_[:, 2:3], in1=r_ap, op=ALU.mult),
              reads=[B_sc[si]] + list(extra_reads), writes=[B_sc[si]])
            A("dve", lambda h: h.scalar_tensor_tensor(out=s_[:, 4:5], in0=mv_[si][:, 0:1], scalar=-1.0, in1=s_[:, 3:4], op0=ALU.mult, op1=ALU.mult),
              reads=[B_sc[si], B_mv[si]], writes=[B_sc[si]])
            return s_[:, 3:4], s_[:, 4:5], B_sc[si]

        def finalize_heads(is_m):
            U, ST, BU, BST = (Um, STm, B_Um, B_STm) if is_m else (Ur, STr, B_Ur, B_STr)
            nf = 256 if is_m else 512
            for c in range(NSUB):
                for hh in range(4):
                    A("dve", lambda h, c=c, hh=hh: h.bn_aggr(out=MVb[:, c, hh, :], in_=ST[:, c, hh, :]), reads=[BST[c * 4 + hh]], writes=[B_MVb])
            X = [XS[:, i, :, :] for i in range(6)]
            if is_m:
                eT = GE[:, :, 8:12]
                den = Um[:, :, :, 256]
                A("dve", lambda h: h.tensor_tensor(out=X[0], in0=den, in1=eT, op=ALU.mult), reads=BU + [B_GE], writes=[B_XS])
                A("dve", lambda h: h.scalar_tensor_tensor(out=X[1], in0=X[0], scalar=-1.0, in1=X[0], op0=ALU.mult, op1=ALU.max), reads=[B_XS], writes=[B_XS])
                A("dve", lambda h: h.tensor_scalar(out=X[0], in0=X[1], scalar1=1.0, scalar2=None, op0=ALU.max), reads=[B_XS], writes=[B_XS])
                A("dve", lambda h: h.reciprocal(out=X[1], in_=X[0]), reads=[B_XS], writes=[B_XS])
                A("dve", lambda h: h.tensor_tensor(out=X[2], in0=X[1], in1=eT, op=ALU.mult), reads=[B_XS, B_GE], writes=[B_XS])
            else:
                for c in range(NSUB):
                    A("dve", lambda h, c=c: h.tensor_copy(out=XS[:, 2, c, :], in_=consts[:, C_RSC:C_RSC + 4]), reads=[B_consts], writes=[B_XS])
            A("dve", lambda h: h.tensor_tensor(out=X[3], in0=X[2], in1=X[2], op=ALU.mult), reads=[B_XS], writes=[B_XS])
            A("dve", lambda h: h.tensor_tensor(out=X[3], in0=X[3], in1=MVb[:, :, :, 1], op=ALU.mult), reads=[B_XS, B_MVb], writes=[B_XS])
            A("act", lambda h: h.activation(out=X[4], in_=X[3], func=AF.Ln, bias=EPS), reads=[B_XS], writes=[B_XS])
            A("act", lambda h: h.activation(out=X[4], in_=X[4], func=AF.Exp, scale=-0.5), reads=[B_XS], writes=[B_XS])
            A("dve", lambda h: h.tensor_tensor(out=X[4], in0=X[4], in1=X[2], op=ALU.mult), reads=[B_XS], writes=[B_XS])
            A("dve", lambda h: h.scalar_tensor_tensor(out=X[5], in0=MVb[:, :, :, 0], scalar=-1.0, in1=X[4], op0=ALU.mult, op1=ALU.mult),
              reads=[B_XS, B_MVb], writes=[B_XS])
            gate, Bg = (sigo, B_sigo) if is_m else (srg, B_srg)
            nch = nf // 128
            for c in range(NSUB):
                ts = slice(c * 128, (c + 1) * 128)
                for hh in range(4):
                    yi = nxt("yt", 2)
                    A("act", lambda h, c=c, hh=hh, yi=yi: h.activation(out=ytmp[yi][:, 0:nf], in_=U[:, c, hh, 0:nf], func=AF.Identity,
                                                                         bias=XS[:, 5, c, hh:hh + 1], scale=XS[:, 4, c, hh:hh + 1]),
                      reads=[BU[c * 4 + hh], B_XS], writes=[B_ytmp[yi]])
                    y2 = nxt("yt2", 2)
                    A("dve" if (c * 4 + hh) % 3 != 2 else "pool", lambda h, c=c, hh=hh, yi=yi, y2=y2: h.tensor_tensor(out=ytm2[y2][:, 0:nf], in0=ytmp[yi][:, 0:nf], in1=gate[:, c, hh, :], op=ALU.mult),
                      reads=[B_ytmp[yi], Bg[c * 4 + hh]], writes=[B_ytm2[y2]])
                    for j in range(nch):
                        A("pe", lambda h, j=j, y2=y2: h.transpose(out=PST[:, j * 128:(j + 1) * 128], in_=ytm2[y2][:, j * 128:(j + 1) * 128], identity=identb[:]),
                          reads=[B_ytm2[y2], B_cb], writes=[B_PSTy])
                    if is_m:
                        A("dve", lambda h, hh=hh: h.tensor_copy(out=ymT[:, 2 * hh:2 * hh + 2, ts], in_=PST[:, 0:256].rearrange("p (j n) -> p j n", n=128)),
                          reads=[B_PSTy], writes=B_ymT[2 * hh:2 * hh + 2])
                    else:
                        A("dve", lambda h, hh=hh: h.tensor_copy(out=yrT[:, 4 * hh:4 * hh + 4, ts], in_=PST[:, 0:512].rearrange("p (j n) -> p j n", n=128)),
                          reads=[B_PSTy], writes=B_yrT[4 * hh:4 * hh + 4])

        steps = []

        def tile_steps(l, t, first_layer, last_layer):
            t0 = t * TT
            seq_start = (t == 0)
            lb = l * NBLK

            def add_step(bname, fn):
                steps.append((None if bname is None else lb + BIDX[bname], fn))

            def s_load(w, bw):
                if t == 0:
                    A("sp", lambda h: h.dma_start(out=smalls[:], in_=smalls_d[l]), writes=[B_smalls], dsem=D_sm)
                if first_layer:
                    for c in range(NSUB):
                        xi = nxt("xtok", NXT)
                        A("pool", lambda h, c=c, xi=xi: h.dma_start(out=xtok[xi][:], in_=x_d[t0 + c * 128:t0 + (c + 1) * 128, :]),
                          writes=[B_xtok[xi]], dsem=D_xtok[xi])
                        for half in range(2):
                            pi = nxt("psd", 4)
                            for q in range(4):
                                kc = half * 4 + q
                                A("pe", lambda h, q=q, kc=kc, xi=xi, pi=pi: h.transpose(out=PSD[pi][:, q * 128:(q + 1) * 128],
                                                                                        in_=xtok[xi][:, kc * 128:(kc + 1) * 128], identity=ident32),
                                  reads=[B_xtok[xi], B_consts], writes=[B_PSD[pi]])
                            A("act", lambda h, half=half, c=c, pi=pi: h.activation(
                                out=xT[:, half * 4:half * 4 + 4, c * 128:(c + 1) * 128],
                                in_=PSD[pi][:, :].rearrange("p (q n) -> p q n", n=128), func=AF.Copy),
                              reads=[B_PSD[pi]], writes=B_xT[half * 4:half * 4 + 4])
                else:
                    A("sp", lambda h: h.dma_start(out=hT[:].rearrange("p k t -> p (k t)"), in_=hscr[t]), reads=[B_hscr[t]], writes=B_hT, dsem=D_h)
                    A("sp", lambda h: h.dma_start(out=xT[:].rearrange("p k t -> p (k t)"), in_=xscr[t]), reads=[B_xscr[t]], writes=B_xT, dsem=D_x)
                A("pool", lambda h: h.dma_start(out=cosb[:], in_=cos_d[:, t0:t0 + TT]), writes=[B_cos], dsem=D_cos)
                A("pool", lambda h: h.dma_start(out=sinb[:], in_=sin_d[:, t0:t0 + TT]), writes=[B_sin], dsem=D_sin)
                if seq_start:
                    A("pool", lambda h: h.memset(Cbf[:], 0.0), writes=B_Cbf)
                    A("pool", lambda h: h.memset(Rbf[:], 0.0), writes=B_Rbf)
                    A("pool", lambda h: h.memset(halo[:], 0.0), writes=[B_halo])
                alias_barrier(B_zqk + B_qkT, B_hid + B_sgm + B_sgr + B_mrg + B_sgT)
                alias_barrier(B_Vext + B_rpre + B_rqkT + B_rv, B_yrT)
                A("pool", lambda h: h.memset(Vext[:, :, 256:258], 1.0), writes=B_Vext)
                A("pool", lambda h: h.tensor_copy(out=zqk[:, :, 0:3], in_=halo[:]), reads=[B_halo], writes=B_zqk)
                if first_layer:
                    rmsnorm_to_hT()

            add_step(None, s_load)

            pend_silu = []
            for j4 in range(4):
                def s_qk(w, bw, j4=j4):
                    def evac(j, pi):
                        c = j4 * 4 + j
                        A("act", lambda h: h.activation(out=zqk[:, c, 3:3 + TT], in_=PSD[pi][:, 0:TT], func=AF.Identity, bias=smc("bqk", c)),
                          reads=[B_PSD[pi], B_smalls], writes=[B_zqk[c]])
                        A("pool", lambda h: h.tensor_copy(out=halo[:, c, :], in_=zqk[:, c, TT:TT + 3]), reads=[B_zqk[c]], writes=[B_halo])
                        ti = nxt("t32", NT32)
                        e = "dve"
                        A(e, lambda h: h.tensor_scalar(out=T32[ti][:], in0=zqk[:, c, 0:TT], scalar1=smc("cw", c * 4 + 0), scalar2=smc("cb", c),
                                                       op0=ALU.mult, op1=ALU.add), reads=[B_zqk[c], B_smalls], writes=[B_T32[ti]])
                        for jj in range(1, 4):
                            A(e, lambda h, jj=jj: h.scalar_tensor_tensor(out=T32[ti][:], in0=zqk[:, c, jj:jj + TT], scalar=smc("cw", c * 4 + jj),
                                                                         in1=T32[ti][:], op0=ALU.mult, op1=ALU.add),
                              reads=[B_zqk[c], B_smalls, B_T32[ti]], writes=[B_T32[ti]])
                        pend_silu.append((c, ti))
                        while len(pend_silu) > 2:
                            c_, ti_ = pend_silu.pop(0)
                            A("act", lambda h, c_=c_, ti_=ti_: h.activation(out=qkT[:, c_, :], in_=T32[ti_][:], func=AF.Silu), reads=[B_T32[ti_]], writes=[B_qkT[c_]])
                    fm_group(w, bw, 4, 0, evac)
                    if j4 == 3:
                        while pend_silu:
                            c_, ti_ = pend_silu.pop(0)
                            A("act", lambda h, c_=c_, ti_=ti_: h.activation(out=qkT[:, c_, :], in_=T32[ti_][:], func=AF.Silu), reads=[B_T32[ti_]], writes=[B_qkT[c_]])
                add_step(f"QK{j4}", s_qk)

            def s_gates(w, bw):
                wv = w[:, 0:KC * 8].rearrange("p (k n) -> p k n", n=8)
                for c in range(NSUB):
                    pi = nxt("psd", 4)
                    for kc in range(KC):
                        A("pe", lambda h, kc=kc, c=c, pi=pi: h.matmul(PSD[pi][:, 0:8], lhsT=hT[:, kc, c * 128:(c + 1) * 128], rhs=wv[:, kc, :],
                                                                       start=(kc == 0), stop=(kc == KC - 1)),
                          reads=[bw, B_hT[kc]], writes=[B_PSD[pi]])
                    A("dve", lambda h, c=c, pi=pi: h.tensor_tensor(out=gif[:, c, :], in0=PSD[pi][:, 0:8], in1=smc("bgate", 0, 8), op=ALU.add),
                      reads=[B_PSD[pi], B_smalls], writes=[B_gif])
                A("act", lambda h: h.activation(out=spl[:], in_=gif[:, :, 4:8], func=AF.Exp, scale=-1.0), reads=[B_gif], writes=[B_spl])
                A("act", lambda h: h.activation(out=spl[:], in_=spl[:], func=AF.Ln, bias=1.0), reads=[B_spl], writes=[B_spl])
                for c in range(NSUB):
                    pi = nxt("psd", 4)
                    A("pe", lambda h, c=c, pi=pi: h.matmul(PSD[pi][:, 0:4], lhsT=triu, rhs=spl[:, c, :], start=True, stop=True),
                      reads=[B_spl, B_consts], writes=[B_PSD[pi]])
                    A("pe", lambda h, c=c, pi=pi: h.matmul(PSD[pi][:, 4:8], lhsT=ones32, rhs=spl[:, c, :], start=True, stop=True),
                      reads=[B_spl, B_consts], writes=[B_PSD[pi]])
                    A("dve", lambda h, c=c, pi=pi: h.tensor_tensor(out=G1[:, c, 0:4], in0=PSD[pi][:, 0:4], in1=gif[:, c, 0:4], op=ALU.add),
                      reads=[B_PSD[pi], B_gif], writes=[B_G1])
                    A("dve", lambda h, c=c, pi=pi: h.scalar_tensor_tensor(out=G1[:, c, 4:8], in0=G1[:, c, 0:4], scalar=-LN16, in1=PSD[pi][:, 4:8],
                                                                          op0=ALU.add, op1=ALU.subtract),
                      reads=[B_PSD[pi], B_G1], writes=[B_G1])
                    A("dve", lambda h, c=c, pi=pi: h.tensor_scalar(out=G1[:, c, 8:16], in0=PSD[pi][:, 0:8], scalar1=-1.0, scalar2=None, op0=ALU.mult),
                      reads=[B_PSD[pi]], writes=[B_G1])
                A("act", lambda h: h.activation(out=GE[:], in_=G1[:], func=AF.Exp), reads=[B_G1], writes=[B_GE])

            add_step("G", s_gates)

            for hd in range(4):
                def s_m(w, bw, hd=hd):
                    if hd == 0:
                        alias_barrier(B_Um + B_sigo, B_Ur + B_srg)
                    def evac(c, pi):
                        A("dve", lambda h: h.tensor_copy(out=Vext[:, c, 0:256], in_=PSD[pi][:, 0:256]), reads=[B_PSD[pi]], writes=[B_Vext[c]])
                        if dbg in (2, 3):
                            A("act", lambda h: h.activation(out=sigo[:, c, hd, :], in_=PSD[pi][:, 256:512], func=AF.Copy), reads=[B_PSD[pi]], writes=[B_sigo[c * 4 + hd]])
                        else:
                            A("act", lambda h: h.activation(out=sigo[:, c, hd, :], in_=PSD[pi][:, 256:512], func=AF.Sigmoid), reads=[B_PSD[pi]], writes=[B_sigo[c * 4 + hd]])
                    tm_group(w, bw, 512, evac, bias=(dbg != 3))
                    if dbg in (1, 2, 3):
                        return
                    for c in range(NSUB):
                        ts = slice(c * 128, (c + 1) * 128)
                        qc = [2 * hd, 2 * hd + 1]
                        kcx = [8 + 2 * hd, 8 + 2 * hd + 1]
                        for j in range(2):
                            A("pe", lambda h, j=j: h.matmul(PSS[:, 0:128], lhsT=qkT[:, kcx[j], ts], rhs=qkT[:, qc[j], ts], start=(j == 0), stop=(j == 1)),
                              reads=[B_qkT[kcx[j]], B_qkT[qc[j]]], writes=[B_PSS])
                        for j in range(2):
                            A("pe", lambda h, j=j: h.transpose(out=PST[:, 512 + j * 128:512 + (j + 1) * 128], in_=qkT[:, kcx[j], ts], identity=identb[:]),
                              reads=[B_qkT[kcx[j]], B_cb], writes=[B_PSTk])
                        ki = nxt("kp", 2)
                        A("act", lambda h: h.activation(out=kp[ki][:], in_=PST[:, 512:768], func=AF.Copy, scale=GE[:, c, 4 + hd:5 + hd]),
                          reads=[B_PSTk, B_GE], writes=[B_kp[ki]])
                        si = nxt("spp", 2)
                        A("dve", lambda h: h.scalar_tensor_tensor(out=Spp[si][:], in0=PSS[:, 0:128], scalar=GE[:, c, hd:hd + 1], in1=maskm,
                                                                  op0=ALU.mult, op1=ALU.mult), reads=[B_PSS, B_GE, B_consts], writes=[B_Spp[si]])
                        oi = nxt("pso", 2)
                        A("pe", lambda h: h.matmul(PSO[oi][:, 0:257], lhsT=Spp[si][:], rhs=Vext[:, c, 0:257], start=True, stop=False),
                          reads=[B_Spp[si], B_Vext[c]], writes=[B_PSO[oi]])
                        for j in range(2):
                            A("pe", lambda h, j=j: h.matmul(PSO[oi][:, 0:257], lhsT=qkT[:, qc[j], ts], rhs=Cbf[:, j, hd, 0:257], start=False, stop=(j == 1)),
                              reads=[B_qkT[qc[j]], B_Cbf[hd]], writes=[B_PSO[oi]])
                        A("act", lambda h: h.activation(out=Um[:, c, hd, 0:257], in_=PSO[oi][:, 0:257], func=AF.Copy),
                          reads=[B_PSO[oi]], writes=[B_Um[c * 4 + hd]])
                        A("dve", lambda h: h.bn_stats(out=STm[:, c, hd, :], in_=PSO[oi][:, 0:256]), reads=[B_PSO[oi]], writes=[B_STm[c * 4 + hd]])
                        for j in range(2):
                            A("pe", lambda h, j=j: h.matmul(PSU[:, j, 0:257], lhsT=kp[ki][:, j * 128:(j + 1) * 128], rhs=Vext[:, c, 0:257], start=True, stop=True),
                              reads=[B_kp[ki], B_Vext[c]], writes=[B_PSU])
                        A("dve", lambda h: h.scalar_tensor_tensor(out=Cbf[:, :, hd, 0:257], in0=Cbf[:, :, hd, 0:257], scalar=GE[:, c, 12 + hd:13 + hd],
                                                                  in1=PSU[:, :, 0:257], op0=ALU.mult, op1=ALU.add),
                          reads=[B_Cbf[hd], B_GE, B_PSU], writes=[B_Cbf[hd]])
                add_step(f"M{hd}", s_m)

            for hd in range(4):
                def s_rqk(w, bw, hd=hd):
                    def evac(j, pi):
                        A("act", lambda h: h.activation(out=rpre[:, j, :], in_=PSD[pi][:, 0:TT], func=AF.Identity, bias=smc("brqk", hd * 4 + j)),
                          reads=[B_PSD[pi], B_smalls], writes=[B_rpre[j]])
                    fm_group(w, bw, 4, 0, evac)
                    for b0 in (0, 2):
                        x1, x2 = rpre[:, b0, :], rpre[:, b0 + 1, :]
                        ta, tb = nxt("t32", NT32), nxt("t32", NT32)
                        e = ew_eng()
                        A(e, lambda h: h.tensor_tensor(out=T32[ta][:], in0=x1, in1=cosb[:], op=ALU.mult), reads=[B_rpre[b0], B_cos], writes=[B_T32[ta]])
                        A(e, lambda h: h.tensor_tensor(out=T32[tb][:], in0=x2, in1=sinb[:], op=ALU.mult), reads=[B_rpre[b0 + 1], B_sin], writes=[B_T32[tb]])
                        A(e, lambda h: h.tensor_tensor(out=rqkT[:, b0, :], in0=T32[ta][:], in1=T32[tb][:], op=ALU.subtract),
                          reads=[B_T32[ta], B_T32[tb]], writes=[B_rqkT[b0]])
                        tc_, td = nxt("t32", NT32), nxt("t32", NT32)
                        e = ew_eng()
                        A(e, lambda h: h.tensor_tensor(out=T32[tc_][:], in0=x1, in1=sinb[:], op=ALU.mult), reads=[B_rpre[b0], B_sin], writes=[B_T32[tc_]])
                        A(e, lambda h: h.tensor_tensor(out=T32[td][:], in0=x2, in1=cosb[:], op=ALU.mult), reads=[B_rpre[b0 + 1], B_cos], writes=[B_T32[td]])
                        A(e, lambda h: h.tensor_tensor(out=rqkT[:, b0 + 1, :], in0=T32[tc_][:], in1=T32[td][:], op=ALU.add),
                          reads=[B_T32[tc_], B_T32[td]], writes=[B_rqkT[b0 + 1]])
                add_step(f"RQK{hd}", s_rqk)

                def s_rv(w, bw, hd=hd):
                    def evac(c, pi):
                        A("dve", lambda h: h.tensor_copy(out=rv[:, c, :], in_=PSD[pi][:, 0:512]), reads=[B_PSD[pi]], writes=[B_rv[c]])
                    tm_group(w, bw, 512, evac)
                    if hd == 0:
                        finalize_heads(True)
                        alias_barrier(B_Ur + B_srg, B_Um + B_sigo)
                add_step(f"RV{hd}", s_rv)

                def s_rg(w, bw, hd=hd):
                    def evac(c, pi):
                        A("act", lambda h: h.activation(out=srg[:, c, hd, :], in_=PSD[pi][:, 0:512], func=AF.Silu), reads=[B_PSD[pi]], writes=[B_srg[c * 4 + hd]])
                    tm_group(w, bw, 512, evac)
                    maskr = consts[:, C_MASKR + hd * 128:C_MASKR + (hd + 1) * 128]
                    rsc = consts[:, C_RSC + hd:C_RSC + hd + 1]
                    kdec = consts[:, C_KDEC + hd:C_KDEC + hd + 1]
                    cdec = float((1.0 - 2.0 ** (-5.0 - hd)) ** 128)
                    for c in range(NSUB):
                        ts = slice(c * 128, (c + 1) * 128)
                        for j in range(2):
                            A("pe", lambda h, j=j: h.matmul(PSS[:, 0:128], lhsT=rqkT[:, 2 + j, ts], rhs=rqkT[:, j, ts], start=(j == 0), stop=(j == 1)),
                              reads=[B_rqkT[2 + j], B_rqkT[j]], writes=[B_PSS])
                        for j in range(2):
                            A("pe", lambda h, j=j: h.transpose(out=PST[:, 512 + j * 128:512 + (j + 1) * 128], in_=rqkT[:, 2 + j, ts], identity=identb[:]),
                              reads=[B_rqkT[2 + j], B_cb], writes=[B_PSTk])
                        ki = nxt("kp", 2)
                        A("act", lambda h: h.activation(out=kp[ki][:], in_=PST[:, 512:768], func=AF.Copy, scale=kdec),
                          reads=[B_PSTk, B_consts], writes=[B_kp[ki]])
                        si = nxt("spp", 2)
                        A("dve", lambda h: h.tensor_tensor(out=Spp[si][:], in0=PSS[:, 0:128], in1=maskr, op=ALU.mult),
                          reads=[B_PSS, B_consts], writes=[B_Spp[si]])
                        oi = nxt("pso", 2)
                        A("pe", lambda h: h.matmul(PSO[oi][:, 0:512], lhsT=Spp[si][:], rhs=rv[:, c, :], start=True, stop=False),
                          reads=[B_Spp[si], B_rv[c]], writes=[B_PSO[oi]])
                        for j in range(2):
                            A("pe", lambda h, j=j: h.matmul(PSO[oi][:, 0:512], lhsT=rqkT[:, j, ts], rhs=Rbf[:, j, hd, :], start=False, stop=(j == 1)),
                              reads=[B_rqkT[j], B_Rbf[hd]], writes=[B_PSO[oi]])
                        A("act", lambda h: h.activation(out=Ur[:, c, hd, :], in_=PSO[oi][:, 0:512], func=AF.Copy),
                          reads=[B_PSO[oi]], writes=[B_Ur[c * 4 + hd]])
                        A("dve", lambda h: h.bn_stats(out=STr[:, c, hd, :], in_=PSO[oi][:, 0:512]), reads=[B_PSO[oi]], writes=[B_STr[c * 4 + hd]])
                        for j in range(2):
                            A("pe", lambda h, j=j: h.matmul(PSU[:, j, :], lhsT=kp[ki][:, j * 128:(j + 1) * 128], rhs=rv[:, c, :], start=True, stop=True),
                              reads=[B_kp[ki], B_rv[c]], writes=[B_PSU])
                        A("dve", lambda h: h.scalar_tensor_tensor(out=Rbf[:, :, hd, :], in0=Rbf[:, :, hd, :], scalar=cdec, in1=PSU[:, :, :],
                                                                  op0=ALU.mult, op1=ALU.add), reads=[B_Rbf[hd], B_PSU], writes=[B_Rbf[hd]])
                add_step(f"RG{hd}", s_rg)

            for j2 in range(2):
                def s_gm(w, bw, j2=j2):
                    if j2 == 0:
                        alias_barrier(B_sgm + B_sgr + B_mrg, B_zqk + B_qkT)

                    def evac(j, pi):
                        c = j2 * 4 + j
                        A("act", lambda h: h.activation(out=sgm[:, c, :], in_=PSD[pi][:, 0:TT], func=AF.Sigmoid, bias=smc("bg", c)),
                          reads=[B_PSD[pi], B_smalls], writes=[B_sgm[c]])
                    fm_group(w, bw, 4, 0, evac)
                add_step(f"GM{j2}", s_gm)
            for j2 in range(2):
                def s_gr(w, bw, j2=j2):
                    def evac(j, pi):
                        c = j2 * 4 + j
                        A("act", lambda h: h.activation(out=sgr[:, c, :], in_=PSD[pi][:, 0:TT], func=AF.Sigmoid, bias=smc("bg", 8 + c)),
                          reads=[B_PSD[pi], B_smalls], writes=[B_sgr[c]])
                    fm_group(w, bw, 4, 0, evac)
                    if j2 == 1:
                        alias_barrier(B_yrT, B_Vext + B_rpre + B_rqkT + B_rv)
                        finalize_heads(False)
                add_step(f"GR{j2}", s_gr)

            for j2 in range(2):
                def s_bm(w, bw, j2=j2):
                    wv = w[:, 0:8 * 512].rearrange("p (k n) -> p k n", n=512)
                    for j in range(4):
                        n = j2 * 4 + j
                        pi = nxt("psd", 4)
                        for kc in range(8):
                            A("pe", lambda h, kc=kc, j=j, pi=pi: h.matmul(PSD[pi][:, 0:TT], lhsT=wv[:, kc, j * 128:(j + 1) * 128], rhs=ymT[:, kc, :],
                                                                           start=(kc == 0), stop=(kc == 7)), reads=[bw, B_ymT[kc]], writes=[B_PSD[pi]])
                        A("dve", lambda h, n=n, pi=pi: h.tensor_tensor(out=mrgT[:, n, :], in0=PSD[pi][:, 0:TT], in1=sgm[:, n, :], op=ALU.mult),
                          reads=[B_PSD[pi], B_sgm[n]], writes=[B_mrg[n]])
                add_step(f"BM{j2}", s_bm)
            for j4 in range(4):
                def s_br(w, bw, j4=j4):
                    wv = w[:, 0:16 * 256].rearrange("p (k n) -> p k n", n=256)
                    for j in range(2):
                        n = j4 * 2 + j
                        pi = nxt("psd", 4)
                        for kc in range(16):
                            A("pe", lambda h, kc=kc, j=j, pi=pi: h.matmul(PSD[pi][:, 0:TT], lhsT=wv[:, kc, j * 128:(j + 1) * 128], rhs=yrT[:, kc, :],
                                                                           start=(kc == 0), stop=(kc == 15)), reads=[bw, B_yrT[kc]], writes=[B_PSD[pi]])
                        ti = nxt("t32", NT32)
                        A("dve", lambda h, n=n, pi=pi, ti=ti: h.tensor_tensor(out=T32[ti][:], in0=PSD[pi][:, 0:TT], in1=sgr[:, n, :], op=ALU.mult),
                          reads=[B_PSD[pi], B_sgr[n]], writes=[B_T32[ti]])
                        A("pool", lambda h, n=n, ti=ti: h.tensor_tensor(out=mrgT[:, n, :], in0=T32[ti][:], in1=mrgT[:, n, :], op=ALU.add),
                          reads=[B_T32[ti], B_mrg[n]], writes=[B_mrg[n]])
                add_step(f"BR{j4}", s_br)
            for j2 in range(2):
                def s_o(w, bw, j2=j2):
                    wv = w[:, 0:8 * 512].rearrange("p (k n) -> p k n", n=512)
                    for j in range(4):
                        n = j2 * 4 + j
                        pi = nxt("psd", 4)
                        for kc in range(8):
                            A("pe", lambda h, kc=kc, j=j, pi=pi: h.matmul(PSD[pi][:, 0:TT], lhsT=wv[:, kc, j * 128:(j + 1) * 128], rhs=mrgT[:, kc, :],
                                                                           start=(kc == 0), stop=(kc == 7)), reads=[bw, B_mrg[kc]], writes=[B_PSD[pi]])
                        A("dve", lambda h, n=n, pi=pi: h.tensor_tensor(out=xT[:, n, :], in0=PSD[pi][:, 0:TT], in1=xT[:, n, :], op=ALU.add),
                          reads=[B_PSD[pi], B_xT[n]], writes=[B_xT[n]])
                        rms_accum(n)
                add_step(f"O{j2}", s_o)

            for j8 in range(8):
                def s_f1(w, bw, j8=j8):
                    if j8 == 0:
                        rmsnorm_to_hT(pre=True)
                        alias_barrier(B_hid, B_sgm + B_sgr + B_mrg + B_zqk + B_qkT)

                    def evac(j, pi):
                        c = j8 * 4 + j
                        ri = nxt("rel", 2)
                        A("act", lambda h: h.activation(out=rel[ri][:], in_=PSD[pi][:, 0:TT], func=AF.Relu, bias=smc("bf1", c)),
                          reads=[B_PSD[pi], B_smalls], writes=[B_rel[ri]])
                        A("dve", lambda h: h.scalar_tensor_tensor(out=hidT[:, c, :], in0=PSD[pi][:, 0:TT], scalar=smc("bf1", c), in1=rel[ri][:],
                                                                  op0=ALU.add, op1=ALU.mult), reads=[B_PSD[pi], B_smalls, B_rel[ri]], writes=[B_hid[c]])
                    fm_group(w, bw, 4, 0, evac)
                add_step(f"F1{j8}", s_f1)
            for n in range(8):
                def s_f2(w, bw, n=n):
                    wv = w[:, 0:32 * 128].rearrange("p (k n) -> p k n", n=128)
                    pi = nxt("psd", 4)
                    for kc in range(32):
                        A("pe", lambda h, kc=kc, pi=pi: h.matmul(PSD[pi][:, 0:TT], lhsT=wv[:, kc, :], rhs=hidT[:, kc, :], start=(kc == 0), stop=(kc == 31)),
                          reads=[bw, B_hid[kc]], writes=[B_PSD[pi]])
                    A("dve", lambda h, pi=pi: h.scalar_tensor_tensor(out=xT[:, n, :], in0=PSD[pi][:, 0:TT], scalar=smc("bf2", n), in1=xT[:, n, :],
                                                                     op0=ALU.add, op1=ALU.add), reads=[B_PSD[pi], B_smalls, B_xT[n]], writes=[B_xT[n]])
                    rms_accum(n)
                add_step(f"F2{n}", s_f2)

            for j2 in range(2):
                def s_pg(w, bw, j2=j2):
                    if j2 == 0:
                        rmsnorm_to_hT(pre=True)
                        alias_barrier(B_sgT, B_hid)
                        for c in range(NSUB):
                            pi_ = nxt("ptok", 2)
                            A("pool", lambda h, c=c, pi_=pi_: h.dma_start(out=ptok[pi_][:], in_=p_d[l, t0 + c * 128:t0 + (c + 1) * 128, :]),
                              writes=[B_ptok[pi_]], dsem=D_ptok[pi_])
                            A("pool", lambda h, pi_=pi_: h.tensor_copy(out=pbf[pi_][:], in_=ptok[pi_][:]), reads=[B_ptok[pi_]], writes=[B_pbf[pi_]])
                            for j in range(2):
                                A("pe", lambda h, j=j, pi_=pi_: h.transpose(out=PST[:, j * 128:(j + 1) * 128], in_=pbf[pi_][:, j * 128:(j + 1) * 128], identity=identb[:]),
                                  reads=[B_pbf[pi_], B_cb], writes=[B_PSTy])
                            A("dve", lambda h, c=c: h.tensor_copy(out=pT[:, :, c * 128:(c + 1) * 128], in_=PST[:, 0:256].rearrange("p (j n) -> p j n", n=128)),
                              reads=[B_PSTy], writes=[B_pT])

                    def evac(j, pi):
                        c = j2 * 4 + j
                        A("act", lambda h: h.activation(out=sgT[:, c, :], in_=PSD[pi][:, 0:TT], func=AF.Sigmoid), reads=[B_PSD[pi]], writes=[B_sgT[c]])
                    fm_group(w, bw, 4, 0, evac)
                add_step(f"PG{j2}", s_pg)

            def s_pe(w, bw):
                wv = w[:, 0:2 * 1024].rearrange("p (k n) -> p k n", n=1024)
                for n in range(8):
                    pi = nxt("psd", 4)
                    for kc in range(2):
                        A("pe", lambda h, kc=kc, n=n, pi=pi: h.matmul(PSD[pi][:, 0:TT], lhsT=wv[:, kc, n * 128:(n + 1) * 128], rhs=pT[:, kc, :],
                                                                       start=(kc == 0), stop=(kc == 1)), reads=[bw, B_pT], writes=[B_PSD[pi]])
                    ti = nxt("t32", NT32)
                    A("dve", lambda h, n=n, pi=pi, ti=ti: h.tensor_tensor(out=T32[ti][:], in0=PSD[pi][:, 0:TT], in1=sgT[:, n, :], op=ALU.mult),
                      reads=[B_PSD[pi], B_sgT[n]], writes=[B_T32[ti]])
                    A("pool", lambda h, n=n, ti=ti: h.tensor_tensor(out=xT[:, n, :], in0=T32[ti][:], in1=xT[:, n, :], op=ALU.add),
                      reads=[B_T32[ti], B_xT[n]], writes=[B_xT[n]])
                    rms_accum(n)
            add_step("PE", s_pe)

            def s_store(w, bw):
                if not last_layer:
                    A("sp", lambda h: h.dma_start(out=xscr[t], in_=xT[:].rearrange("p k t -> p (k t)")), reads=B_xT, writes=[B_xscr[t]], dsem=D_xs)
                    rmsnorm_to_hT(pre=True)
                    A("sp", lambda h: h.dma_start(out=hscr[t], in_=hT[:].rearrange("p k t -> p (k t)")), reads=B_hT, writes=[B_hscr[t]], dsem=D_hs)
                else:
                    A("act", lambda h: h.activation(out=rstd[:], in_=PSS[:, 0:TT], func=AF.Ln, bias=EPS), reads=[B_PSS], writes=[B_rstd])
                    A("act", lambda h: h.activation(out=rstd[:], in_=rstd[:], func=AF.Exp, scale=-0.5), reads=[B_rstd], writes=[B_rstd])
                    for kc in range(KC):
                        A("dve", lambda h, kc=kc: h.scalar_tensor_tensor(out=xT[:, kc, :], in0=xT[:, kc, :], scalar=smc("fg", kc), in1=rstd[:],
                                                                         op0=ALU.mult, op1=ALU.mult), reads=[B_xT[kc], B_rstd, B_smalls], writes=[B_xT[kc]])
                    for c in range(NSUB):
                        xi = nxt("xtok", NXT)
                        for half in range(2):
                            pi = nxt("psd", 4)
                            for q in range(4):
                                kc = half * 4 + q
                                A("pe", lambda h, q=q, kc=kc, c=c, pi=pi: h.transpose(out=PSD[pi][:, q * 128:(q + 1) * 128],
                                                                                      in_=xT[:, kc, c * 128:(c + 1) * 128], identity=ident32),
                                  reads=[B_xT[kc], B_consts], writes=[B_PSD[pi]])
                            A("act", lambda h, half=half, xi=xi, pi=pi: h.activation(out=xtok[xi][:, half * 512:(half + 1) * 512], in_=PSD[pi][:, :], func=AF.Copy),
                              reads=[B_PSD[pi]], writes=[B_xtok[xi]])
                        A("sp", lambda h, c=c, xi=xi: h.dma_start(out=out_d[t0 + c * 128:t0 + (c + 1) * 128, :], in_=xtok[xi][:]),
                          reads=[B_xtok[xi]], writes=[B_out], dsem=D_ost[xi])
            add_step(None, s_store)

        A("sp", lambda h: h.dma_start(out=smalls[:], in_=smalls_d[0]), writes=[B_smalls], dsem=D_sm)
        for job in ([] if skip_conv else conv_jobs(0)):
            do_conv_job(job, smalls, B_smalls)

        for l in range(NL):
            nxt_jobs = conv_jobs(l + 1) if l + 1 < NL else []
            per_tile = (len(nxt_jobs) + NT - 1) // NT if nxt_jobs else 0
            for t in range(NT):
                tile_steps(l, t, l == 0, l == NL - 1)
                if nxt_jobs:
                    jl = nxt_jobs[t * per_tile:(t + 1) * per_tile]
                    steps.append(("conv", jl, l + 1))

        wsteps = [i for i, s_ in enumerate(steps) if s_[0] is not None and s_[0] != "conv"]
        wpos = {i: k for k, i in enumerate(wsteps)}
        issued = [0]

        def issue_loads(upto):
            while issued[0] < len(wsteps) and issued[0] <= upto:
                k = issued[0]
                gi = steps[wsteps[k]][0]
                b = BLOCKS[gi % NBLK]
                width = b["kcb"] * b["ncb"]
                s = k % NSLOT
                A("sp", lambda h, gi=gi, s=s, width=width: h.dma_start(out=wsl[s][:, 0:width], in_=wscr[gi][:, 0:width]),
                  reads=[B_wscr[gi]], writes=[B_wsl[s]], dsem=D_wsl[s])
                if b["bias"]:
                    nb_ = b["ncb"]
                    A("sp", lambda h, gi=gi, s=s, nb_=nb_: h.dma_start(out=wsl[s][0:1, BIASOFF:BIASOFF + nb_], in_=wscr[gi][0:1, BIASOFF:BIASOFF + nb_]),
                      reads=[B_wscr[gi]], writes=[B_wsl[s]], dsem=D_wsl[s])
                issued[0] += 1

        smalls2 = sb("smalls2", [P, NSM], F32)
        B_smalls2 = Buf("smalls2")
        D_sm2 = newdsem("sm2")
        conv_layer_loaded = [-1]
        if max_steps is not None:
            steps = steps[:max_steps]
            wsteps = [i for i, s_ in enumerate(steps) if s_[0] is not None and s_[0] != "conv"]
            wpos = {i: k for k, i in enumerate(wsteps)}
        for i, s_ in enumerate(steps):
            if s_[0] == "conv":
                _, jl, ln = s_
                if conv_layer_loaded[0] != ln:
                    A("sp", lambda h, ln=ln: h.dma_start(out=smalls2[:], in_=smalls_d[ln]), writes=[B_smalls2], dsem=D_sm2)
                    conv_layer_loaded[0] = ln
                for job in jl:
                    do_conv_job(job, smalls2, B_smalls2)
                continue
            if s_[0] is None:
                s_[1](None, None)
            else:
                k = wpos[i]
                issue_loads(k + NSLOT - 1)
                slot = k % NSLOT
                s_[1](wsl[slot], B_wsl[slot])

        fin = [(d.sem, d.count) for d in D_ost + [D_xs, D_hs] if d.count > 0]
        S_.wait_only("sp", fin)
        S_.emit()
    return nc


def _smalls(inp, NLW):
    sm = np.zeros((NLW, P, NSM), np.float32)

    def colmajor(v):
        return np.ascontiguousarray(v.reshape(-1, P).T)

    for l in range(NLW):
        b = inp["b_in"][l]
        sm[l, :, SM["bqk"]:SM["bqk"] + 16] = colmajor(b[0:2048])
        for h in range(4):
            sm[l, :, SM["brqk"] + h * 4:SM["brqk"] + h * 4 + 2] = colmajor(b[RQ + h * 256:RQ + (h + 1) * 256])
            sm[l, :, SM["brqk"] + h * 4 + 2:SM["brqk"] + h * 4 + 4] = colmajor(b[RK + h * 256:RK + (h + 1) * 256])
        sm[l, :, SM["bg"]:SM["bg"] + 8] = colmajor(b[GM:GM + 1024])
        sm[l, :, SM["bg"] + 8:SM["bg"] + 16] = colmajor(b[GR:GR + 1024])
        cw = inp["conv_w"][l]
        for c in range(16):
            sm[l, :, SM["cw"] + c * 4:SM["cw"] + c * 4 + 4] = cw[:, c * P:(c + 1) * P].T
        sm[l, :, SM["cb"]:SM["cb"] + 16] = colmajor(inp["conv_b"][l])
        sm[l, :, SM["bf1"]:SM["bf1"] + 32] = colmajor(inp["b_ff1"][l])
        sm[l, :, SM["bf2"]:SM["bf2"] + 8] = colmajor(inp["b_ff2"][l])
        sm[l, :, SM["g1"]:SM["g1"] + 8] = colmajor(inp["norm1_g"][l])
        sm[l, :, SM["gmn"]:SM["gmn"] + 8] = colmajor(inp["m_norm_g"][l])
        sm[l, :, SM["grn"]:SM["grn"] + 16] = colmajor(inp["r_norm_g"][l])
        sm[l, :, SM["g2"]:SM["g2"] + 8] = colmajor(inp["norm2_g"][l])
        sm[l, :, SM["g3"]:SM["g3"] + 8] = colmajor(inp["norm3_g"][l])
        sm[l, :, SM["bgate"]:SM["bgate"] + 8] = b[MI:MI + 8][None, :]
        sm[l, :, SM["fg"]:SM["fg"] + 8] = colmajor(inp["final_g"])
    return sm


def _consts():
    c = np.zeros((P, NCST), np.float64)
    idx = np.arange(P)
    c[:, C_ID:C_ID + P] = np.eye(P)
    tri = (idx[:, None] <= idx[None, :]).astype(np.float64)
    c[:, C_TRIU:C_TRIU + P] = tri
    c[:, C_ONES:C_ONES + P] = 1.0
    c[:, C_MASKM:C_MASKM + P] = tri / 16.0
    for h in range(4):
        g = 1.0 - 2.0 ** (-5.0 - h)
        c[:, C_MASKR + h * P:C_MASKR + (h + 1) * P] = tri * (g ** (-(idx[:, None] + 1.0))) / 16.0
        c[:, C_RSC + h] = g ** (idx + 1.0)
        c[:, C_KDEC + h] = g ** (127.0 - idx) / 16.0
    return c.astype(np.float32)


def _rope_tables(S):
    pos = np.arange(S, dtype=np.float32)
    inv_freq = (10000.0 ** (-np.arange(0, 256, 2, dtype=np.float32) / np.float32(256))).astype(np.float32)
    ang = (pos[None, :] * inv_freq[:, None]).astype(np.float32)
    return np.cos(ang).astype(np.float32), np.sin(ang).astype(np.float32)


_WNAMES = ("w_in", "w_bm", "w_br", "w_out", "w_ff1", "w_ff2", "w_pe_gate", "w_pe")


def run_model(inp, NL, TT=256, n_cores=8, **bkw):
    x = np.asarray(inp["x"], np.float32)
    p = np.asarray(inp["p"], np.float32)
    B, S, _ = x.shape
    NLW = inp["w_in"].shape[0]
    nc = build(NL, S, TT=TT, NLW=NLW, **bkw)
    sm = _smalls(inp, NLW)
    cst = _consts()
    cos_t, sin_t = _rope_tables(S)
    shared = {k: np.ascontiguousarray(np.asarray(inp[k], np.float32)) for k in _WNAMES}
    shared["b_in"] = np.ascontiguousarray(np.asarray(inp["b_in"], np.float32))
    shared.update(smalls=sm, consts=cst, cos_t=cos_t, sin_t=sin_t)
    in_maps = []
    for c in range(n_cores):
        b = c % B
        m = dict(shared)
        m["x"] = np.ascontiguousarray(x[b])
        m["p"] = np.ascontiguousarray(p[:, b])
        in_maps.append(m)
    res = run_bass_kernel_spmd(nc, in_maps, core_ids=list(range(n_cores)))
    out = np.stack([res.results[b]["out"] for b in range(B)], axis=0)
    return out.astype(np.float32)


def kernel(**inputs):
    return run_model(inputs, NL=4, TT=512, n_cores=4)
```

```python
import contextlib
import math
import numpy as np
import concourse.bass as bass
import concourse.mybir as mybir
from concourse.bass_utils import run_bass_kernel_spmd

F32 = mybir.dt.float32
BF16 = mybir.dt.bfloat16
AF = mybir.ActivationFunctionType
ALU = mybir.AluOpType
AX = mybir.AxisListType

P = 128
D = 1024
KC = 8
EPS = 1e-6
N_IN = 12296
MQ, MK, MV, MO, MI, MF, RQ, RK, RV, RG, GM, GR = 0, 1024, 2048, 3072, 4096, 4100, 4104, 5128, 6152, 8200, 10248, 11272
DFF = 4096
WBLK = 4608
BIASOFF = 4096
LN16 = math.log(16.0)

SM = {}
_o = 0
for _n, _w in (("bqk", 16), ("brqk", 16), ("bg", 16), ("cw", 64), ("cb", 16), ("bf1", 32), ("bf2", 8),
               ("g1", 8), ("gmn", 8), ("grn", 16), ("g2", 8), ("g3", 8), ("bgate", 8), ("fg", 8)):
    SM[_n] = _o
    _o += _w
NSM = _o
C_ID, C_TRIU, C_ONES, C_MASKM, C_MASKR, C_RSC, C_KDEC = 0, 128, 256, 384, 512, 1024, 1028
NCST = 1032


class Buf:
    __slots__ = ("name", "w", "r", "excl")

    def __init__(self, name, excl=False):
        self.name = name
        self.w = None
        self.r = {}
        self.excl = excl


class DSem:
    def __init__(self, sem):
        self.sem = sem
        self.count = 0


class _Eng:
    def __init__(self, name, sem, same_sync):
        self.name = name
        self.sem = sem
        self.count = 0
        self.ops = []
        self.waited = {}
        self.same_sync = same_sync


class _Rec:
    def __init__(self):
        self.call = None

    def __getattr__(self, name):
        def f(*a, **k):
            self.call = (name, a, k)
            return self
        return f


class Sched:
    def __init__(self, nc, sems, same_sync=True):
        self.nc = nc
        self.engs = {}
        for name, same in (("pe", False), ("act", same_sync), ("dve", same_sync), ("pool", same_sync), ("sp", False)):
            self.engs[name] = _Eng(name, sems[name], same)
        self.nops = 0

    def add(self, eng, fn, reads=(), writes=(), dsem=None):
        E = self.engs[eng]
        need = {}

        def dep(t):
            s, v = t
            k = id(s)
            if k not in need or need[k][1] < v:
                need[k] = (s, v)

        for b in reads:
            if b.w is not None:
                dep(b.w)
            if b.excl:
                for k_, t in b.r.items():
                    if k_ != id(E.sem):
                        dep(t)
        for b in writes:
            if b.w is not None:
                dep(b.w)
            for t in b.r.values():
                dep(t)
        waits = []
        for k, (s, v) in need.items():
            if s is E.sem and not E.same_sync:
                continue
            if E.waited.get(k, 0) >= v:
                continue
            E.waited[k] = v
            waits.append((s, v))
        if dsem is None:
            E.count += 1
            tok = (E.sem, E.count)
            inc = (E.sem, 1)
        else:
            dsem.count += 16
            tok = (dsem.sem, dsem.count)
            inc = (dsem.sem, 16)
        rec = _Rec()
        fn(rec)
        E.ops.append((waits, rec.call, inc))
        for b in reads:
            k = id(tok[0])
            if k not in b.r or b.r[k][1] < tok[1]:
                b.r[k] = tok
        for b in writes:
            b.w = tok
            b.r = {}
        self.nops += 1
        return tok

    def wait_only(self, eng, toks):
        E = self.engs[eng]
        waits = []
        for (s, v) in toks:
            if E.waited.get(id(s), 0) >= v:
                continue
            E.waited[id(s)] = v
            waits.append((s, v))
        E.ops.append((waits, None, None))

    def emit(self):
        nc = self.nc

        def run(h, name):
            for waits, fn, inc in self.engs[name].ops:
                for (s, v) in waits:
                    h.wait_ge(s, v)
                if fn is not None:
                    getattr(h, fn[0])(*fn[1], **fn[2]).then_inc(inc[0], inc[1])

        with nc.Block() as block:
            @block.tensor
            def _(h):
                run(h, "pe")

            @block.scalar
            def _(h):
                run(h, "act")

            @block.vector
            def _(h):
                run(h, "dve")

            @block.gpsimd
            def _(h):
                run(h, "pool")

            @block.sync
            def _(h):
                run(h, "sp")


def alias_barrier(new_bufs, old_bufs):
    for nb in new_bufs:
        for ob in old_bufs:
            if ob.w is not None:
                k = id(ob.w[0])
                if k not in nb.r or nb.r[k][1] < ob.w[1]:
                    nb.r[k] = ob.w
            for k, t in ob.r.items():
                if k not in nb.r or nb.r[k][1] < t[1]:
                    nb.r[k] = t


def block_table():
    B = []

    def blk(name, kcb, ncb, pieces, bias=()):
        B.append(dict(name=name, kcb=kcb, ncb=ncb, pieces=pieces, bias=bias))

    for j in range(4):
        blk(f"QK{j}", 8, 512, [("w_in", j * 512, 512, 0, "g1")])
    blk("G", 8, 8, [("w_in", MI, 8, 0, "g1")])
    for h in range(4):
        blk(f"M{h}", 8, 512, [("w_in", MV + h * 256, 256, 0, "g1"), ("w_in", MO + h * 256, 256, 256, "g1")],
            bias=[(MV + h * 256, 256, 0), (MO + h * 256, 256, 256)])
    for h in range(4):
        blk(f"RQK{h}", 8, 512, [("w_in", RQ + h * 256, 256, 0, "g1"), ("w_in", RK + h * 256, 256, 256, "g1")])
        blk(f"RV{h}", 8, 512, [("w_in", RV + h * 512, 512, 0, "g1")], bias=[(RV + h * 512, 512, 0)])
        blk(f"RG{h}", 8, 512, [("w_in", RG + h * 512, 512, 0, "g1")], bias=[(RG + h * 512, 512, 0)])
    for j in range(2):
        blk(f"GM{j}", 8, 512, [("w_in", GM + j * 512, 512, 0, "g1")])
    for j in range(2):
        blk(f"GR{j}", 8, 512, [("w_in", GR + j * 512, 512, 0, "g1")])
    for j in range(2):
        blk(f"BM{j}", 8, 512, [("w_bm", j * 512, 512, 0, "gmn")])
    for j in range(4):
        blk(f"BR{j}", 16, 256, [("w_br", j * 256, 256, 0, "grn")])
    for j in range(2):
        blk(f"O{j}", 8, 512, [("w_out", j * 512, 512, 0, None)])
    for j in range(8):
        blk(f"F1{j}", 8, 512, [("w_ff1", j * 512, 512, 0, "g2")])
    for j in range(8):
        blk(f"F2{j}", 32, 128, [("w_ff2", j * 128, 128, 0, None)])
    for j in range(2):
        blk(f"PG{j}", 8, 512, [("w_pe_gate", j * 512, 512, 0, "g3")])
    blk("PE", 2, 1024, [("w_pe", 0, 1024, 0, None)])
    return B


BLOCKS = block_table()
NBLK = len(BLOCKS)
BIDX = {b["name"]: i for i, b in enumerate(BLOCKS)}


def build(NL, S, TT=256, NLW=4, same_sync=True, NSLOT=3, NSTAGE=1, NXT=1, max_steps=None, skip_conv=False, dbg=0):
    NSUB = TT // 128
    NT = S // TT
    assert S % TT == 0
    nc = bass.Bass("TRN2", target_bir_lowering=False)

    def din(name, shape):
        return nc.dram_tensor(name, list(shape), F32, kind="ExternalInput").ap()

    x_d = din("x", [S, D])
    p_d = din("p", [NLW, S, 256])
    wsrc = {
        "w_in": din("w_in", [NLW, D, N_IN]),
        "w_bm": din("w_bm", [NLW, 1024, D]),
        "w_br": din("w_br", [NLW, 2048, D]),
        "w_out": din("w_out", [NLW, D, D]),
        "w_ff1": din("w_ff1", [NLW, D, DFF]),
        "w_ff2": din("w_ff2", [NLW, DFF, D]),
        "w_pe_gate": din("w_pe_gate", [NLW, D, D]),
        "w_pe": din("w_pe", [NLW, 256, D]),
    }
    b_in_d = din("b_in", [NLW, N_IN])
    smalls_d = din("smalls", [NLW, P, NSM])
    consts_d = din("consts", [P, NCST])
    cos_d = din("cos_t", [P, S])
    sin_d = din("sin_t", [P, S])
    out_d = nc.dram_tensor("out", [S, D], F32, kind="ExternalOutput").ap()
    wscr = nc.dram_tensor("wscr", [NL * NBLK, P, WBLK], BF16).ap()
    xscr = nc.dram_tensor("xscr", [NT, P, KC * TT], F32).ap()

    st = contextlib.ExitStack()
    with st:
        sems = {n: st.enter_context(nc.semaphore("s_" + n)) for n in ("pe", "act", "dve", "pool", "sp")}
        S_ = Sched(nc, sems, same_sync=same_sync)

        def newdsem(name):
            return DSem(st.enter_context(nc.semaphore("d_" + name)))

        def sb(name, shape, dt):
            return nc.alloc_sbuf_tensor("sb_" + name, list(shape), dt)

        xT = sb("xT", [P, KC, TT], F32)
        hT = sb("hT", [P, KC, TT], BF16)
        ZW = TT + 3
        BIGN = max(16 * ZW + 16 * TT, 32 * TT)
        BIGA = sb("BIGA", [P, BIGN], BF16)
        zqk = BIGA[:, 0:16 * ZW].rearrange("p (c t) -> p c t", t=ZW)
        qkT = BIGA[:, 16 * ZW:16 * ZW + 16 * TT].rearrange("p (c t) -> p c t", t=TT)
        hidT = BIGA[:, 0:32 * TT].rearrange("p (c t) -> p c t", t=TT)
        sgm = BIGA[:, 0:8 * TT].rearrange("p (c t) -> p c t", t=TT)
        sgr = BIGA[:, 8 * TT:16 * TT].rearrange("p (c t) -> p c t", t=TT)
        mrgT = BIGA[:, 16 * TT:24 * TT].rearrange("p (c t) -> p c t", t=TT)
        sgT = BIGA[:, 0:8 * TT].rearrange("p (c t) -> p c t", t=TT)
        RETB = sb("RETB", [P, 16 * TT], BF16)
        rpre = RETB[:, 0:4 * TT].rearrange("p (c t) -> p c t", t=TT)
        rqkT = RETB[:, 4 * TT:8 * TT].rearrange("p (c t) -> p c t", t=TT)
        rv = RETB[:, 8 * TT:12 * TT].rearrange("p (c n) -> p c n", n=512)
        Vext = RETB[:, 12 * TT:12 * TT + NSUB * 258].rearrange("p (c n) -> p c n", n=258)
        yrT = RETB[:, :].rearrange("p (c t) -> p c t", t=TT)
        srg = sb("srg", [P, NSUB, 4, 512], BF16)
        Ur = sb("Ur", [P, NSUB, 4, 512], BF16)
        sigo = srg[:].rearrange("p c h n -> p (c h n)")[:, 0:NSUB * 4 * 256].rearrange("p (c h n) -> p c h n", c=NSUB, h=4)
        Um = Ur[:].rearrange("p c h n -> p (c h n)")[:, 0:NSUB * 4 * 258].rearrange("p (c h n) -> p c h n", c=NSUB, h=4)
        STm = sb("STm", [P, NSUB, 4, 6], F32)
        STr = sb("STr", [P, NSUB, 4, 6], F32)
        MVb = sb("MVb", [P, NSUB, 4, 2], F32)
        XS = sb("XS", [P, 6, NSUB, 4], F32)
        ymT = sb("ymT", [P, 8, TT], BF16)
        wsl = [sb(f"wsl{i}", [P, WBLK], BF16) for i in range(NSLOT)]
        stage32 = [sb(f"st32_{i}", [P, 2048], F32) for i in range(NSTAGE)]
        stage16 = [sb(f"st16_{i}", [P, 2048], BF16) for i in range(NSTAGE)]
        Cbf = sb("Cbf", [P, 2, 4, 258], BF16)
        Rbf = sb("Rbf", [P, 2, 4, 512], BF16)
        cosb = sb("cosb", [P, TT], F32)
        sinb = sb("sinb", [P, TT], F32)
        NT32 = 3
        T32 = [sb(f"T32_{i}", [P, TT], F32) for i in range(NT32)]
        rstd = sb("rstd", [P, TT], F32)
        Spp = [sb(f"Spp{i}", [P, 128], BF16) for i in range(2)]
        kp = [sb(f"kp{i}", [P, 256], BF16) for i in range(2)]
        ytmp = [sb(f"ytmp{i}", [P, 512], BF16) for i in range(2)]
        ytm2 = [sb(f"ytm2{i}", [P, 512], BF16) for i in range(2)]
        xtok = [sb(f"xtok{i}", [P, D], F32) for i in range(NXT)]
        ptok = [sb(f"ptok{i}", [P, 256], F32) for i in range(2)]
        pbf = [sb(f"pbf{i}", [P, 256], BF16) for i in range(2)]
        pT = sb("pT", [P, 2, TT], BF16)
        rel = [sb(f"rel{i}", [P, TT], BF16) for i in range(2)]
        gif = sb("gif", [P, NSUB, 8], F32)
        spl = sb("spl", [P, NSUB, 4], F32)
        G1 = sb("G1", [P, NSUB, 16], F32)
        GE = sb("GE", [P, NSUB, 16], F32)
        stats = [sb(f"stats{i}", [P, 6], F32) for i in range(2)]
        mv_ = [sb(f"mv{i}", [P, 2], F32) for i in range(2)]
        sc = [sb(f"sc{i}", [P, 8], F32) for i in range(2)]
        smalls = sb("smalls", [P, NSM], F32)
        consts = sb("consts", [P, NCST], F32)
        identb = sb("identb", [P, 128], BF16)
        onesb = sb("onesb", [P, 128], BF16)
        onerow = sb("onerow", [P, 128], BF16)
        halo = sb("halo", [P, 16, 3], BF16)

        PSD = [nc.alloc_psum_tensor(f"psd{i}", [P, 512], F32) for i in range(2)]
        PSS = nc.alloc_psum_tensor("pss", [P, 512], F32)
        PSO = [nc.alloc_psum_tensor(f"pso{i}", [P, 512], F32) for i in range(2)]
        PSD = PSD + PSO
        PSU = nc.alloc_psum_tensor("psu", [P, 2, 512], F32)
        PST = nc.alloc_psum_tensor("pst", [P, 1024], BF16)

        def bl(name, n):
            return [Buf(f"{name}{i}") for i in range(n)]

        B_xT = bl("xT", KC)
        B_hT = bl("hT", KC)
        B_zqk = bl("zqk", 16)
        B_qkT = bl("qkT", 16)
        B_hid = bl("hid", 32)
        B_sgm = bl("sgm", 8)
        B_sgr = bl("sgr", 8)
        B_mrg = bl("mrg", 8)
        B_sgT = bl("sgT", 8)
        B_rpre = bl("rpre", 4)
        B_rqkT = bl("rqkT", 4)
        B_Vext = bl("Vext", NSUB)
        B_sigo = bl("sigo", NSUB * 4)
        B_rv = bl("rv", NSUB)
        B_srg = bl("srg", NSUB * 4)
        B_Um = bl("Um", NSUB * 4)
        B_Ur = bl("Ur", NSUB * 4)
        B_STm = bl("STm", NSUB * 4)
        B_STr = bl("STr", NSUB * 4)
        B_MVb = Buf("MVb")
        B_XS = Buf("XS")
        B_ymT = bl("ymT", 8)
        B_yrT = bl("yrT", 16)
        B_wsl = bl("wsl", NSLOT)
        B_st32 = bl("st32", NSTAGE)
        B_st16 = bl("st16", NSTAGE)
        B_C32 = bl("C32", 4)
        B_Cbf = bl("Cbf", 4)
        B_R32 = bl("R32", 4)
        B_Rbf = bl("Rbf", 4)
        B_cos = Buf("cos")
        B_sin = Buf("sin")
        B_T32 = bl("T32", NT32)
        B_rstd = Buf("rstd")
        B_Spp = bl("Spp", 2)
        B_kp = bl("kp", 2)
        B_ytmp = bl("ytmp", 2)
        B_ytm2 = bl("ytm2", 2)
        B_xtok = bl("xtok", NXT)
        B_ptok = bl("ptok", 2)
        B_pbf = bl("pbf", 2)
        B_pT = Buf("pT")
        B_rel = bl("rel", 2)
        B_gif = Buf("gif")
        B_spl = Buf("spl")
        B_G1 = Buf("G1")
        B_GE = Buf("GE")
        B_stats = bl("stats", 2)
        B_mv = bl("mv", 2)
        B_sc = bl("sc", 2)
        B_smalls = Buf("smalls")
        B_consts = Buf("consts")
        B_cb = Buf("constb")
        B_halo = Buf("halo")
        B_PSD = [Buf(f"psd{i}", excl=True) for i in range(2)]
        B_PSS = Buf("pss", excl=True)
        B_PSO = [Buf(f"pso{i}", excl=True) for i in range(2)]
        B_PSD = B_PSD + B_PSO
        B_PSU = Buf("psu", excl=True)
        B_PSTy = Buf("psty", excl=True)
        B_PSTk = B_PSTy
        B_wscr = [Buf(f"wscr{i}") for i in range(NL * NBLK)]
        B_xscr = bl("xscr", NT)
        B_out = Buf("out")

        D_wsl = [newdsem(f"wsl{i}") for i in range(NSLOT)]
        D_stin = [newdsem(f"stin{i}") for i in range(NSTAGE)]
        D_stout = [newdsem(f"stout{i}") for i in range(NSTAGE)]
        D_x = newdsem("x")
        D_xs = newdsem("xs")
        D_misc = newdsem("misc")
        D_cos = newdsem("cos")
        D_sin = newdsem("sin")
        D_xtok = [newdsem(f"xtok{i}") for i in range(NXT)]
        D_ptok = [newdsem(f"ptok{i}") for i in range(2)]
        D_sm = newdsem("sm")

        A = S_.add
        rr = {"psd": 0, "pso": 0, "t32": 0, "spp": 0, "kp": 0, "yt": 0, "yt2": 0, "st": 0, "rel": 0,
              "stats": 0, "ew": 0, "xtok": 0, "ptok": 0}

        def nxt(k, n):
            v = rr[k]
            rr[k] = (v + 1) % n
            return v

        def ew_eng():
            return "dve" if nxt("ew", 2) == 0 else "pool"

        A("sp", lambda h: h.dma_start(out=consts[:], in_=consts_d), writes=[B_consts], dsem=D_misc)
        A("dve", lambda h: h.tensor_copy(out=identb[:], in_=consts[:, C_ID:C_ID + 128]), reads=[B_consts], writes=[B_cb])
        A("dve", lambda h: h.memset(onesb[:], 1.0 / D), writes=[B_cb])
        A("dve", lambda h: h.memset(onerow[:], 0.0), writes=[B_cb])
        A("dve", lambda h: h.memset(onerow[0:1, :], 1.0), writes=[B_cb])
        for i_ in range(NSLOT):
            A("dve", lambda h, i_=i_: h.memset(wsl[i_][:, BIASOFF:WBLK], 0.0), writes=[B_wsl[i_]])
        ident32 = consts[:, C_ID:C_ID + 128]
        triu = consts[:, C_TRIU:C_TRIU + 128]
        ones32 = consts[:, C_ONES:C_ONES + 128]
        maskm = consts[:, C_MASKM:C_MASKM + 128]

        def smc(name, j, n=1, smt=None):
            o = SM[name] + j
            return (smalls if smt is None else smt)[:, o:o + n]

        def conv_jobs(l):
            jobs = []
            for bi, b in enumerate(BLOCKS):
                kcb, ncb = b["kcb"], b["ncb"]
                for (src, c0, ncols, dc, gname) in b["pieces"]:
                    per = max(1, 2048 // ncols)
                    for k0 in range(0, kcb, per):
                        k1 = min(kcb, k0 + per)
                        jobs.append(("w", l, bi, src, c0, ncols, dc, gname, k0, k1))
                for (c0, ncols, dc) in b["bias"]:
                    jobs.append(("b", l, bi, c0, ncols, dc))
            return jobs

        def do_conv_job(job, smt, smb):
            s = nxt("st", NSTAGE)
            if job[0] == "w":
                _, l, bi, src, c0, ncols, dc, gname, k0, k1 = job
                b = BLOCKS[bi]
                kcb, ncb = b["kcb"], b["ncb"]
                nk = k1 - k0
                srcap = wsrc[src][l].rearrange("(k p) n -> p k n", p=P)[:, k0:k1, c0:c0 + ncols]
                s32 = stage32[s][:, 0:nk * ncols].rearrange("p (k n) -> p k n", n=ncols)
                s16 = stage16[s][:, 0:nk * ncols].rearrange("p (k n) -> p k n", n=ncols)
                A("sp", lambda h: h.dma_start(out=s32, in_=srcap), writes=[B_st32[s]], dsem=D_stin[s])
                if gname is None:
                    A("pool", lambda h: h.tensor_copy(out=stage16[s][:, 0:nk * ncols], in_=stage32[s][:, 0:nk * ncols]),
                      reads=[B_st32[s]], writes=[B_st16[s]])
                else:
                    for k in range(nk):
                        g = smc(gname, k0 + k, smt=smt)
                        eng = "pool" if k % 2 == 0 else "act"
                        if eng == "pool":
                            A("pool", lambda h, k=k, g=g: h.tensor_scalar(out=s16[:, k, :], in0=s32[:, k, :], scalar1=g,
                                                                          scalar2=None, op0=ALU.mult),
                              reads=[B_st32[s], smb], writes=[B_st16[s]])
                        else:
                            A("act", lambda h, k=k, g=g: h.activation(out=s16[:, k, :], in_=s32[:, k, :], func=AF.Copy, scale=g),
                              reads=[B_st32[s], smb], writes=[B_st16[s]])
                dst = wscr[l * NBLK + bi][:, 0:kcb * ncb].rearrange("p (k n) -> p k n", n=ncb)[:, k0:k1, dc:dc + ncols]
                A("pool", lambda h: h.dma_start(out=dst, in_=s16), reads=[B_st16[s]], writes=[B_wscr[l * NBLK + bi]], dsem=D_stout[s])
            else:
                _, l, bi, c0, ncols, dc = job
                srcap = b_in_d[l:l + 1, c0:c0 + ncols]
                A("sp", lambda h: h.dma_start(out=stage32[s][0:1, 0:ncols], in_=srcap), writes=[B_st32[s]], dsem=D_stin[s])
                A("pool", lambda h: h.tensor_copy(out=stage16[s][0:1, 0:ncols], in_=stage32[s][0:1, 0:ncols]),
                  reads=[B_st32[s]], writes=[B_st16[s]])
                dst = wscr[l * NBLK + bi][0:1, BIASOFF + dc:BIASOFF + dc + ncols]
                A("pool", lambda h: h.dma_start(out=dst, in_=stage16[s][0:1, 0:ncols]), reads=[B_st16[s]],
                  writes=[B_wscr[l * NBLK + bi]], dsem=D_stout[s])

        def rmsnorm_to_hT():
            for kc in range(KC):
                A("act", lambda h, kc=kc: h.activation(out=hT[:, kc, :], in_=xT[:, kc, :], func=AF.Square),
                  reads=[B_xT[kc]], writes=[B_hT[kc]])
            pi = nxt("psd", 4)
            for kc in range(KC):
                A("pe", lambda h, kc=kc: h.matmul(PSD[pi][:, 0:TT], lhsT=onesb[:], rhs=hT[:, kc, :], start=(kc == 0), stop=(kc == KC - 1)),
                  reads=[B_hT[kc], B_cb], writes=[B_PSD[pi]])
            A("act", lambda h: h.activation(out=rstd[:], in_=PSD[pi][:, 0:TT], func=AF.Ln, bias=EPS), reads=[B_PSD[pi]], writes=[B_rstd])
            A("act", lambda h: h.activation(out=rstd[:], in_=rstd[:], func=AF.Exp, scale=-0.5), reads=[B_rstd], writes=[B_rstd])
            for kc in range(KC):
                e = ew_eng()
                A(e, lambda h, kc=kc: h.tensor_tensor(out=hT[:, kc, :], in0=xT[:, kc, :], in1=rstd[:], op=ALU.mult),
                  reads=[B_xT[kc], B_rstd], writes=[B_hT[kc]])

        def fm_group(w, bw, nchunks, col0, evac):
            wv = w[:, 0:KC * 512].rearrange("p (k n) -> p k n", n=512)
            for j in range(nchunks):
                pi = nxt("psd", 4)
                for kc in range(KC):
                    A("pe", lambda h, kc=kc, j=j, pi=pi: h.matmul(PSD[pi][:, 0:TT], lhsT=wv[:, kc, col0 + j * 128:col0 + (j + 1) * 128],
                                                                   rhs=hT[:, kc, :], start=(kc == 0), stop=(kc == KC - 1)),
                      reads=[bw, B_hT[kc]], writes=[B_PSD[pi]])
                evac(j, pi)

        def tm_group(w, bw, ncols, evac, bias=True):
            wv = w[:, 0:KC * ncols].rearrange("p (k n) -> p k n", n=ncols)
            for c in range(NSUB):
                pi = nxt("psd", 4)
                if bias:
                    A("pe", lambda h, pi=pi: h.matmul(PSD[pi][:, 0:ncols], lhsT=onerow[:], rhs=w[:, BIASOFF:BIASOFF + ncols],
                                                     start=True, stop=False), reads=[bw, B_cb], writes=[B_PSD[pi]])
                for kc in range(KC):
                    A("pe", lambda h, kc=kc, c=c, pi=pi: h.matmul(PSD[pi][:, 0:ncols], lhsT=hT[:, kc, c * 128:(c + 1) * 128], rhs=wv[:, kc, :],
                                                                   start=(kc == 0 and not bias), stop=(kc == KC - 1)),
                      reads=[bw, B_hT[kc]], writes=[B_PSD[pi]])
                evac(c, pi)

        def ln_scalars(pso_ap, nfeat, r_ap, Bpso, extra_reads):
            si = nxt("stats", 2)
            A("dve", lambda h: h.bn_stats(out=stats[si][:], in_=pso_ap), reads=[Bpso], writes=[B_stats[si]])
            A("dve", lambda h: h.bn_aggr(out=mv_[si][:], in_=stats[si][:]), reads=[B_stats[si]], writes=[B_mv[si]])
            s_ = sc[si]
            A("dve", lambda h: h.tensor_tensor(out=s_[:, 0:1], in0=r_ap, in1=r_ap, op=ALU.mult), reads=extra_reads, writes=[B_sc[si]])
            A("dve", lambda h: h.tensor_tensor(out=s_[:, 1:2], in0=s_[:, 0:1], in1=mv_[si][:, 1:2], op=ALU.mult),
              reads=[B_sc[si], B_mv[si]], writes=[B_sc[si]])
            A("act", lambda h: h.activation(out=s_[:, 2:3], in_=s_[:, 1:2], func=AF.Ln, bias=EPS), reads=[B_sc[si]], writes=[B_sc[si]])
            A("act", lambda h: h.activation(out=s_[:, 2:3], in_=s_[:, 2:3], func=AF.Exp, scale=-0.5), reads=[B_sc[si]], writes=[B_sc[si]])
            A("dve", lambda h: h.tensor_tensor(out=s_[:, 3:4], in0=s_[:, 2:3], in1=r_ap, op=ALU.mult),
              reads=[B_sc[si]] + list(extra_reads), writes=[B_sc[si]])
            A("dve", lambda h: h.scalar_tensor_tensor(out=s_[:, 4:5], in0=mv_[si][:, 0:1], scalar=-1.0, in1=s_[:, 3:4], op0=ALU.mult, op1=ALU.mult),
              reads=[B_sc[si], B_mv[si]], writes=[B_sc[si]])
            return s_[:, 3:4], s_[:, 4:5], B_sc[si]

        def finalize_heads(is_m):
            U, ST, BU, BST = (Um, STm, B_Um, B_STm) if is_m else (Ur, STr, B_Ur, B_STr)
            nf = 256 if is_m else 512
            for c in range(NSUB):
                for hh in range(4):
                    A("dve", lambda h, c=c, hh=hh: h.bn_aggr(out=MVb[:, c, hh, :], in_=ST[:, c, hh, :]), reads=[BST[c * 4 + hh]], writes=[B_MVb])
            X = [XS[:, i, :, :] for i in range(6)]
            if is_m:
                eT = GE[:, :, 8:12]
                den = Um[:, :, :, 256]
                A("dve", lambda h: h.tensor_tensor(out=X[0], in0=den, in1=eT, op=ALU.mult), reads=BU + [B_GE], writes=[B_XS])
                A("dve", lambda h: h.scalar_tensor_tensor(out=X[1], in0=X[0], scalar=-1.0, in1=X[0], op0=ALU.mult, op1=ALU.max), reads=[B_XS], writes=[B_XS])
                A("dve", lambda h: h.tensor_scalar(out=X[0], in0=X[1], scalar1=1.0, scalar2=None, op0=ALU.max), reads=[B_XS], writes=[B_XS])
                A("dve", lambda h: h.reciprocal(out=X[1], in_=X[0]), reads=[B_XS], writes=[B_XS])
                A("dve", lambda h: h.tensor_tensor(out=X[2], in0=X[1], in1=eT, op=ALU.mult), reads=[B_XS, B_GE], writes=[B_XS])
            else:
                for c in range(NSUB):
                    A("dve", lambda h, c=c: h.tensor_copy(out=XS[:, 2, c, :], in_=consts[:, C_RSC:C_RSC + 4]), reads=[B_consts], writes=[B_XS])
            A("dve", lambda h: h.tensor_tensor(out=X[3], in0=X[2], in1=X[2], op=ALU.mult), reads=[B_XS], writes=[B_XS])
            A("dve", lambda h: h.tensor_tensor(out=X[3], in0=X[3], in1=MVb[:, :, :, 1], op=ALU.mult), reads=[B_XS, B_MVb], writes=[B_XS])
            A("act", lambda h: h.activation(out=X[4], in_=X[3], func=AF.Ln, bias=EPS), reads=[B_XS], writes=[B_XS])
            A("act", lambda h: h.activation(out=X[4], in_=X[4], func=AF.Exp, scale=-0.5), reads=[B_XS], writes=[B_XS])
            A("dve", lambda h: h.tensor_tensor(out=X[4], in0=X[4], in1=X[2], op=ALU.mult), reads=[B_XS], writes=[B_XS])
            A("dve", lambda h: h.scalar_tensor_tensor(out=X[5], in0=MVb[:, :, :, 0], scalar=-1.0, in1=X[4], op0=ALU.mult, op1=ALU.mult),
              reads=[B_XS, B_MVb], writes=[B_XS])
            gate, Bg = (sigo, B_sigo) if is_m else (srg, B_srg)
            nch = nf // 128
            for c in range(NSUB):
                ts = slice(c * 128, (c + 1) * 128)
                for hh in range(4):
                    yi = nxt("yt", 2)
                    A("act", lambda h, c=c, hh=hh, yi=yi: h.activation(out=ytmp[yi][:, 0:nf], in_=U[:, c, hh, 0:nf], func=AF.Identity,
                                                                         bias=XS[:, 5, c, hh:hh + 1], scale=XS[:, 4, c, hh:hh + 1]),
                      reads=[BU[c * 4 + hh], B_XS], writes=[B_ytmp[yi]])
                    y2 = nxt("yt2", 2)
                    A("dve" if (c * 4 + hh) % 3 != 2 else "pool", lambda h, c=c, hh=hh, yi=yi, y2=y2: h.tensor_tensor(out=ytm2[y2][:, 0:nf], in0=ytmp[yi][:, 0:nf], in1=gate[:, c, hh, :], op=ALU.mult),
                      reads=[B_ytmp[yi], Bg[c * 4 + hh]], writes=[B_ytm2[y2]])
                    for j in range(nch):
                        A("pe", lambda h, j=j, y2=y2: h.transpose(out=PST[:, j * 128:(j + 1) * 128], in_=ytm2[y2][:, j * 128:(j + 1) * 128], identity=identb[:]),
                          reads=[B_ytm2[y2], B_cb], writes=[B_PSTy])
                    if is_m:
                        A("dve", lambda h, hh=hh: h.tensor_copy(out=ymT[:, 2 * hh:2 * hh + 2, ts], in_=PST[:, 0:256].rearrange("p (j n) -> p j n", n=128)),
                          reads=[B_PSTy], writes=B_ymT[2 * hh:2 * hh + 2])
                    else:
                        A("dve", lambda h, hh=hh: h.tensor_copy(out=yrT[:, 4 * hh:4 * hh + 4, ts], in_=PST[:, 0:512].rearrange("p (j n) -> p j n", n=128)),
                          reads=[B_PSTy], writes=B_yrT[4 * hh:4 * hh + 4])

        steps = []

        def tile_steps(l, t, first_layer, last_layer):
            t0 = t * TT
            seq_start = (t == 0)
            lb = l * NBLK

            def add_step(bname, fn):
                steps.append((None if bname is None else lb + BIDX[bname], fn))

            def s_load(w, bw):
                if t == 0:
                    A("sp", lambda h: h.dma_start(out=smalls[:], in_=smalls_d[l]), writes=[B_smalls], dsem=D_sm)
                if first_layer:
                    for c in range(NSUB):
                        xi = nxt("xtok", NXT)
                        A("sp", lambda h, c=c, xi=xi: h.dma_start(out=xtok[xi][:], in_=x_d[t0 + c * 128:t0 + (c + 1) * 128, :]),
                          writes=[B_xtok[xi]], dsem=D_xtok[xi])
                        for half in range(2):
                            pi = nxt("psd", 4)
                            for q in range(4):
                                kc = half * 4 + q
                                A("pe", lambda h, q=q, kc=kc, xi=xi, pi=pi: h.transpose(out=PSD[pi][:, q * 128:(q + 1) * 128],
                                                                                        in_=xtok[xi][:, kc * 128:(kc + 1) * 128], identity=ident32),
                                  reads=[B_xtok[xi], B_consts], writes=[B_PSD[pi]])
                            A("act", lambda h, half=half, c=c, pi=pi: h.activation(
                                out=xT[:, half * 4:half * 4 + 4, c * 128:(c + 1) * 128],
                                in_=PSD[pi][:, :].rearrange("p (q n) -> p q n", n=128), func=AF.Copy),
                              reads=[B_PSD[pi]], writes=B_xT[half * 4:half * 4 + 4])
                else:
                    A("sp", lambda h: h.dma_start(out=xT[:].rearrange("p k t -> p (k t)"), in_=xscr[t]), reads=[B_xscr[t]], writes=B_xT, dsem=D_x)
                A("sp", lambda h: h.dma_start(out=cosb[:], in_=cos_d[:, t0:t0 + TT]), writes=[B_cos], dsem=D_cos)
                A("sp", lambda h: h.dma_start(out=sinb[:], in_=sin_d[:, t0:t0 + TT]), writes=[B_sin], dsem=D_sin)
                if seq_start:
                    A("pool", lambda h: h.memset(Cbf[:], 0.0), writes=B_Cbf)
                    A("pool", lambda h: h.memset(Rbf[:], 0.0), writes=B_Rbf)
                    A("pool", lambda h: h.memset(halo[:], 0.0), writes=[B_halo])
                alias_barrier(B_zqk + B_qkT, B_hid + B_sgm + B_sgr + B_mrg + B_sgT)
                alias_barrier(B_Vext + B_rpre + B_rqkT + B_rv, B_yrT)
                A("pool", lambda h: h.memset(Vext[:, :, 256:258], 1.0), writes=B_Vext)
                A("pool", lambda h: h.tensor_copy(out=zqk[:, :, 0:3], in_=halo[:]), reads=[B_halo], writes=B_zqk)
                rmsnorm_to_hT()

            add_step(None, s_load)

            pend_silu = []
            for j4 in range(4):
                def s_qk(w, bw, j4=j4):
                    def evac(j, pi):
                        c = j4 * 4 + j
                        A("act", lambda h: h.activation(out=zqk[:, c, 3:3 + TT], in_=PSD[pi][:, 0:TT], func=AF.Identity, bias=smc("bqk", c)),
                          reads=[B_PSD[pi], B_smalls], writes=[B_zqk[c]])
                        A("pool", lambda h: h.tensor_copy(out=halo[:, c, :], in_=zqk[:, c, TT:TT + 3]), reads=[B_zqk[c]], writes=[B_halo])
                        ti = nxt("t32", NT32)
                        e = "dve"
                        A(e, lambda h: h.tensor_scalar(out=T32[ti][:], in0=zqk[:, c, 0:TT], scalar1=smc("cw", c * 4 + 0), scalar2=smc("cb", c),
                                                       op0=ALU.mult, op1=ALU.add), reads=[B_zqk[c], B_smalls], writes=[B_T32[ti]])
                        for jj in range(1, 4):
                            A(e, lambda h, jj=jj: h.scalar_tensor_tensor(out=T32[ti][:], in0=zqk[:, c, jj:jj + TT], scalar=smc("cw", c * 4 + jj),
                                                                         in1=T32[ti][:], op0=ALU.mult, op1=ALU.add),
                              reads=[B_zqk[c], B_smalls, B_T32[ti]], writes=[B_T32[ti]])
                        pend_silu.append((c, ti))
                        while len(pend_silu) > 2:
                            c_, ti_ = pend_silu.pop(0)
                            A("act", lambda h, c_=c_, ti_=ti_: h.activation(out=qkT[:, c_, :], in_=T32[ti_][:], func=AF.Silu), reads=[B_T32[ti_]], writes=[B_qkT[c_]])
                    fm_group(w, bw, 4, 0, evac)
                    if j4 == 3:
                        while pend_silu:
                            c_, ti_ = pend_silu.pop(0)
                            A("act", lambda h, c_=c_, ti_=ti_: h.activation(out=qkT[:, c_, :], in_=T32[ti_][:], func=AF.Silu), reads=[B_T32[ti_]], writes=[B_qkT[c_]])
                add_step(f"QK{j4}", s_qk)

            def s_gates(w, bw):
                wv = w[:, 0:KC * 8].rearrange("p (k n) -> p k n", n=8)
                for c in range(NSUB):
                    pi = nxt("psd", 4)
                    for kc in range(KC):
                        A("pe", lambda h, kc=kc, c=c, pi=pi: h.matmul(PSD[pi][:, 0:8], lhsT=hT[:, kc, c * 128:(c + 1) * 128], rhs=wv[:, kc, :],
                                                                       start=(kc == 0), stop=(kc == KC - 1)),
                          reads=[bw, B_hT[kc]], writes=[B_PSD[pi]])
                    A("dve", lambda h, c=c, pi=pi: h.tensor_tensor(out=gif[:, c, :], in0=PSD[pi][:, 0:8], in1=smc("bgate", 0, 8), op=ALU.add),
                      reads=[B_PSD[pi], B_smalls], writes=[B_gif])
                A("act", lambda h: h.activation(out=spl[:], in_=gif[:, :, 4:8], func=AF.Exp, scale=-1.0), reads=[B_gif], writes=[B_spl])
                A("act", lambda h: h.activation(out=spl[:], in_=spl[:], func=AF.Ln, bias=1.0), reads=[B_spl], writes=[B_spl])
                for c in range(NSUB):
                    pi = nxt("psd", 4)
                    A("pe", lambda h, c=c, pi=pi: h.matmul(PSD[pi][:, 0:4], lhsT=triu, rhs=spl[:, c, :], start=True, stop=True),
                      reads=[B_spl, B_consts], writes=[B_PSD[pi]])
                    A("pe", lambda h, c=c, pi=pi: h.matmul(PSD[pi][:, 4:8], lhsT=ones32, rhs=spl[:, c, :], start=True, stop=True),
                      reads=[B_spl, B_consts], writes=[B_PSD[pi]])
                    A("dve", lambda h, c=c, pi=pi: h.tensor_tensor(out=G1[:, c, 0:4], in0=PSD[pi][:, 0:4], in1=gif[:, c, 0:4], op=ALU.add),
                      reads=[B_PSD[pi], B_gif], writes=[B_G1])
                    A("dve", lambda h, c=c, pi=pi: h.scalar_tensor_tensor(out=G1[:, c, 4:8], in0=G1[:, c, 0:4], scalar=-LN16, in1=PSD[pi][:, 4:8],
                                                                          op0=ALU.add, op1=ALU.subtract),
                      reads=[B_PSD[pi], B_G1], writes=[B_G1])
                    A("dve", lambda h, c=c, pi=pi: h.tensor_scalar(out=G1[:, c, 8:16], in0=PSD[pi][:, 0:8], scalar1=-1.0, scalar2=None, op0=ALU.mult),
                      reads=[B_PSD[pi]], writes=[B_G1])
                A("act", lambda h: h.activation(out=GE[:], in_=G1[:], func=AF.Exp), reads=[B_G1], writes=[B_GE])

            add_step("G", s_gates)

            for hd in range(4):
                def s_m(w, bw, hd=hd):
                    if hd == 0:
                        alias_barrier(B_Um + B_sigo, B_Ur + B_srg)
                    def evac(c, pi):
                        A("dve", lambda h: h.tensor_copy(out=Vext[:, c, 0:256], in_=PSD[pi][:, 0:256]), reads=[B_PSD[pi]], writes=[B_Vext[c]])
                        if dbg in (2, 3):
                            A("act", lambda h: h.activation(out=sigo[:, c, hd, :], in_=PSD[pi][:, 256:512], func=AF.Copy), reads=[B_PSD[pi]], writes=[B_sigo[c * 4 + hd]])
                        else:
                            A("act", lambda h: h.activation(out=sigo[:, c, hd, :], in_=PSD[pi][:, 256:512], func=AF.Sigmoid), reads=[B_PSD[pi]], writes=[B_sigo[c * 4 + hd]])
                    tm_group(w, bw, 512, evac, bias=(dbg != 3))
                    if dbg in (1, 2, 3):
                        return
                    for c in range(NSUB):
                        ts = slice(c * 128, (c + 1) * 128)
                        qc = [2 * hd, 2 * hd + 1]
                        kcx = [8 + 2 * hd, 8 + 2 * hd + 1]
                        for j in range(2):
                            A("pe", lambda h, j=j: h.matmul(PSS[:, 0:128], lhsT=qkT[:, kcx[j], ts], rhs=qkT[:, qc[j], ts], start=(j == 0), stop=(j == 1)),
                              reads=[B_qkT[kcx[j]], B_qkT[qc[j]]], writes=[B_PSS])
                        for j in range(2):
                            A("pe", lambda h, j=j: h.transpose(out=PST[:, 512 + j * 128:512 + (j + 1) * 128], in_=qkT[:, kcx[j], ts], identity=identb[:]),
                              reads=[B_qkT[kcx[j]], B_cb], writes=[B_PSTk])
                        ki = nxt("kp", 2)
                        A("act", lambda h: h.activation(out=kp[ki][:], in_=PST[:, 512:768], func=AF.Copy, scale=GE[:, c, 4 + hd:5 + hd]),
                          reads=[B_PSTk, B_GE], writes=[B_kp[ki]])
                        si = nxt("spp", 2)
                        A("dve", lambda h: h.scalar_tensor_tensor(out=Spp[si][:], in0=PSS[:, 0:128], scalar=GE[:, c, hd:hd + 1], in1=maskm,
                                                                  op0=ALU.mult, op1=ALU.mult), reads=[B_PSS, B_GE, B_consts], writes=[B_Spp[si]])
                        oi = nxt("pso", 2)
                        A("pe", lambda h: h.matmul(PSO[oi][:, 0:257], lhsT=Spp[si][:], rhs=Vext[:, c, 0:257], start=True, stop=False),
                          reads=[B_Spp[si], B_Vext[c]], writes=[B_PSO[oi]])
                        for j in range(2):
                            A("pe", lambda h, j=j: h.matmul(PSO[oi][:, 0:257], lhsT=qkT[:, qc[j], ts], rhs=Cbf[:, j, hd, 0:257], start=False, stop=(j == 1)),
                              reads=[B_qkT[qc[j]], B_Cbf[hd]], writes=[B_PSO[oi]])
                        A("act", lambda h: h.activation(out=Um[:, c, hd, 0:257], in_=PSO[oi][:, 0:257], func=AF.Copy),
                          reads=[B_PSO[oi]], writes=[B_Um[c * 4 + hd]])
                        A("dve", lambda h: h.bn_stats(out=STm[:, c, hd, :], in_=PSO[oi][:, 0:256]), reads=[B_PSO[oi]], writes=[B_STm[c * 4 + hd]])
                        for j in range(2):
                            A("pe", lambda h, j=j: h.matmul(PSU[:, j, 0:257], lhsT=kp[ki][:, j * 128:(j + 1) * 128], rhs=Vext[:, c, 0:257], start=True, stop=True),
                              reads=[B_kp[ki], B_Vext[c]], writes=[B_PSU])
                        A("dve", lambda h: h.scalar_tensor_tensor(out=Cbf[:, :, hd, 0:257], in0=Cbf[:, :, hd, 0:257], scalar=GE[:, c, 12 + hd:13 + hd],
                                                                  in1=PSU[:, :, 0:257], op0=ALU.mult, op1=ALU.add),
                          reads=[B_Cbf[hd], B_GE, B_PSU], writes=[B_Cbf[hd]])
                add_step(f"M{hd}", s_m)

            for hd in range(4):
                def s_rqk(w, bw, hd=hd):
                    def evac(j, pi):
                        A("act", lambda h: h.activation(out=rpre[:, j, :], in_=PSD[pi][:, 0:TT], func=AF.Identity, bias=smc("brqk", hd * 4 + j)),
                          reads=[B_PSD[pi], B_smalls], writes=[B_rpre[j]])
                    fm_group(w, bw, 4, 0, evac)
                    for b0 in (0, 2):
                        x1, x2 = rpre[:, b0, :], rpre[:, b0 + 1, :]
                        ta, tb = nxt("t32", NT32), nxt("t32", NT32)
                        e = ew_eng()
                        A(e, lambda h: h.tensor_tensor(out=T32[ta][:], in0=x1, in1=cosb[:], op=ALU.mult), reads=[B_rpre[b0], B_cos], writes=[B_T32[ta]])
                        A(e, lambda h: h.tensor_tensor(out=T32[tb][:], in0=x2, in1=sinb[:], op=ALU.mult), reads=[B_rpre[b0 + 1], B_sin], writes=[B_T32[tb]])
                        A(e, lambda h: h.tensor_tensor(out=rqkT[:, b0, :], in0=T32[ta][:], in1=T32[tb][:], op=ALU.subtract),
                          reads=[B_T32[ta], B_T32[tb]], writes=[B_rqkT[b0]])
                        tc_, td = nxt("t32", NT32), nxt("t32", NT32)
                        e = ew_eng()
                        A(e, lambda h: h.tensor_tensor(out=T32[tc_][:], in0=x1, in1=sinb[:], op=ALU.mult), reads=[B_rpre[b0], B_sin], writes=[B_T32[tc_]])
                        A(e, lambda h: h.tensor_tensor(out=T32[td][:], in0=x2, in1=cosb[:], op=ALU.mult), reads=[B_rpre[b0 + 1], B_cos], writes=[B_T32[td]])
                        A(e, lambda h: h.tensor_tensor(out=rqkT[:, b0 + 1, :], in0=T32[tc_][:], in1=T32[td][:], op=ALU.add),
                          reads=[B_T32[tc_], B_T32[td]], writes=[B_rqkT[b0 + 1]])
                add_step(f"RQK{hd}", s_rqk)

                def s_rv(w, bw, hd=hd):
                    def evac(c, pi):
                        A("dve", lambda h: h.tensor_copy(out=rv[:, c, :], in_=PSD[pi][:, 0:512]), reads=[B_PSD[pi]], writes=[B_rv[c]])
                    tm_group(w, bw, 512, evac)
                    if hd == 0:
                        finalize_heads(True)
                        alias_barrier(B_Ur + B_srg, B_Um + B_sigo)
                add_step(f"RV{hd}", s_rv)

                def s_rg(w, bw, hd=hd):
                    def evac(c, pi):
                        A("act", lambda h: h.activation(out=srg[:, c, hd, :], in_=PSD[pi][:, 0:512], func=AF.Silu), reads=[B_PSD[pi]], writes=[B_srg[c * 4 + hd]])
                    tm_group(w, bw, 512, evac)
                    maskr = consts[:, C_MASKR + hd * 128:C_MASKR + (hd + 1) * 128]
                    rsc = consts[:, C_RSC + hd:C_RSC + hd + 1]
                    kdec = consts[:, C_KDEC + hd:C_KDEC + hd + 1]
                    cdec = float((1.0 - 2.0 ** (-5.0 - hd)) ** 128)
                    for c in range(NSUB):
                        ts = slice(c * 128, (c + 1) * 128)
                        for j in range(2):
                            A("pe", lambda h, j=j: h.matmul(PSS[:, 0:128], lhsT=rqkT[:, 2 + j, ts], rhs=rqkT[:, j, ts], start=(j == 0), stop=(j == 1)),
                              reads=[B_rqkT[2 + j], B_rqkT[j]], writes=[B_PSS])
                        for j in range(2):
                            A("pe", lambda h, j=j: h.transpose(out=PST[:, 512 + j * 128:512 + (j + 1) * 128], in_=rqkT[:, 2 + j, ts], identity=identb[:]),
                              reads=[B_rqkT[2 + j], B_cb], writes=[B_PSTk])
                        ki = nxt("kp", 2)
                        A("act", lambda h: h.activation(out=kp[ki][:], in_=PST[:, 512:768], func=AF.Copy, scale=kdec),
                          reads=[B_PSTk, B_consts], writes=[B_kp[ki]])
                        si = nxt("spp", 2)
                        A("dve", lambda h: h.tensor_tensor(out=Spp[si][:], in0=PSS[:, 0:128], in1=maskr, op=ALU.mult),
                          reads=[B_PSS, B_consts], writes=[B_Spp[si]])
                        oi = nxt("pso", 2)
                        A("pe", lambda h: h.matmul(PSO[oi][:, 0:512], lhsT=Spp[si][:], rhs=rv[:, c, :], start=True, stop=False),
                          reads=[B_Spp[si], B_rv[c]], writes=[B_PSO[oi]])
                        for j in range(2):
                            A("pe", lambda h, j=j: h.matmul(PSO[oi][:, 0:512], lhsT=rqkT[:, j, ts], rhs=Rbf[:, j, hd, :], start=False, stop=(j == 1)),
                              reads=[B_rqkT[j], B_Rbf[hd]], writes=[B_PSO[oi]])
                        A("act", lambda h: h.activation(out=Ur[:, c, hd, :], in_=PSO[oi][:, 0:512], func=AF.Copy),
                          reads=[B_PSO[oi]], writes=[B_Ur[c * 4 + hd]])
                        A("dve", lambda h: h.bn_stats(out=STr[:, c, hd, :], in_=PSO[oi][:, 0:512]), reads=[B_PSO[oi]], writes=[B_STr[c * 4 + hd]])
                        for j in range(2):
                            A("pe", lambda h, j=j: h.matmul(PSU[:, j, :], lhsT=kp[ki][:, j * 128:(j + 1) * 128], rhs=rv[:, c, :], start=True, stop=True),
                              reads=[B_kp[ki], B_rv[c]], writes=[B_PSU])
                        A("dve", lambda h: h.scalar_tensor_tensor(out=Rbf[:, :, hd, :], in0=Rbf[:, :, hd, :], scalar=cdec, in1=PSU[:, :, :],
                                                                  op0=ALU.mult, op1=ALU.add), reads=[B_Rbf[hd], B_PSU], writes=[B_Rbf[hd]])
                add_step(f"RG{hd}", s_rg)

            for j2 in range(2):
                def s_gm(w, bw, j2=j2):
                    if j2 == 0:
                        alias_barrier(B_sgm + B_sgr + B_mrg, B_zqk + B_qkT)

                    def evac(j, pi):
                        c = j2 * 4 + j
                        A("act", lambda h: h.activation(out=sgm[:, c, :], in_=PSD[pi][:, 0:TT], func=AF.Sigmoid, bias=smc("bg", c)),
                          reads=[B_PSD[pi], B_smalls], writes=[B_sgm[c]])
                    fm_group(w, bw, 4, 0, evac)
                add_step(f"GM{j2}", s_gm)
            for j2 in range(2):
                def s_gr(w, bw, j2=j2):
                    def evac(j, pi):
                        c = j2 * 4 + j
                        A("act", lambda h: h.activation(out=sgr[:, c, :], in_=PSD[pi][:, 0:TT], func=AF.Sigmoid, bias=smc("bg", 8 + c)),
                          reads=[B_PSD[pi], B_smalls], writes=[B_sgr[c]])
                    fm_group(w, bw, 4, 0, evac)
                    if j2 == 1:
                        alias_barrier(B_yrT, B_Vext + B_rpre + B_rqkT + B_rv)
                        finalize_heads(False)
                add_step(f"GR{j2}", s_gr)

            for j2 in range(2):
                def s_bm(w, bw, j2=j2):
                    wv = w[:, 0:8 * 512].rearrange("p (k n) -> p k n", n=512)
                    for j in range(4):
                        n = j2 * 4 + j
                        pi = nxt("psd", 4)
                        for kc in range(8):
                            A("pe", lambda h, kc=kc, j=j, pi=pi: h.matmul(PSD[pi][:, 0:TT], lhsT=wv[:, kc, j * 128:(j + 1) * 128], rhs=ymT[:, kc, :],
                                                                           start=(kc == 0), stop=(kc == 7)), reads=[bw, B_ymT[kc]], writes=[B_PSD[pi]])
                        A("dve", lambda h, n=n, pi=pi: h.tensor_tensor(out=mrgT[:, n, :], in0=PSD[pi][:, 0:TT], in1=sgm[:, n, :], op=ALU.mult),
                          reads=[B_PSD[pi], B_sgm[n]], writes=[B_mrg[n]])
                add_step(f"BM{j2}", s_bm)
            for j4 in range(4):
                def s_br(w, bw, j4=j4):
                    wv = w[:, 0:16 * 256].rearrange("p (k n) -> p k n", n=256)
                    for j in range(2):
                        n = j4 * 2 + j
                        pi = nxt("psd", 4)
                        for kc in range(16):
                            A("pe", lambda h, kc=kc, j=j, pi=pi: h.matmul(PSD[pi][:, 0:TT], lhsT=wv[:, kc, j * 128:(j + 1) * 128], rhs=yrT[:, kc, :],
                                                                           start=(kc == 0), stop=(kc == 15)), reads=[bw, B_yrT[kc]], writes=[B_PSD[pi]])
                        ti = nxt("t32", NT32)
                        A("dve", lambda h, n=n, pi=pi, ti=ti: h.tensor_tensor(out=T32[ti][:], in0=PSD[pi][:, 0:TT], in1=sgr[:, n, :], op=ALU.mult),
                          reads=[B_PSD[pi], B_sgr[n]], writes=[B_T32[ti]])
                        A("pool", lambda h, n=n, ti=ti: h.tensor_tensor(out=mrgT[:, n, :], in0=T32[ti][:], in1=mrgT[:, n, :], op=ALU.add),
                          reads=[B_T32[ti], B_mrg[n]], writes=[B_mrg[n]])
                add_step(f"BR{j4}", s_br)
            for j2 in range(2):
                def s_o(w, bw, j2=j2):
                    wv = w[:, 0:8 * 512].rearrange("p (k n) -> p k n", n=512)
                    for j in range(4):
                        n = j2 * 4 + j
                        pi = nxt("psd", 4)
                        for kc in range(8):
                            A("pe", lambda h, kc=kc, j=j, pi=pi: h.matmul(PSD[pi][:, 0:TT], lhsT=wv[:, kc, j * 128:(j + 1) * 128], rhs=mrgT[:, kc, :],
                                                                           start=(kc == 0), stop=(kc == 7)), reads=[bw, B_mrg[kc]], writes=[B_PSD[pi]])
                        A("dve", lambda h, n=n, pi=pi: h.tensor_tensor(out=xT[:, n, :], in0=PSD[pi][:, 0:TT], in1=xT[:, n, :], op=ALU.add),
                          reads=[B_PSD[pi], B_xT[n]], writes=[B_xT[n]])
                add_step(f"O{j2}", s_o)

            for j8 in range(8):
                def s_f1(w, bw, j8=j8):
                    if j8 == 0:
                        rmsnorm_to_hT()
                        alias_barrier(B_hid, B_sgm + B_sgr + B_mrg + B_zqk + B_qkT)

                    def evac(j, pi):
                        c = j8 * 4 + j
                        ri = nxt("rel", 2)
                        A("act", lambda h: h.activation(out=rel[ri][:], in_=PSD[pi][:, 0:TT], func=AF.Relu, bias=smc("bf1", c)),
                          reads=[B_PSD[pi], B_smalls], writes=[B_rel[ri]])
                        A("dve", lambda h: h.scalar_tensor_tensor(out=hidT[:, c, :], in0=PSD[pi][:, 0:TT], scalar=smc("bf1", c), in1=rel[ri][:],
                                                                  op0=ALU.add, op1=ALU.mult), reads=[B_PSD[pi], B_smalls, B_rel[ri]], writes=[B_hid[c]])
                    fm_group(w, bw, 4, 0, evac)
                add_step(f"F1{j8}", s_f1)
            for n in range(8):
                def s_f2(w, bw, n=n):
                    wv = w[:, 0:32 * 128].rearrange("p (k n) -> p k n", n=128)
                    pi = nxt("psd", 4)
                    for kc in range(32):
                        A("pe", lambda h, kc=kc, pi=pi: h.matmul(PSD[pi][:, 0:TT], lhsT=wv[:, kc, :], rhs=hidT[:, kc, :], start=(kc == 0), stop=(kc == 31)),
                          reads=[bw, B_hid[kc]], writes=[B_PSD[pi]])
                    A("dve", lambda h, pi=pi: h.scalar_tensor_tensor(out=xT[:, n, :], in0=PSD[pi][:, 0:TT], scalar=smc("bf2", n), in1=xT[:, n, :],
                                                                     op0=ALU.add, op1=ALU.add), reads=[B_PSD[pi], B_smalls, B_xT[n]], writes=[B_xT[n]])
                add_step(f"F2{n}", s_f2)

            for j2 in range(2):
                def s_pg(w, bw, j2=j2):
                    if j2 == 0:
                        rmsnorm_to_hT()
                        alias_barrier(B_sgT, B_hid)
                        for c in range(NSUB):
                            pi_ = nxt("ptok", 2)
                            A("sp", lambda h, c=c, pi_=pi_: h.dma_start(out=ptok[pi_][:], in_=p_d[l, t0 + c * 128:t0 + (c + 1) * 128, :]),
                              writes=[B_ptok[pi_]], dsem=D_ptok[pi_])
                            A("pool", lambda h, pi_=pi_: h.tensor_copy(out=pbf[pi_][:], in_=ptok[pi_][:]), reads=[B_ptok[pi_]], writes=[B_pbf[pi_]])
                            for j in range(2):
                                A("pe", lambda h, j=j, pi_=pi_: h.transpose(out=PST[:, j * 128:(j + 1) * 128], in_=pbf[pi_][:, j * 128:(j + 1) * 128], identity=identb[:]),
                                  reads=[B_pbf[pi_], B_cb], writes=[B_PSTy])
                            A("dve", lambda h, c=c: h.tensor_copy(out=pT[:, :, c * 128:(c + 1) * 128], in_=PST[:, 0:256].rearrange("p (j n) -> p j n", n=128)),
                              reads=[B_PSTy], writes=[B_pT])

                    def evac(j, pi):
                        c = j2 * 4 + j
                        A("act", lambda h: h.activation(out=sgT[:, c, :], in_=PSD[pi][:, 0:TT], func=AF.Sigmoid), reads=[B_PSD[pi]], writes=[B_sgT[c]])
                    fm_group(w, bw, 4, 0, evac)
                add_step(f"PG{j2}", s_pg)

            def s_pe(w, bw):
                wv = w[:, 0:2 * 1024].rearrange("p (k n) -> p k n", n=1024)
                for n in range(8):
                    pi = nxt("psd", 4)
                    for kc in range(2):
                        A("pe", lambda h, kc=kc, n=n, pi=pi: h.matmul(PSD[pi][:, 0:TT], lhsT=wv[:, kc, n * 128:(n + 1) * 128], rhs=pT[:, kc, :],
                                                                       start=(kc == 0), stop=(kc == 1)), reads=[bw, B_pT], writes=[B_PSD[pi]])
                    ti = nxt("t32", NT32)
                    A("dve", lambda h, n=n, pi=pi, ti=ti: h.tensor_tensor(out=T32[ti][:], in0=PSD[pi][:, 0:TT], in1=sgT[:, n, :], op=ALU.mult),
                      reads=[B_PSD[pi], B_sgT[n]], writes=[B_T32[ti]])
                    A("pool", lambda h, n=n, ti=ti: h.tensor_tensor(out=xT[:, n, :], in0=T32[ti][:], in1=xT[:, n, :], op=ALU.add),
                      reads=[B_T32[ti], B_xT[n]], writes=[B_xT[n]])
            add_step("PE", s_pe)

            def s_store(w, bw):
                if not last_layer:
                    A("sp", lambda h: h.dma_start(out=xscr[t], in_=xT[:].rearrange("p k t -> p (k t)")), reads=B_xT, writes=[B_xscr[t]], dsem=D_xs)
                else:
                    for kc in range(KC):
                        A("act", lambda h, kc=kc: h.activation(out=hT[:, kc, :], in_=xT[:, kc, :], func=AF.Square), reads=[B_xT[kc]], writes=[B_hT[kc]])
                    pi = nxt("psd", 4)
                    for kc in range(KC):
                        A("pe", lambda h, kc=kc: h.matmul(PSD[pi][:, 0:TT], lhsT=onesb[:], rhs=hT[:, kc, :], start=(kc == 0), stop=(kc == KC - 1)),
                          reads=[B_hT[kc], B_cb], writes=[B_PSD[pi]])
                    A("act", lambda h: h.activation(out=rstd[:], in_=PSD[pi][:, 0:TT], func=AF.Ln, bias=EPS), reads=[B_PSD[pi]], writes=[B_rstd])
                    A("act", lambda h: h.activation(out=rstd[:], in_=rstd[:], func=AF.Exp, scale=-0.5), reads=[B_rstd], writes=[B_rstd])
                    for kc in range(KC):
                        A("dve", lambda h, kc=kc: h.scalar_tensor_tensor(out=xT[:, kc, :], in0=xT[:, kc, :], scalar=smc("fg", kc), in1=rstd[:],
                                                                         op0=ALU.mult, op1=ALU.mult), reads=[B_xT[kc], B_rstd, B_smalls], writes=[B_xT[kc]])
                    for c in range(NSUB):
                        xi = nxt("xtok", NXT)
                        for half in range(2):
                            pi = nxt("psd", 4)
                            for q in range(4):
                                kc = half * 4 + q
                                A("pe", lambda h, q=q, kc=kc, c=c, pi=pi: h.transpose(out=PSD[pi][:, q * 128:(q + 1) * 128],
                                                                                      in_=xT[:, kc, c * 128:(c + 1) * 128], identity=ident32),
                                  reads=[B_xT[kc], B_consts], writes=[B_PSD[pi]])
                            A("act", lambda h, half=half, xi=xi, pi=pi: h.activation(out=xtok[xi][:, half * 512:(half + 1) * 512], in_=PSD[pi][:, :], func=AF.Copy),
                              reads=[B_PSD[pi]], writes=[B_xtok[xi]])
                        A("sp", lambda h, c=c, xi=xi: h.dma_start(out=out_d[t0 + c * 128:t0 + (c + 1) * 128, :], in_=xtok[xi][:]),
                          reads=[B_xtok[xi]], writes=[B_out], dsem=D_xtok[xi])
            add_step(None, s_store)

        A("sp", lambda h: h.dma_start(out=smalls[:], in_=smalls_d[0]), writes=[B_smalls], dsem=D_sm)
        for job in ([] if skip_conv else conv_jobs(0)):
            do_conv_job(job, smalls, B_smalls)

        for l in range(NL):
            nxt_jobs = conv_jobs(l + 1) if l + 1 < NL else []
            per_tile = (len(nxt_jobs) + NT - 1) // NT if nxt_jobs else 0
            for t in range(NT):
                i0 = len(steps)
                tile_steps(l, t, l == 0, l == NL - 1)
                if nxt_jobs:
                    jl = nxt_jobs[t * per_tile:(t + 1) * per_tile]
                    tile = steps[i0:]
                    del steps[i0:]
                    every = max(1, (len(tile) - 8) // max(1, len(jl)))
                    ji = 0
                    for si_, st_ in enumerate(tile):
                        steps.append(st_)
                        if si_ >= 6 and (si_ - 6) % every == 0 and ji < len(jl):
                            steps.append(("conv", [jl[ji]], l + 1))
                            ji += 1
                    if ji < len(jl):
                        steps.append(("conv", jl[ji:], l + 1))

        wsteps = [i for i, s_ in enumerate(steps) if s_[0] is not None and s_[0] != "conv"]
        wpos = {i: k for k, i in enumerate(wsteps)}
        issued = [0]

        def issue_loads(upto):
            while issued[0] < len(wsteps) and issued[0] <= upto:
                k = issued[0]
                gi = steps[wsteps[k]][0]
                b = BLOCKS[gi % NBLK]
                width = b["kcb"] * b["ncb"]
                s = k % NSLOT
                A("sp", lambda h, gi=gi, s=s, width=width: h.dma_start(out=wsl[s][:, 0:width], in_=wscr[gi][:, 0:width]),
                  reads=[B_wscr[gi]], writes=[B_wsl[s]], dsem=D_wsl[s])
                if b["bias"]:
                    nb_ = b["ncb"]
                    A("sp", lambda h, gi=gi, s=s, nb_=nb_: h.dma_start(out=wsl[s][0:1, BIASOFF:BIASOFF + nb_], in_=wscr[gi][0:1, BIASOFF:BIASOFF + nb_]),
                      reads=[B_wscr[gi]], writes=[B_wsl[s]], dsem=D_wsl[s])
                issued[0] += 1

        smalls2 = sb("smalls2", [P, NSM], F32)
        B_smalls2 = Buf("smalls2")
        D_sm2 = newdsem("sm2")
        conv_layer_loaded = [-1]
        if max_steps is not None:
            steps = steps[:max_steps]
            wsteps = [i for i, s_ in enumerate(steps) if s_[0] is not None and s_[0] != "conv"]
            wpos = {i: k for k, i in enumerate(wsteps)}
        for i, s_ in enumerate(steps):
            if s_[0] == "conv":
                _, jl, ln = s_
                if conv_layer_loaded[0] != ln:
                    A("sp", lambda h, ln=ln: h.dma_start(out=smalls2[:], in_=smalls_d[ln]), writes=[B_smalls2], dsem=D_sm2)
                    conv_layer_loaded[0] = ln
                for job in jl:
                    do_conv_job(job, smalls2, B_smalls2)
                continue
            if s_[0] is None:
                s_[1](None, None)
            else:
                k = wpos[i]
                issue_loads(k + NSLOT - 1)
                slot = k % NSLOT
                s_[1](wsl[slot], B_wsl[slot])

        fin = [(d.sem, d.count) for d in D_xtok + [D_xs] if d.count > 0]
        S_.wait_only("sp", fin)
        S_.emit()
    return nc


def _smalls(inp, NLW):
    sm = np.zeros((NLW, P, NSM), np.float32)

    def colmajor(v):
        return np.ascontiguousarray(v.reshape(-1, P).T)

    for l in range(NLW):
        b = inp["b_in"][l]
        sm[l, :, SM["bqk"]:SM["bqk"] + 16] = colmajor(b[0:2048])
        for h in range(4):
            sm[l, :, SM["brqk"] + h * 4:SM["brqk"] + h * 4 + 2] = colmajor(b[RQ + h * 256:RQ + (h + 1) * 256])
            sm[l, :, SM["brqk"] + h * 4 + 2:SM["brqk"] + h * 4 + 4] = colmajor(b[RK + h * 256:RK + (h + 1) * 256])
        sm[l, :, SM["bg"]:SM["bg"] + 8] = colmajor(b[GM:GM + 1024])
        sm[l, :, SM["bg"] + 8:SM["bg"] + 16] = colmajor(b[GR:GR + 1024])
        cw = inp["conv_w"][l]
        for c in range(16):
            sm[l, :, SM["cw"] + c * 4:SM["cw"] + c * 4 + 4] = cw[:, c * P:(c + 1) * P].T
        sm[l, :, SM["cb"]:SM["cb"] + 16] = colmajor(inp["conv_b"][l])
        sm[l, :, SM["bf1"]:SM["bf1"] + 32] = colmajor(inp["b_ff1"][l])
        sm[l, :, SM["bf2"]:SM["bf2"] + 8] = colmajor(inp["b_ff2"][l])
        sm[l, :, SM["g1"]:SM["g1"] + 8] = colmajor(inp["norm1_g"][l])
        sm[l, :, SM["gmn"]:SM["gmn"] + 8] = colmajor(inp["m_norm_g"][l])
        sm[l, :, SM["grn"]:SM["grn"] + 16] = colmajor(inp["r_norm_g"][l])
        sm[l, :, SM["g2"]:SM["g2"] + 8] = colmajor(inp["norm2_g"][l])
        sm[l, :, SM["g3"]:SM["g3"] + 8] = colmajor(inp["norm3_g"][l])
        sm[l, :, SM["bgate"]:SM["bgate"] + 8] = b[MI:MI + 8][None, :]
        sm[l, :, SM["fg"]:SM["fg"] + 8] = colmajor(inp["final_g"])
    return sm


def _consts():
    c = np.zeros((P, NCST), np.float64)
    idx = np.arange(P)
    c[:, C_ID:C_ID + P] = np.eye(P)
    tri = (idx[:, None] <= idx[None, :]).astype(np.float64)
    c[:, C_TRIU:C_TRIU + P] = tri
    c[:, C_ONES:C_ONES + P] = 1.0
    c[:, C_MASKM:C_MASKM + P] = tri / 16.0
    for h in range(4):
        g = 1.0 - 2.0 ** (-5.0 - h)
        c[:, C_MASKR + h * P:C_MASKR + (h + 1) * P] = tri * (g ** (-(idx[:, None] + 1.0))) / 16.0
        c[:, C_RSC + h] = g ** (idx + 1.0)
        c[:, C_KDEC + h] = g ** (127.0 - idx) / 16.0
    return c.astype(np.float32)


def _rope_tables(S):
    pos = np.arange(S, dtype=np.float32)
    inv_freq = (10000.0 ** (-np.arange(0, 256, 2, dtype=np.float32) / np.float32(256))).astype(np.float32)
    ang = (pos[None, :] * inv_freq[:, None]).astype(np.float32)
    return np.cos(ang).astype(np.float32), np.sin(ang).astype(np.float32)


_WNAMES = ("w_in", "w_bm", "w_br", "w_out", "w_ff1", "w_ff2", "w_pe_gate", "w_pe")


def run_model(inp, NL, TT=256, n_cores=8, **bkw):
    x = np.asarray(inp["x"], np.float32)
    p = np.asarray(inp["p"], np.float32)
    B, S, _ = x.shape
    NLW = inp["w_in"].shape[0]
    nc = build(NL, S, TT=TT, NLW=NLW, **bkw)
    sm = _smalls(inp, NLW)
    cst = _consts()
    cos_t, sin_t = _rope_tables(S)
    shared = {k: np.ascontiguousarray(np.asarray(inp[k], np.float32)) for k in _WNAMES}
    shared["b_in"] = np.ascontiguousarray(np.asarray(inp["b_in"], np.float32))
    shared.update(smalls=sm, consts=cst, cos_t=cos_t, sin_t=sin_t)
    in_maps = []
    for c in range(n_cores):
        b = c % B
        m = dict(shared)
        m["x"] = np.ascontiguousarray(x[b])
        m["p"] = np.ascontiguousarray(p[:, b])
        in_maps.append(m)
    res = run_bass_kernel_spmd(nc, in_maps, core_ids=list(range(n_cores)))
    out = np.stack([res.results[b]["out"] for b in range(B)], axis=0)
    return out.astype(np.float32)


def kernel(**inputs):
    return run_model(inputs, NL=4, TT=512, n_cores=4)
```

```python
import contextlib
import math
import numpy as np
import concourse.bass as bass
import concourse.mybir as mybir
from concourse.bass_utils import run_bass_kernel_spmd

F32 = mybir.dt.float32
BF16 = mybir.dt.bfloat16
AF = mybir.ActivationFunctionType
ALU = mybir.AluOpType
AX = mybir.AxisListType

P = 128
D = 1024
KC = 8
EPS = 1e-6
N_IN = 12296
MQ, MK, MV, MO, MI, MF, RQ, RK, RV, RG, GM, GR = 0, 1024, 2048, 3072, 4096, 4100, 4104, 5128, 6152, 8200, 10248, 11272
DFF = 4096
WBLK = 4608
BIASOFF = 4096
LN16 = math.log(16.0)

SM = {}
_o = 0
for _n, _w in (("bqk", 16), ("brqk", 16), ("bg", 16), ("cw", 64), ("cb", 16), ("bf1", 32), ("bf2", 8),
               ("g1", 8), ("gmn", 8), ("grn", 16), ("g2", 8), ("g3", 8), ("bgate", 8), ("fg", 8)):
    SM[_n] = _o
    _o += _w
NSM = _o
C_ID, C_TRIU, C_ONES, C_MASKM, C_MASKR, C_RSC, C_KDEC = 0, 128, 256, 384, 512, 1024, 1028
NCST = 1032


class Buf:
    __slots__ = ("name", "w", "r", "excl")

    def __init__(self, name, excl=False):
        self.name = name
        self.w = None
        self.r = {}
        self.excl = excl


class DSem:
    def __init__(self, sem):
        self.sem = sem
        self.count = 0


class _Eng:
    def __init__(self, name, sem, same_sync):
        self.name = name
        self.sem = sem
        self.count = 0
        self.ops = []
        self.waited = {}
        self.same_sync = same_sync


class _Rec:
    def __init__(self):
        self.call = None

    def __getattr__(self, name):
        def f(*a, **k):
            self.call = (name, a, k)
            return self
        return f


class Sched:
    def __init__(self, nc, sems, same_sync=True):
        self.nc = nc
        self.engs = {}
        for name, same in (("pe", False), ("act", same_sync), ("dve", same_sync), ("pool", same_sync), ("sp", False)):
            self.engs[name] = _Eng(name, sems[name], same)
        self.nops = 0

    def add(self, eng, fn, reads=(), writes=(), dsem=None):
        E = self.engs[eng]
        need = {}

        def dep(t):
            s, v = t
            k = id(s)
            if k not in need or need[k][1] < v:
                need[k] = (s, v)

        for b in reads:
            if b.w is not None:
                dep(b.w)
            if b.excl:
                for k_, t in b.r.items():
                    if k_ != id(E.sem):
                        dep(t)
        for b in writes:
            if b.w is not None:
                dep(b.w)
            for t in b.r.values():
                dep(t)
        waits = []
        for k, (s, v) in need.items():
            if s is E.sem and not E.same_sync:
                continue
            if E.waited.get(k, 0) >= v:
                continue
            E.waited[k] = v
            waits.append((s, v))
        if dsem is None:
            E.count += 1
            tok = (E.sem, E.count)
            inc = (E.sem, 1)
        else:
            dsem.count += 16
            tok = (dsem.sem, dsem.count)
            inc = (dsem.sem, 16)
        rec = _Rec()
        fn(rec)
        E.ops.append((waits, rec.call, inc))
        for b in reads:
            k = id(tok[0])
            if k not in b.r or b.r[k][1] < tok[1]:
                b.r[k] = tok
        for b in writes:
            b.w = tok
            b.r = {}
        self.nops += 1
        return tok

    def wait_only(self, eng, toks):
        E = self.engs[eng]
        waits = []
        for (s, v) in toks:
            if E.waited.get(id(s), 0) >= v:
                continue
            E.waited[id(s)] = v
            waits.append((s, v))
        E.ops.append((waits, None, None))

    def emit(self):
        nc = self.nc

        def run(h, name):
            for waits, fn, inc in self.engs[name].ops:
                for (s, v) in waits:
                    h.wait_ge(s, v)
                if fn is not None:
                    getattr(h, fn[0])(*fn[1], **fn[2]).then_inc(inc[0], inc[1])

        with nc.Block() as block:
            @block.tensor
            def _(h):
                run(h, "pe")

            @block.scalar
            def _(h):
                run(h, "act")

            @block.vector
            def _(h):
                run(h, "dve")

            @block.gpsimd
            def _(h):
                run(h, "pool")

            @block.sync
            def _(h):
                run(h, "sp")


def alias_barrier(new_bufs, old_bufs):
    for nb in new_bufs:
        for ob in old_bufs:
            if ob.w is not None:
                k = id(ob.w[0])
                if k not in nb.r or nb.r[k][1] < ob.w[1]:
                    nb.r[k] = ob.w
            for k, t in ob.r.items():
                if k not in nb.r or nb.r[k][1] < t[1]:
                    nb.r[k] = t


def block_table():
    B = []

    def blk(name, kcb, ncb, pieces, bias=()):
        B.append(dict(name=name, kcb=kcb, ncb=ncb, pieces=pieces, bias=bias))

    for j in range(4):
        blk(f"QK{j}", 8, 512, [("w_in", j * 512, 512, 0, "g1")])
    blk("G", 8, 8, [("w_in", MI, 8, 0, "g1")])
    for h in range(4):
        blk(f"M{h}", 8, 512, [("w_in", MV + h * 256, 256, 0, "g1"), ("w_in", MO + h * 256, 256, 256, "g1")],
            bias=[(MV + h * 256, 256, 0), (MO + h * 256, 256, 256)])
    for h in range(4):
        blk(f"RQK{h}", 8, 512, [("w_in", RQ + h * 256, 256, 0, "g1"), ("w_in", RK + h * 256, 256, 256, "g1")])
        blk(f"RV{h}", 8, 512, [("w_in", RV + h * 512, 512, 0, "g1")], bias=[(RV + h * 512, 512, 0)])
        blk(f"RG{h}", 8, 512, [("w_in", RG + h * 512, 512, 0, "g1")], bias=[(RG + h * 512, 512, 0)])
    for j in range(2):
        blk(f"GM{j}", 8, 512, [("w_in", GM + j * 512, 512, 0, "g1")])
    for j in range(2):
        blk(f"GR{j}", 8, 512, [("w_in", GR + j * 512, 512, 0, "g1")])
    for j in range(2):
        blk(f"BM{j}", 8, 512, [("w_bm", j * 512, 512, 0, "gmn")])
    for j in range(4):
        blk(f"BR{j}", 16, 256, [("w_br", j * 256, 256, 0, "grn")])
    for j in range(2):
        blk(f"O{j}", 8, 512, [("w_out", j * 512, 512, 0, None)])
    for j in range(8):
        blk(f"F1{j}", 8, 512, [("w_ff1", j * 512, 512, 0, "g2")])
    for j in range(8):
        blk(f"F2{j}", 32, 128, [("w_ff2", j * 128, 128, 0, None)])
    for j in range(2):
        blk(f"PG{j}", 8, 512, [("w_pe_gate", j * 512, 512, 0, "g3")])
    blk("PE", 2, 1024, [("w_pe", 0, 1024, 0, None)])
    return B


BLOCKS = block_table()
NBLK = len(BLOCKS)
BIDX = {b["name"]: i for i, b in enumerate(BLOCKS)}


def build(NL, S, TT=256, NLW=4, same_sync=True, NSLOT=3, NSTAGE=1, NXT=1, max_steps=None, skip_conv=False, dbg=0):
    NSUB = TT // 128
    NT = S // TT
    assert S % TT == 0
    nc = bass.Bass("TRN2", target_bir_lowering=False)

    def din(name, shape):
        return nc.dram_tensor(name, list(shape), F32, kind="ExternalInput").ap()

    x_d = din("x", [S, D])
    p_d = din("p", [NLW, S, 256])
    wsrc = {
        "w_in": din("w_in", [NLW, D, N_IN]),
        "w_bm": din("w_bm", [NLW, 1024, D]),
        "w_br": din("w_br", [NLW, 2048, D]),
        "w_out": din("w_out", [NLW, D, D]),
        "w_ff1": din("w_ff1", [NLW, D, DFF]),
        "w_ff2": din("w_ff2", [NLW, DFF, D]),
        "w_pe_gate": din("w_pe_gate", [NLW, D, D]),
        "w_pe": din("w_pe", [NLW, 256, D]),
    }
    b_in_d = din("b_in", [NLW, N_IN])
    smalls_d = din("smalls", [NLW, P, NSM])
    consts_d = din("consts", [P, NCST])
    cos_d = din("cos_t", [P, S])
    sin_d = din("sin_t", [P, S])
    out_d = nc.dram_tensor("out", [S, D], F32, kind="ExternalOutput").ap()
    wscr = nc.dram_tensor("wscr", [NL * NBLK, P, WBLK], BF16).ap()
    xscr = nc.dram_tensor("xscr", [NT, P, KC * TT], F32).ap()

    st = contextlib.ExitStack()
    with st:
        sems = {n: st.enter_context(nc.semaphore("s_" + n)) for n in ("pe", "act", "dve", "pool", "sp")}
        S_ = Sched(nc, sems, same_sync=same_sync)

        def newdsem(name):
            return DSem(st.enter_context(nc.semaphore("d_" + name)))

        def sb(name, shape, dt):
            return nc.alloc_sbuf_tensor("sb_" + name, list(shape), dt)

        xT = sb("xT", [P, KC, TT], F32)
        hT = sb("hT", [P, KC, TT], BF16)
        ZW = TT + 3
        BIGN = max(16 * ZW + 16 * TT, 32 * TT)
        BIGA = sb("BIGA", [P, BIGN], BF16)
        zqk = BIGA[:, 0:16 * ZW].rearrange("p (c t) -> p c t", t=ZW)
        qkT = BIGA[:, 16 * ZW:16 * ZW + 16 * TT].rearrange("p (c t) -> p c t", t=TT)
        hidT = BIGA[:, 0:32 * TT].rearrange("p (c t) -> p c t", t=TT)
        sgm = BIGA[:, 0:8 * TT].rearrange("p (c t) -> p c t", t=TT)
        sgr = BIGA[:, 8 * TT:16 * TT].rearrange("p (c t) -> p c t", t=TT)
        mrgT = BIGA[:, 16 * TT:24 * TT].rearrange("p (c t) -> p c t", t=TT)
        sgT = BIGA[:, 0:8 * TT].rearrange("p (c t) -> p c t", t=TT)
        RETB = sb("RETB", [P, 16 * TT], BF16)
        rpre = RETB[:, 0:4 * TT].rearrange("p (c t) -> p c t", t=TT)
        rqkT = RETB[:, 4 * TT:8 * TT].rearrange("p (c t) -> p c t", t=TT)
        rv = RETB[:, 8 * TT:12 * TT].rearrange("p (c n) -> p c n", n=512)
        Vext = RETB[:, 12 * TT:12 * TT + NSUB * 258].rearrange("p (c n) -> p c n", n=258)
        yrT = RETB[:, :].rearrange("p (c t) -> p c t", t=TT)
        srg = sb("srg", [P, NSUB, 4, 512], BF16)
        Ur = sb("Ur", [P, NSUB, 4, 512], BF16)
        sigo = srg[:].rearrange("p c h n -> p (c h n)")[:, 0:NSUB * 4 * 256].rearrange("p (c h n) -> p c h n", c=NSUB, h=4)
        Um = Ur[:].rearrange("p c h n -> p (c h n)")[:, 0:NSUB * 4 * 258].rearrange("p (c h n) -> p c h n", c=NSUB, h=4)
        STm = sb("STm", [P, NSUB, 4, 6], F32)
        STr = sb("STr", [P, NSUB, 4, 6], F32)
        MVb = sb("MVb", [P, NSUB, 4, 2], F32)
        XS = sb("XS", [P, 6, NSUB, 4], F32)
        ymT = sb("ymT", [P, 8, TT], BF16)
        wsl = [sb(f"wsl{i}", [P, WBLK], BF16) for i in range(NSLOT)]
        stage32 = [sb(f"st32_{i}", [P, 2048], F32) for i in range(NSTAGE)]
        stage16 = [sb(f"st16_{i}", [P, 2048], BF16) for i in range(NSTAGE)]
        Cbf = sb("Cbf", [P, 2, 4, 258], BF16)
        Rbf = sb("Rbf", [P, 2, 4, 512], BF16)
        cosb = sb("cosb", [P, TT], F32)
        sinb = sb("sinb", [P, TT], F32)
        NT32 = 3
        T32 = [sb(f"T32_{i}", [P, TT], F32) for i in range(NT32)]
        rstd = sb("rstd", [P, TT], F32)
        Spp = [sb(f"Spp{i}", [P, 128], BF16) for i in range(2)]
        kp = [sb(f"kp{i}", [P, 256], BF16) for i in range(2)]
        ytmp = [sb(f"ytmp{i}", [P, 512], BF16) for i in range(2)]
        ytm2 = [sb(f"ytm2{i}", [P, 512], BF16) for i in range(2)]
        xtok = [sb(f"xtok{i}", [P, D], F32) for i in range(NXT)]
        ptok = [sb(f"ptok{i}", [P, 256], F32) for i in range(2)]
        pbf = [sb(f"pbf{i}", [P, 256], BF16) for i in range(2)]
        pT = sb("pT", [P, 2, TT], BF16)
        rel = [sb(f"rel{i}", [P, TT], BF16) for i in range(2)]
        gif = sb("gif", [P, NSUB, 8], F32)
        spl = sb("spl", [P, NSUB, 4], F32)
        G1 = sb("G1", [P, NSUB, 16], F32)
        GE = sb("GE", [P, NSUB, 16], F32)
        stats = [sb(f"stats{i}", [P, 6], F32) for i in range(2)]
        mv_ = [sb(f"mv{i}", [P, 2], F32) for i in range(2)]
        sc = [sb(f"sc{i}", [P, 8], F32) for i in range(2)]
        smalls = sb("smalls", [P, NSM], F32)
        consts = sb("consts", [P, NCST], F32)
        identb = sb("identb", [P, 128], BF16)
        onesb = sb("onesb", [P, 128], BF16)
        onerow = sb("onerow", [P, 128], BF16)
        halo = sb("halo", [P, 16, 3], BF16)

        PSD = [nc.alloc_psum_tensor(f"psd{i}", [P, 512], F32) for i in range(2)]
        PSS = nc.alloc_psum_tensor("pss", [P, 512], F32)
        PSO = [nc.alloc_psum_tensor(f"pso{i}", [P, 512], F32) for i in range(2)]
        PSD = PSD + PSO
        PSU = nc.alloc_psum_tensor("psu", [P, 2, 512], F32)
        PST = nc.alloc_psum_tensor("pst", [P, 1024], BF16)

        def bl(name, n):
            return [Buf(f"{name}{i}") for i in range(n)]

        B_xT = bl("xT", KC)
        B_hT = bl("hT", KC)
        B_zqk = bl("zqk", 16)
        B_qkT = bl("qkT", 16)
        B_hid = bl("hid", 32)
        B_sgm = bl("sgm", 8)
        B_sgr = bl("sgr", 8)
        B_mrg = bl("mrg", 8)
        B_sgT = bl("sgT", 8)
        B_rpre = bl("rpre", 4)
        B_rqkT = bl("rqkT", 4)
        B_Vext = bl("Vext", NSUB)
        B_sigo = bl("sigo", NSUB * 4)
        B_rv = bl("rv", NSUB)
        B_srg = bl("srg", NSUB * 4)
        B_Um = bl("Um", NSUB * 4)
        B_Ur = bl("Ur", NSUB * 4)
        B_STm = bl("STm", NSUB * 4)
        B_STr = bl("STr", NSUB * 4)
        B_MVb = Buf("MVb")
        B_XS = Buf("XS")
        B_ymT = bl("ymT", 8)
        B_yrT = bl("yrT", 16)
        B_wsl = bl("wsl", NSLOT)
        B_st32 = bl("st32", NSTAGE)
        B_st16 = bl("st16", NSTAGE)
        B_C32 = bl("C32", 4)
        B_Cbf = bl("Cbf", 4)
        B_R32 = bl("R32", 4)
        B_Rbf = bl("Rbf", 4)
        B_cos = Buf("cos")
        B_sin = Buf("sin")
        B_T32 = bl("T32", NT32)
        B_rstd = Buf("rstd")
        B_Spp = bl("Spp", 2)
        B_kp = bl("kp", 2)
        B_ytmp = bl("ytmp", 2)
        B_ytm2 = bl("ytm2", 2)
        B_xtok = bl("xtok", NXT)
        B_ptok = bl("ptok", 2)
        B_pbf = bl("pbf", 2)
        B_pT = Buf("pT")
        B_rel = bl("rel", 2)
        B_gif = Buf("gif")
        B_spl = Buf("spl")
        B_G1 = Buf("G1")
        B_GE = Buf("GE")
        B_stats = bl("stats", 2)
        B_mv = bl("mv", 2)
        B_sc = bl("sc", 2)
        B_smalls = Buf("smalls")
        B_consts = Buf("consts")
        B_cb = Buf("constb")
        B_halo = Buf("halo")
        B_PSD = [Buf(f"psd{i}", excl=True) for i in range(2)]
        B_PSS = Buf("pss", excl=True)
        B_PSO = [Buf(f"pso{i}", excl=True) for i in range(2)]
        B_PSD = B_PSD + B_PSO
        B_PSU = Buf("psu", excl=True)
        B_PSTy = Buf("psty", excl=True)
        B_PSTk = B_PSTy
        B_wscr = [Buf(f"wscr{i}") for i in range(NL * NBLK)]
        B_xscr = bl("xscr", NT)
        B_out = Buf("out")

        D_wsl = [newdsem(f"wsl{i}") for i in range(NSLOT)]
        D_stin = [newdsem(f"stin{i}") for i in range(NSTAGE)]
        D_stout = [newdsem(f"stout{i}") for i in range(NSTAGE)]
        D_x = newdsem("x")
        D_xs = newdsem("xs")
        D_misc = newdsem("misc")
        D_cos = newdsem("cos")
        D_sin = newdsem("sin")
        D_xtok = [newdsem(f"xtok{i}") for i in range(NXT)]
        D_ptok = [newdsem(f"ptok{i}") for i in range(2)]
        D_sm = newdsem("sm")

        A = S_.add
        rr = {"psd": 0, "pso": 0, "t32": 0, "spp": 0, "kp": 0, "yt": 0, "yt2": 0, "st": 0, "rel": 0,
              "stats": 0, "ew": 0, "xtok": 0, "ptok": 0}

        def nxt(k, n):
            v = rr[k]
            rr[k] = (v + 1) % n
            return v

        def ew_eng():
            return "dve" if nxt("ew", 2) == 0 else "pool"

        A("sp", lambda h: h.dma_start(out=consts[:], in_=consts_d), writes=[B_consts], dsem=D_misc)
        A("dve", lambda h: h.tensor_copy(out=identb[:], in_=consts[:, C_ID:C_ID + 128]), reads=[B_consts], writes=[B_cb])
        A("dve", lambda h: h.memset(onesb[:], 1.0 / D), writes=[B_cb])
        A("dve", lambda h: h.memset(onerow[:], 0.0), writes=[B_cb])
        A("dve", lambda h: h.memset(onerow[0:1, :], 1.0), writes=[B_cb])
        for i_ in range(NSLOT):
            A("dve", lambda h, i_=i_: h.memset(wsl[i_][:, BIASOFF:WBLK], 0.0), writes=[B_wsl[i_]])
        ident32 = consts[:, C_ID:C_ID + 128]
        triu = consts[:, C_TRIU:C_TRIU + 128]
        ones32 = consts[:, C_ONES:C_ONES + 128]
        maskm = consts[:, C_MASKM:C_MASKM + 128]

        def smc(name, j, n=1, smt=None):
            o = SM[name] + j
            return (smalls if smt is None else smt)[:, o:o + n]

        def conv_jobs(l):
            jobs = []
            for bi, b in enumerate(BLOCKS):
                kcb, ncb = b["kcb"], b["ncb"]
                for (src, c0, ncols, dc, gname) in b["pieces"]:
                    per = max(1, 2048 // ncols)
                    for k0 in range(0, kcb, per):
                        k1 = min(kcb, k0 + per)
                        jobs.append(("w", l, bi, src, c0, ncols, dc, gname, k0, k1))
                for (c0, ncols, dc) in b["bias"]:
                    jobs.append(("b", l, bi, c0, ncols, dc))
            return jobs

        def do_conv_job(job, smt, smb):
            s = nxt("st", NSTAGE)
            if job[0] == "w":
                _, l, bi, src, c0, ncols, dc, gname, k0, k1 = job
                b = BLOCKS[bi]
                kcb, ncb = b["kcb"], b["ncb"]
                nk = k1 - k0
                srcap = wsrc[src][l].rearrange("(k p) n -> p k n", p=P)[:, k0:k1, c0:c0 + ncols]
                s32 = stage32[s][:, 0:nk * ncols].rearrange("p (k n) -> p k n", n=ncols)
                s16 = stage16[s][:, 0:nk * ncols].rearrange("p (k n) -> p k n", n=ncols)
                A("sp", lambda h: h.dma_start(out=s32, in_=srcap), writes=[B_st32[s]], dsem=D_stin[s])
                if gname is None:
                    A("pool", lambda h: h.tensor_copy(out=stage16[s][:, 0:nk * ncols], in_=stage32[s][:, 0:nk * ncols]),
                      reads=[B_st32[s]], writes=[B_st16[s]])
                else:
                    for k in range(nk):
                        g = smc(gname, k0 + k, smt=smt)
                        eng = "pool" if k % 2 == 0 else "act"
                        if eng == "pool":
                            A("pool", lambda h, k=k, g=g: h.tensor_scalar(out=s16[:, k, :], in0=s32[:, k, :], scalar1=g,
                                                                          scalar2=None, op0=ALU.mult),
                              reads=[B_st32[s], smb], writes=[B_st16[s]])
                        else:
                            A("act", lambda h, k=k, g=g: h.activation(out=s16[:, k, :], in_=s32[:, k, :], func=AF.Copy, scale=g),
                              reads=[B_st32[s], smb], writes=[B_st16[s]])
                dst = wscr[l * NBLK + bi][:, 0:kcb * ncb].rearrange("p (k n) -> p k n", n=ncb)[:, k0:k1, dc:dc + ncols]
                A("pool", lambda h: h.dma_start(out=dst, in_=s16), reads=[B_st16[s]], writes=[B_wscr[l * NBLK + bi]], dsem=D_stout[s])
            else:
                _, l, bi, c0, ncols, dc = job
                srcap = b_in_d[l:l + 1, c0:c0 + ncols]
                A("sp", lambda h: h.dma_start(out=stage32[s][0:1, 0:ncols], in_=srcap), writes=[B_st32[s]], dsem=D_stin[s])
                A("pool", lambda h: h.tensor_copy(out=stage16[s][0:1, 0:ncols], in_=stage32[s][0:1, 0:ncols]),
                  reads=[B_st32[s]], writes=[B_st16[s]])
                dst = wscr[l * NBLK + bi][0:1, BIASOFF + dc:BIASOFF + dc + ncols]
                A("pool", lambda h: h.dma_start(out=dst, in_=stage16[s][0:1, 0:ncols]), reads=[B_st16[s]],
                  writes=[B_wscr[l * NBLK + bi]], dsem=D_stout[s])

        def rmsnorm_to_hT():
            for kc in range(KC):
                A("act", lambda h, kc=kc: h.activation(out=hT[:, kc, :], in_=xT[:, kc, :], func=AF.Square),
                  reads=[B_xT[kc]], writes=[B_hT[kc]])
            pi = nxt("psd", 4)
            for kc in range(KC):
                A("pe", lambda h, kc=kc: h.matmul(PSD[pi][:, 0:TT], lhsT=onesb[:], rhs=hT[:, kc, :], start=(kc == 0), stop=(kc == KC - 1)),
                  reads=[B_hT[kc], B_cb], writes=[B_PSD[pi]])
            A("act", lambda h: h.activation(out=rstd[:], in_=PSD[pi][:, 0:TT], func=AF.Ln, bias=EPS), reads=[B_PSD[pi]], writes=[B_rstd])
            A("act", lambda h: h.activation(out=rstd[:], in_=rstd[:], func=AF.Exp, scale=-0.5), reads=[B_rstd], writes=[B_rstd])
            for kc in range(KC):
                e = ew_eng()
                A(e, lambda h, kc=kc: h.tensor_tensor(out=hT[:, kc, :], in0=xT[:, kc, :], in1=rstd[:], op=ALU.mult),
                  reads=[B_xT[kc], B_rstd], writes=[B_hT[kc]])

        def fm_group(w, bw, nchunks, col0, evac):
            wv = w[:, 0:KC * 512].rearrange("p (k n) -> p k n", n=512)
            for j in range(nchunks):
                pi = nxt("psd", 4)
                for kc in range(KC):
                    A("pe", lambda h, kc=kc, j=j, pi=pi: h.matmul(PSD[pi][:, 0:TT], lhsT=wv[:, kc, col0 + j * 128:col0 + (j + 1) * 128],
                                                                   rhs=hT[:, kc, :], start=(kc == 0), stop=(kc == KC - 1)),
                      reads=[bw, B_hT[kc]], writes=[B_PSD[pi]])
                evac(j, pi)

        def tm_group(w, bw, ncols, evac, bias=True):
            wv = w[:, 0:KC * ncols].rearrange("p (k n) -> p k n", n=ncols)
            for c in range(NSUB):
                pi = nxt("psd", 4)
                if bias:
                    A("pe", lambda h, pi=pi: h.matmul(PSD[pi][:, 0:ncols], lhsT=onerow[:], rhs=w[:, BIASOFF:BIASOFF + ncols],
                                                     start=True, stop=False), reads=[bw, B_cb], writes=[B_PSD[pi]])
                for kc in range(KC):
                    A("pe", lambda h, kc=kc, c=c, pi=pi: h.matmul(PSD[pi][:, 0:ncols], lhsT=hT[:, kc, c * 128:(c + 1) * 128], rhs=wv[:, kc, :],
                                                                   start=(kc == 0 and not bias), stop=(kc == KC - 1)),
                      reads=[bw, B_hT[kc]], writes=[B_PSD[pi]])
                evac(c, pi)

        def ln_scalars(pso_ap, nfeat, r_ap, Bpso, extra_reads):
            si = nxt("stats", 2)
            A("dve", lambda h: h.bn_stats(out=stats[si][:], in_=pso_ap), reads=[Bpso], writes=[B_stats[si]])
            A("dve", lambda h: h.bn_aggr(out=mv_[si][:], in_=stats[si][:]), reads=[B_stats[si]], writes=[B_mv[si]])
            s_ = sc[si]
            A("dve", lambda h: h.tensor_tensor(out=s_[:, 0:1], in0=r_ap, in1=r_ap, op=ALU.mult), reads=extra_reads, writes=[B_sc[si]])
            A("dve", lambda h: h.tensor_tensor(out=s_[:, 1:2], in0=s_[:, 0:1], in1=mv_[si][:, 1:2], op=ALU.mult),
              reads=[B_sc[si], B_mv[si]], writes=[B_sc[si]])
            A("act", lambda h: h.activation(out=s_[:, 2:3], in_=s_[:, 1:2], func=AF.Ln, bias=EPS), reads=[B_sc[si]], writes=[B_sc[si]])
            A("act", lambda h: h.activation(out=s_[:, 2:3], in_=s_[:, 2:3], func=AF.Exp, scale=-0.5), reads=[B_sc[si]], writes=[B_sc[si]])
            A("dve", lambda h: h.tensor_tensor(out=s_[:, 3:4], in0=s_[:, 2:3], in1=r_ap, op=ALU.mult),
              reads=[B_sc[si]] + list(extra_reads), writes=[B_sc[si]])
            A("dve", lambda h: h.scalar_tensor_tensor(out=s_[:, 4:5], in0=mv_[si][:, 0:1], scalar=-1.0, in1=s_[:, 3:4], op0=ALU.mult, op1=ALU.mult),
              reads=[B_sc[si], B_mv[si]], writes=[B_sc[si]])
            return s_[:, 3:4], s_[:, 4:5], B_sc[si]

        def finalize_heads(is_m):
            U, ST, BU, BST = (Um, STm, B_Um, B_STm) if is_m else (Ur, STr, B_Ur, B_STr)
            nf = 256 if is_m else 512
            for c in range(NSUB):
                for hh in range(4):
                    A("dve", lambda h, c=c, hh=hh: h.bn_aggr(out=MVb[:, c, hh, :], in_=ST[:, c, hh, :]), reads=[BST[c * 4 + hh]], writes=[B_MVb])
            X = [XS[:, i, :, :] for i in range(6)]
            if is_m:
                eT = GE[:, :, 8:12]
                den = Um[:, :, :, 256]
                A("dve", lambda h: h.tensor_tensor(out=X[0], in0=den, in1=eT, op=ALU.mult), reads=BU + [B_GE], writes=[B_XS])
                A("dve", lambda h: h.scalar_tensor_tensor(out=X[1], in0=X[0], scalar=-1.0, in1=X[0], op0=ALU.mult, op1=ALU.max), reads=[B_XS], writes=[B_XS])
                A("dve", lambda h: h.tensor_scalar(out=X[0], in0=X[1], scalar1=1.0, scalar2=None, op0=ALU.max), reads=[B_XS], writes=[B_XS])
                A("dve", lambda h: h.reciprocal(out=X[1], in_=X[0]), reads=[B_XS], writes=[B_XS])
                A("dve", lambda h: h.tensor_tensor(out=X[2], in0=X[1], in1=eT, op=ALU.mult), reads=[B_XS, B_GE], writes=[B_XS])
            else:
                for c in range(NSUB):
                    A("dve", lambda h, c=c: h.tensor_copy(out=XS[:, 2, c, :], in_=consts[:, C_RSC:C_RSC + 4]), reads=[B_consts], writes=[B_XS])
            A("dve", lambda h: h.tensor_tensor(out=X[3], in0=X[2], in1=X[2], op=ALU.mult), reads=[B_XS], writes=[B_XS])
            A("dve", lambda h: h.tensor_tensor(out=X[3], in0=X[3], in1=MVb[:, :, :, 1], op=ALU.mult), reads=[B_XS, B_MVb], writes=[B_XS])
            A("act", lambda h: h.activation(out=X[4], in_=X[3], func=AF.Ln, bias=EPS), reads=[B_XS], writes=[B_XS])
            A("act", lambda h: h.activation(out=X[4], in_=X[4], func=AF.Exp, scale=-0.5), reads=[B_XS], writes=[B_XS])
            A("dve", lambda h: h.tensor_tensor(out=X[4], in0=X[4], in1=X[2], op=ALU.mult), reads=[B_XS], writes=[B_XS])
            A("dve", lambda h: h.scalar_tensor_tensor(out=X[5], in0=MVb[:, :, :, 0], scalar=-1.0, in1=X[4], op0=ALU.mult, op1=ALU.mult),
              reads=[B_XS, B_MVb], writes=[B_XS])
            gate, Bg = (sigo, B_sigo) if is_m else (srg, B_srg)
            nch = nf // 128
            for c in range(NSUB):
                ts = slice(c * 128, (c + 1) * 128)
                for hh in range(4):
                    yi = nxt("yt", 2)
                    A("act", lambda h, c=c, hh=hh, yi=yi: h.activation(out=ytmp[yi][:, 0:nf], in_=U[:, c, hh, 0:nf], func=AF.Identity,
                                                                         bias=XS[:, 5, c, hh:hh + 1], scale=XS[:, 4, c, hh:hh + 1]),
                      reads=[BU[c * 4 + hh], B_XS], writes=[B_ytmp[yi]])
                    y2 = nxt("yt2", 2)
                    A("dve" if (c * 4 + hh) % 3 != 2 else "pool", lambda h, c=c, hh=hh, yi=yi, y2=y2: h.tensor_tensor(out=ytm2[y2][:, 0:nf], in0=ytmp[yi][:, 0:nf], in1=gate[:, c, hh, :], op=ALU.mult),
                      reads=[B_ytmp[yi], Bg[c * 4 + hh]], writes=[B_ytm2[y2]])
                    for j in range(nch):
                        A("pe", lambda h, j=j, y2=y2: h.transpose(out=PST[:, j * 128:(j + 1) * 128], in_=ytm2[y2][:, j * 128:(j + 1) * 128], identity=identb[:]),
                          reads=[B_ytm2[y2], B_cb], writes=[B_PSTy])
                    if is_m:
                        A("dve", lambda h, hh=hh: h.tensor_copy(out=ymT[:, 2 * hh:2 * hh + 2, ts], in_=PST[:, 0:256].rearrange("p (j n) -> p j n", n=128)),
                          reads=[B_PSTy], writes=B_ymT[2 * hh:2 * hh + 2])
                    else:
                        A("dve", lambda h, hh=hh: h.tensor_copy(out=yrT[:, 4 * hh:4 * hh + 4, ts], in_=PST[:, 0:512].rearrange("p (j n) -> p j n", n=128)),
                          reads=[B_PSTy], writes=B_yrT[4 * hh:4 * hh + 4])

        steps = []

        def tile_steps(l, t, first_layer, last_layer):
            t0 = t * TT
            seq_start = (t == 0)
            lb = l * NBLK

            def add_step(bname, fn):
                steps.append((None if bname is None else lb + BIDX[bname], fn))

            def s_load(w, bw):
                if t == 0:
                    A("sp", lambda h: h.dma_start(out=smalls[:], in_=smalls_d[l]), writes=[B_smalls], dsem=D_sm)
                if first_layer:
                    for c in range(NSUB):
                        xi = nxt("xtok", NXT)
                        A("sp", lambda h, c=c, xi=xi: h.dma_start(out=xtok[xi][:], in_=x_d[t0 + c * 128:t0 + (c + 1) * 128, :]),
                          writes=[B_xtok[xi]], dsem=D_xtok[xi])
                        for half in range(2):
                            pi = nxt("psd", 4)
                            for q in range(4):
                                kc = half * 4 + q
                                A("pe", lambda h, q=q, kc=kc, xi=xi, pi=pi: h.transpose(out=PSD[pi][:, q * 128:(q + 1) * 128],
                                                                                        in_=xtok[xi][:, kc * 128:(kc + 1) * 128], identity=ident32),
                                  reads=[B_xtok[xi], B_consts], writes=[B_PSD[pi]])
                            A("act", lambda h, half=half, c=c, pi=pi: h.activation(
                                out=xT[:, half * 4:half * 4 + 4, c * 128:(c + 1) * 128],
                                in_=PSD[pi][:, :].rearrange("p (q n) -> p q n", n=128), func=AF.Copy),
                              reads=[B_PSD[pi]], writes=B_xT[half * 4:half * 4 + 4])
                else:
                    A("sp", lambda h: h.dma_start(out=xT[:].rearrange("p k t -> p (k t)"), in_=xscr[t]), reads=[B_xscr[t]], writes=B_xT, dsem=D_x)
                A("sp", lambda h: h.dma_start(out=cosb[:], in_=cos_d[:, t0:t0 + TT]), writes=[B_cos], dsem=D_cos)
                A("sp", lambda h: h.dma_start(out=sinb[:], in_=sin_d[:, t0:t0 + TT]), writes=[B_sin], dsem=D_sin)
                if seq_start:
                    A("pool", lambda h: h.memset(Cbf[:], 0.0), writes=B_Cbf)
                    A("pool", lambda h: h.memset(Rbf[:], 0.0), writes=B_Rbf)
                    A("pool", lambda h: h.memset(halo[:], 0.0), writes=[B_halo])
                alias_barrier(B_zqk + B_qkT, B_hid + B_sgm + B_sgr + B_mrg + B_sgT)
                alias_barrier(B_Vext + B_rpre + B_rqkT + B_rv, B_yrT)
                A("pool", lambda h: h.memset(Vext[:, :, 256:258], 1.0), writes=B_Vext)
                A("pool", lambda h: h.tensor_copy(out=zqk[:, :, 0:3], in_=halo[:]), reads=[B_halo], writes=B_zqk)
                rmsnorm_to_hT()

            add_step(None, s_load)

            pend_silu = []
            for j4 in range(4):
                def s_qk(w, bw, j4=j4):
                    def evac(j, pi):
                        c = j4 * 4 + j
                        A("act", lambda h: h.activation(out=zqk[:, c, 3:3 + TT], in_=PSD[pi][:, 0:TT], func=AF.Identity, bias=smc("bqk", c)),
                          reads=[B_PSD[pi], B_smalls], writes=[B_zqk[c]])
                        A("pool", lambda h: h.tensor_copy(out=halo[:, c, :], in_=zqk[:, c, TT:TT + 3]), reads=[B_zqk[c]], writes=[B_halo])
                        ti = nxt("t32", NT32)
                        e = "dve"
                        A(e, lambda h: h.tensor_scalar(out=T32[ti][:], in0=zqk[:, c, 0:TT], scalar1=smc("cw", c * 4 + 0), scalar2=smc("cb", c),
                                                       op0=ALU.mult, op1=ALU.add), reads=[B_zqk[c], B_smalls], writes=[B_T32[ti]])
                        for jj in range(1, 4):
                            A(e, lambda h, jj=jj: h.scalar_tensor_tensor(out=T32[ti][:], in0=zqk[:, c, jj:jj + TT], scalar=smc("cw", c * 4 + jj),
                                                                         in1=T32[ti][:], op0=ALU.mult, op1=ALU.add),
                              reads=[B_zqk[c], B_smalls, B_T32[ti]], writes=[B_T32[ti]])
                        pend_silu.append((c, ti))
                        while len(pend_silu) > 2:
                            c_, ti_ = pend_silu.pop(0)
                            A("act", lambda h, c_=c_, ti_=ti_: h.activation(out=qkT[:, c_, :], in_=T32[ti_][:], func=AF.Silu), reads=[B_T32[ti_]], writes=[B_qkT[c_]])
                    fm_group(w, bw, 4, 0, evac)
                    if j4 == 3:
                        while pend_silu:
                            c_, ti_ = pend_silu.pop(0)
                            A("act", lambda h, c_=c_, ti_=ti_: h.activation(out=qkT[:, c_, :], in_=T32[ti_][:], func=AF.Silu), reads=[B_T32[ti_]], writes=[B_qkT[c_]])
                add_step(f"QK{j4}", s_qk)

            def s_gates(w, bw):
                wv = w[:, 0:KC * 8].rearrange("p (k n) -> p k n", n=8)
                for c in range(NSUB):
                    pi = nxt("psd", 4)
                    for kc in range(KC):
                        A("pe", lambda h, kc=kc, c=c, pi=pi: h.matmul(PSD[pi][:, 0:8], lhsT=hT[:, kc, c * 128:(c + 1) * 128], rhs=wv[:, kc, :],
                                                                       start=(kc == 0), stop=(kc == KC - 1)),
                          reads=[bw, B_hT[kc]], writes=[B_PSD[pi]])
                    A("dve", lambda h, c=c, pi=pi: h.tensor_tensor(out=gif[:, c, :], in0=PSD[pi][:, 0:8], in1=smc("bgate", 0, 8), op=ALU.add),
                      reads=[B_PSD[pi], B_smalls], writes=[B_gif])
                A("act", lambda h: h.activation(out=spl[:], in_=gif[:, :, 4:8], func=AF.Exp, scale=-1.0), reads=[B_gif], writes=[B_spl])
                A("act", lambda h: h.activation(out=spl[:], in_=spl[:], func=AF.Ln, bias=1.0), reads=[B_spl], writes=[B_spl])
                for c in range(NSUB):
                    pi = nxt("psd", 4)
                    A("pe", lambda h, c=c, pi=pi: h.matmul(PSD[pi][:, 0:4], lhsT=triu, rhs=spl[:, c, :], start=True, stop=True),
                      reads=[B_spl, B_consts], writes=[B_PSD[pi]])
                    A("pe", lambda h, c=c, pi=pi: h.matmul(PSD[pi][:, 4:8], lhsT=ones32, rhs=spl[:, c, :], start=True, stop=True),
                      reads=[B_spl, B_consts], writes=[B_PSD[pi]])
                    A("dve", lambda h, c=c, pi=pi: h.tensor_tensor(out=G1[:, c, 0:4], in0=PSD[pi][:, 0:4], in1=gif[:, c, 0:4], op=ALU.add),
                      reads=[B_PSD[pi], B_gif], writes=[B_G1])
                    A("dve", lambda h, c=c, pi=pi: h.scalar_tensor_tensor(out=G1[:, c, 4:8], in0=G1[:, c, 0:4], scalar=-LN16, in1=PSD[pi][:, 4:8],
                                                                          op0=ALU.add, op1=ALU.subtract),
                      reads=[B_PSD[pi], B_G1], writes=[B_G1])
                    A("dve", lambda h, c=c, pi=pi: h.tensor_scalar(out=G1[:, c, 8:16], in0=PSD[pi][:, 0:8], scalar1=-1.0, scalar2=None, op0=ALU.mult),
                      reads=[B_PSD[pi]], writes=[B_G1])
                A("act", lambda h: h.activation(out=GE[:], in_=G1[:], func=AF.Exp), reads=[B_G1], writes=[B_GE])

            add_step("G", s_gates)

            for hd in range(4):
                def s_m(w, bw, hd=hd):
                    if hd == 0:
                        alias_barrier(B_Um + B_sigo, B_Ur + B_srg)
                    def evac(c, pi):
                        A("dve", lambda h: h.tensor_copy(out=Vext[:, c, 0:256], in_=PSD[pi][:, 0:256]), reads=[B_PSD[pi]], writes=[B_Vext[c]])
                        if dbg in (2, 3):
                            A("act", lambda h: h.activation(out=sigo[:, c, hd, :], in_=PSD[pi][:, 256:512], func=AF.Copy), reads=[B_PSD[pi]], writes=[B_sigo[c * 4 + hd]])
                        else:
                            A("act", lambda h: h.activation(out=sigo[:, c, hd, :], in_=PSD[pi][:, 256:512], func=AF.Sigmoid), reads=[B_PSD[pi]], writes=[B_sigo[c * 4 + hd]])
                    tm_group(w, bw, 512, evac, bias=(dbg != 3))
                    if dbg in (1, 2, 3):
                        return
                    qc = [2 * hd, 2 * hd + 1]
                    kcx = [8 + 2 * hd, 8 + 2 * hd + 1]

                    def pre(c):
                        ts = slice(c * 128, (c + 1) * 128)
                        for j in range(2):
                            A("pe", lambda h, j=j: h.matmul(PSS[:, 0:128], lhsT=qkT[:, kcx[j], ts], rhs=qkT[:, qc[j], ts], start=(j == 0), stop=(j == 1)),
                              reads=[B_qkT[kcx[j]], B_qkT[qc[j]]], writes=[B_PSS])
                        for j in range(2):
                            A("pe", lambda h, j=j: h.transpose(out=PST[:, 512 + j * 128:512 + (j + 1) * 128], in_=qkT[:, kcx[j], ts], identity=identb[:]),
                              reads=[B_qkT[kcx[j]], B_cb], writes=[B_PSTk])
                        ki = nxt("kp", 2)
                        A("act", lambda h: h.activation(out=kp[ki][:], in_=PST[:, 512:768], func=AF.Copy, scale=GE[:, c, 4 + hd:5 + hd]),
                          reads=[B_PSTk, B_GE], writes=[B_kp[ki]])
                        si = nxt("spp", 2)
                        A("dve", lambda h: h.scalar_tensor_tensor(out=Spp[si][:], in0=PSS[:, 0:128], scalar=GE[:, c, hd:hd + 1], in1=maskm,
                                                                  op0=ALU.mult, op1=ALU.mult), reads=[B_PSS, B_GE, B_consts], writes=[B_Spp[si]])
                        return si, ki

                    def outp(c, si):
                        ts = slice(c * 128, (c + 1) * 128)
                        oi = nxt("pso", 2)
                        A("pe", lambda h: h.matmul(PSO[oi][:, 0:257], lhsT=Spp[si][:], rhs=Vext[:, c, 0:257], start=True, stop=False),
                          reads=[B_Spp[si], B_Vext[c]], writes=[B_PSO[oi]])
                        for j in range(2):
                            A("pe", lambda h, j=j: h.matmul(PSO[oi][:, 0:257], lhsT=qkT[:, qc[j], ts], rhs=Cbf[:, j, hd, 0:257], start=False, stop=(j == 1)),
                              reads=[B_qkT[qc[j]], B_Cbf[hd]], writes=[B_PSO[oi]])
                        A("act", lambda h: h.activation(out=Um[:, c, hd, 0:257], in_=PSO[oi][:, 0:257], func=AF.Copy),
                          reads=[B_PSO[oi]], writes=[B_Um[c * 4 + hd]])
                        A("dve", lambda h: h.bn_stats(out=STm[:, c, hd, :], in_=PSO[oi][:, 0:256]), reads=[B_PSO[oi]], writes=[B_STm[c * 4 + hd]])

                    def state(c, ki):
                        for j in range(2):
                            A("pe", lambda h, j=j: h.matmul(PSU[:, j, 0:257], lhsT=kp[ki][:, j * 128:(j + 1) * 128], rhs=Vext[:, c, 0:257], start=True, stop=True),
                              reads=[B_kp[ki], B_Vext[c]], writes=[B_PSU])
                        A("dve", lambda h: h.scalar_tensor_tensor(out=Cbf[:, :, hd, 0:257], in0=Cbf[:, :, hd, 0:257], scalar=GE[:, c, 12 + hd:13 + hd],
                                                                  in1=PSU[:, :, 0:257], op0=ALU.mult, op1=ALU.add),
                          reads=[B_Cbf[hd], B_GE, B_PSU], writes=[B_Cbf[hd]])

                    nx = pre(0)
                    for c in range(NSUB):
                        si, ki = nx
                        outp(c, si)
                        if c + 1 < NSUB:
                            nx = pre(c + 1)
                        state(c, ki)
                add_step(f"M{hd}", s_m)

            for hd in range(4):
                def s_rqk(w, bw, hd=hd):
                    def evac(j, pi):
                        A("act", lambda h: h.activation(out=rpre[:, j, :], in_=PSD[pi][:, 0:TT], func=AF.Identity, bias=smc("brqk", hd * 4 + j)),
                          reads=[B_PSD[pi], B_smalls], writes=[B_rpre[j]])
                    fm_group(w, bw, 4, 0, evac)
                    for b0 in (0, 2):
                        x1, x2 = rpre[:, b0, :], rpre[:, b0 + 1, :]
                        ta, tb = nxt("t32", NT32), nxt("t32", NT32)
                        e = ew_eng()
                        A(e, lambda h: h.tensor_tensor(out=T32[ta][:], in0=x1, in1=cosb[:], op=ALU.mult), reads=[B_rpre[b0], B_cos], writes=[B_T32[ta]])
                        A(e, lambda h: h.tensor_tensor(out=T32[tb][:], in0=x2, in1=sinb[:], op=ALU.mult), reads=[B_rpre[b0 + 1], B_sin], writes=[B_T32[tb]])
                        A(e, lambda h: h.tensor_tensor(out=rqkT[:, b0, :], in0=T32[ta][:], in1=T32[tb][:], op=ALU.subtract),
                          reads=[B_T32[ta], B_T32[tb]], writes=[B_rqkT[b0]])
                        tc_, td = nxt("t32", NT32), nxt("t32", NT32)
                        e = ew_eng()
                        A(e, lambda h: h.tensor_tensor(out=T32[tc_][:], in0=x1, in1=sinb[:], op=ALU.mult), reads=[B_rpre[b0], B_sin], writes=[B_T32[tc_]])
                        A(e, lambda h: h.tensor_tensor(out=T32[td][:], in0=x2, in1=cosb[:], op=ALU.mult), reads=[B_rpre[b0 + 1], B_cos], writes=[B_T32[td]])
                        A(e, lambda h: h.tensor_tensor(out=rqkT[:, b0 + 1, :], in0=T32[tc_][:], in1=T32[td][:], op=ALU.add),
                          reads=[B_T32[tc_], B_T32[td]], writes=[B_rqkT[b0 + 1]])
                add_step(f"RQK{hd}", s_rqk)

                def s_rv(w, bw, hd=hd):
                    def evac(c, pi):
                        A("dve", lambda h: h.tensor_copy(out=rv[:, c, :], in_=PSD[pi][:, 0:512]), reads=[B_PSD[pi]], writes=[B_rv[c]])
                    tm_group(w, bw, 512, evac)
                    if hd == 0:
                        finalize_heads(True)
                        alias_barrier(B_Ur + B_srg, B_Um + B_sigo)
                add_step(f"RV{hd}", s_rv)

                def s_rg(w, bw, hd=hd):
                    def evac(c, pi):
                        A("act", lambda h: h.activation(out=srg[:, c, hd, :], in_=PSD[pi][:, 0:512], func=AF.Silu), reads=[B_PSD[pi]], writes=[B_srg[c * 4 + hd]])
                    tm_group(w, bw, 512, evac)
                    maskr = consts[:, C_MASKR + hd * 128:C_MASKR + (hd + 1) * 128]
                    rsc = consts[:, C_RSC + hd:C_RSC + hd + 1]
                    kdec = consts[:, C_KDEC + hd:C_KDEC + hd + 1]
                    cdec = float((1.0 - 2.0 ** (-5.0 - hd)) ** 128)
                    def pre(c):
                        ts = slice(c * 128, (c + 1) * 128)
                        for j in range(2):
                            A("pe", lambda h, j=j: h.matmul(PSS[:, 0:128], lhsT=rqkT[:, 2 + j, ts], rhs=rqkT[:, j, ts], start=(j == 0), stop=(j == 1)),
                              reads=[B_rqkT[2 + j], B_rqkT[j]], writes=[B_PSS])
                        for j in range(2):
                            A("pe", lambda h, j=j: h.transpose(out=PST[:, 512 + j * 128:512 + (j + 1) * 128], in_=rqkT[:, 2 + j, ts], identity=identb[:]),
                              reads=[B_rqkT[2 + j], B_cb], writes=[B_PSTk])
                        ki = nxt("kp", 2)
                        A("act", lambda h: h.activation(out=kp[ki][:], in_=PST[:, 512:768], func=AF.Copy, scale=kdec),
                          reads=[B_PSTk, B_consts], writes=[B_kp[ki]])
                        si = nxt("spp", 2)
                        A("dve", lambda h: h.tensor_tensor(out=Spp[si][:], in0=PSS[:, 0:128], in1=maskr, op=ALU.mult),
                          reads=[B_PSS, B_consts], writes=[B_Spp[si]])
                        return si, ki

                    def outp(c, si):
                        ts = slice(c * 128, (c + 1) * 128)
                        oi = nxt("pso", 2)
                        A("pe", lambda h: h.matmul(PSO[oi][:, 0:512], lhsT=Spp[si][:], rhs=rv[:, c, :], start=True, stop=False),
                          reads=[B_Spp[si], B_rv[c]], writes=[B_PSO[oi]])
                        for j in range(2):
                            A("pe", lambda h, j=j: h.matmul(PSO[oi][:, 0:512], lhsT=rqkT[:, j, ts], rhs=Rbf[:, j, hd, :], start=False, stop=(j == 1)),
                              reads=[B_rqkT[j], B_Rbf[hd]], writes=[B_PSO[oi]])
                        A("act", lambda h: h.activation(out=Ur[:, c, hd, :], in_=PSO[oi][:, 0:512], func=AF.Copy),
                          reads=[B_PSO[oi]], writes=[B_Ur[c * 4 + hd]])
                        A("dve", lambda h: h.bn_stats(out=STr[:, c, hd, :], in_=PSO[oi][:, 0:512]), reads=[B_PSO[oi]], writes=[B_STr[c * 4 + hd]])

                    def state(c, ki):
                        for j in range(2):
                            A("pe", lambda h, j=j: h.matmul(PSU[:, j, :], lhsT=kp[ki][:, j * 128:(j + 1) * 128], rhs=rv[:, c, :], start=True, stop=True),
                              reads=[B_kp[ki], B_rv[c]], writes=[B_PSU])
                        A("dve", lambda h: h.scalar_tensor_tensor(out=Rbf[:, :, hd, :], in0=Rbf[:, :, hd, :], scalar=cdec, in1=PSU[:, :, :],
                                                                  op0=ALU.mult, op1=ALU.add), reads=[B_Rbf[hd], B_PSU], writes=[B_Rbf[hd]])

                    nx = pre(0)
                    for c in range(NSUB):
                        si, ki = nx
                        outp(c, si)
                        if c + 1 < NSUB:
                            nx = pre(c + 1)
                        state(c, ki)
                add_step(f"RG{hd}", s_rg)

            for j2 in range(2):
                def s_gm(w, bw, j2=j2):
                    if j2 == 0:
                        alias_barrier(B_sgm + B_sgr + B_mrg, B_zqk + B_qkT)

                    def evac(j, pi):
                        c = j2 * 4 + j
                        A("act", lambda h: h.activation(out=sgm[:, c, :], in_=PSD[pi][:, 0:TT], func=AF.Sigmoid, bias=smc("bg", c)),
                          reads=[B_PSD[pi], B_smalls], writes=[B_sgm[c]])
                    fm_group(w, bw, 4, 0, evac)
                add_step(f"GM{j2}", s_gm)
            for j2 in range(2):
                def s_gr(w, bw, j2=j2):
                    def evac(j, pi):
                        c = j2 * 4 + j
                        A("act", lambda h: h.activation(out=sgr[:, c, :], in_=PSD[pi][:, 0:TT], func=AF.Sigmoid, bias=smc("bg", 8 + c)),
                          reads=[B_PSD[pi], B_smalls], writes=[B_sgr[c]])
                    fm_group(w, bw, 4, 0, evac)
                    if j2 == 1:
                        alias_barrier(B_yrT, B_Vext + B_rpre + B_rqkT + B_rv)
                        finalize_heads(False)
                add_step(f"GR{j2}", s_gr)

            for j2 in range(2):
                def s_bm(w, bw, j2=j2):
                    wv = w[:, 0:8 * 512].rearrange("p (k n) -> p k n", n=512)
                    for j in range(4):
                        n = j2 * 4 + j
                        pi = nxt("psd", 4)
                        for kc in range(8):
                            A("pe", lambda h, kc=kc, j=j, pi=pi: h.matmul(PSD[pi][:, 0:TT], lhsT=wv[:, kc, j * 128:(j + 1) * 128], rhs=ymT[:, kc, :],
                                                                           start=(kc == 0), stop=(kc == 7)), reads=[bw, B_ymT[kc]], writes=[B_PSD[pi]])
                        A("dve", lambda h, n=n, pi=pi: h.tensor_tensor(out=mrgT[:, n, :], in0=PSD[pi][:, 0:TT], in1=sgm[:, n, :], op=ALU.mult),
                          reads=[B_PSD[pi], B_sgm[n]], writes=[B_mrg[n]])
                add_step(f"BM{j2}", s_bm)
            for j4 in range(4):
                def s_br(w, bw, j4=j4):
                    wv = w[:, 0:16 * 256].rearrange("p (k n) -> p k n", n=256)
                    for j in range(2):
                        n = j4 * 2 + j
                        pi = nxt("psd", 4)
                        for kc in range(16):
                            A("pe", lambda h, kc=kc, j=j, pi=pi: h.matmul(PSD[pi][:, 0:TT], lhsT=wv[:, kc, j * 128:(j + 1) * 128], rhs=yrT[:, kc, :],
                                                                           start=(kc == 0), stop=(kc == 15)), reads=[bw, B_yrT[kc]], writes=[B_PSD[pi]])
                        ti = nxt("t32", NT32)
                        A("dve", lambda h, n=n, pi=pi, ti=ti: h.tensor_tensor(out=T32[ti][:], in0=PSD[pi][:, 0:TT], in1=sgr[:, n, :], op=ALU.mult),
                          reads=[B_PSD[pi], B_sgr[n]], writes=[B_T32[ti]])
                        A("pool", lambda h, n=n, ti=ti: h.tensor_tensor(out=mrgT[:, n, :], in0=T32[ti][:], in1=mrgT[:, n, :], op=ALU.add),
                          reads=[B_T32[ti], B_mrg[n]], writes=[B_mrg[n]])
                add_step(f"BR{j4}", s_br)
            for j2 in range(2):
                def s_o(w, bw, j2=j2):
                    wv = w[:, 0:8 * 512].rearrange("p (k n) -> p k n", n=512)
                    for j in range(4):
                        n = j2 * 4 + j
                        pi = nxt("psd", 4)
                        for kc in range(8):
                            A("pe", lambda h, kc=kc, j=j, pi=pi: h.matmul(PSD[pi][:, 0:TT], lhsT=wv[:, kc, j * 128:(j + 1) * 128], rhs=mrgT[:, kc, :],
                                                                           start=(kc == 0), stop=(kc == 7)), reads=[bw, B_mrg[kc]], writes=[B_PSD[pi]])
                        A("dve", lambda h, n=n, pi=pi: h.tensor_tensor(out=xT[:, n, :], in0=PSD[pi][:, 0:TT], in1=xT[:, n, :], op=ALU.add),
                          reads=[B_PSD[pi], B_xT[n]], writes=[B_xT[n]])
                add_step(f"O{j2}", s_o)

            for j8 in range(8):
                def s_f1(w, bw, j8=j8):
                    if j8 == 0:
                        rmsnorm_to_hT()
                        alias_barrier(B_hid, B_sgm + B_sgr + B_mrg + B_zqk + B_qkT)

                    def evac(j, pi):
                        c = j8 * 4 + j
                        ri = nxt("rel", 2)
                        A("act", lambda h: h.activation(out=rel[ri][:], in_=PSD[pi][:, 0:TT], func=AF.Relu, bias=smc("bf1", c)),
                          reads=[B_PSD[pi], B_smalls], writes=[B_rel[ri]])
                        A("dve", lambda h: h.scalar_tensor_tensor(out=hidT[:, c, :], in0=PSD[pi][:, 0:TT], scalar=smc("bf1", c), in1=rel[ri][:],
                                                                  op0=ALU.add, op1=ALU.mult), reads=[B_PSD[pi], B_smalls, B_rel[ri]], writes=[B_hid[c]])
                    fm_group(w, bw, 4, 0, evac)
                add_step(f"F1{j8}", s_f1)
            for n in range(8):
                def s_f2(w, bw, n=n):
                    wv = w[:, 0:32 * 128].rearrange("p (k n) -> p k n", n=128)
                    pi = nxt("psd", 4)
                    for kc in range(32):
                        A("pe", lambda h, kc=kc, pi=pi: h.matmul(PSD[pi][:, 0:TT], lhsT=wv[:, kc, :], rhs=hidT[:, kc, :], start=(kc == 0), stop=(kc == 31)),
                          reads=[bw, B_hid[kc]], writes=[B_PSD[pi]])
                    A("dve", lambda h, pi=pi: h.scalar_tensor_tensor(out=xT[:, n, :], in0=PSD[pi][:, 0:TT], scalar=smc("bf2", n), in1=xT[:, n, :],
                                                                     op0=ALU.add, op1=ALU.add), reads=[B_PSD[pi], B_smalls, B_xT[n]], writes=[B_xT[n]])
                add_step(f"F2{n}", s_f2)

            for j2 in range(2):
                def s_pg(w, bw, j2=j2):
                    if j2 == 0:
                        rmsnorm_to_hT()
                        alias_barrier(B_sgT, B_hid)
                        for c in range(NSUB):
                            pi_ = nxt("ptok", 2)
                            A("sp", lambda h, c=c, pi_=pi_: h.dma_start(out=ptok[pi_][:], in_=p_d[l, t0 + c * 128:t0 + (c + 1) * 128, :]),
                              writes=[B_ptok[pi_]], dsem=D_ptok[pi_])
                            A("pool", lambda h, pi_=pi_: h.tensor_copy(out=pbf[pi_][:], in_=ptok[pi_][:]), reads=[B_ptok[pi_]], writes=[B_pbf[pi_]])
                            for j in range(2):
                                A("pe", lambda h, j=j, pi_=pi_: h.transpose(out=PST[:, j * 128:(j + 1) * 128], in_=pbf[pi_][:, j * 128:(j + 1) * 128], identity=identb[:]),
                                  reads=[B_pbf[pi_], B_cb], writes=[B_PSTy])
                            A("dve", lambda h, c=c: h.tensor_copy(out=pT[:, :, c * 128:(c + 1) * 128], in_=PST[:, 0:256].rearrange("p (j n) -> p j n", n=128)),
                              reads=[B_PSTy], writes=[B_pT])

                    def evac(j, pi):
                        c = j2 * 4 + j
                        A("act", lambda h: h.activation(out=sgT[:, c, :], in_=PSD[pi][:, 0:TT], func=AF.Sigmoid), reads=[B_PSD[pi]], writes=[B_sgT[c]])
                    fm_group(w, bw, 4, 0, evac)
                add_step(f"PG{j2}", s_pg)

            def s_pe(w, bw):
                wv = w[:, 0:2 * 1024].rearrange("p (k n) -> p k n", n=1024)
                for n in range(8):
                    pi = nxt("psd", 4)
                    for kc in range(2):
                        A("pe", lambda h, kc=kc, n=n, pi=pi: h.matmul(PSD[pi][:, 0:TT], lhsT=wv[:, kc, n * 128:(n + 1) * 128], rhs=pT[:, kc, :],
                                                                       start=(kc == 0), stop=(kc == 1)), reads=[bw, B_pT], writes=[B_PSD[pi]])
                    ti = nxt("t32", NT32)
                    A("dve", lambda h, n=n, pi=pi, ti=ti: h.tensor_tensor(out=T32[ti][:], in0=PSD[pi][:, 0:TT], in1=sgT[:, n, :], op=ALU.mult),
                      reads=[B_PSD[pi], B_sgT[n]], writes=[B_T32[ti]])
                    A("pool", lambda h, n=n, ti=ti: h.tensor_tensor(out=xT[:, n, :], in0=T32[ti][:], in1=xT[:, n, :], op=ALU.add),
                      reads=[B_T32[ti], B_xT[n]], writes=[B_xT[n]])
            add_step("PE", s_pe)

            def s_store(w, bw):
                if not last_layer:
                    A("sp", lambda h: h.dma_start(out=xscr[t], in_=xT[:].rearrange("p k t -> p (k t)")), reads=B_xT, writes=[B_xscr[t]], dsem=D_xs)
                else:
                    for kc in range(KC):
                        A("act", lambda h, kc=kc: h.activation(out=hT[:, kc, :], in_=xT[:, kc, :], func=AF.Square), reads=[B_xT[kc]], writes=[B_hT[kc]])
                    pi = nxt("psd", 4)
                    for kc in range(KC):
                        A("pe", lambda h, kc=kc: h.matmul(PSD[pi][:, 0:TT], lhsT=onesb[:], rhs=hT[:, kc, :], start=(kc == 0), stop=(kc == KC - 1)),
                          reads=[B_hT[kc], B_cb], writes=[B_PSD[pi]])
                    A("act", lambda h: h.activation(out=rstd[:], in_=PSD[pi][:, 0:TT], func=AF.Ln, bias=EPS), reads=[B_PSD[pi]], writes=[B_rstd])
                    A("act", lambda h: h.activation(out=rstd[:], in_=rstd[:], func=AF.Exp, scale=-0.5), reads=[B_rstd], writes=[B_rstd])
                    for kc in range(KC):
                        A("dve", lambda h, kc=kc: h.scalar_tensor_tensor(out=xT[:, kc, :], in0=xT[:, kc, :], scalar=smc("fg", kc), in1=rstd[:],
                                                                         op0=ALU.mult, op1=ALU.mult), reads=[B_xT[kc], B_rstd, B_smalls], writes=[B_xT[kc]])
                    for c in range(NSUB):
                        xi = nxt("xtok", NXT)
                        for half in range(2):
                            pi = nxt("psd", 4)
                            for q in range(4):
                                kc = half * 4 + q
                                A("pe", lambda h, q=q, kc=kc, c=c, pi=pi: h.transpose(out=PSD[pi][:, q * 128:(q + 1) * 128],
                                                                                      in_=xT[:, kc, c * 128:(c + 1) * 128], identity=ident32),
                                  reads=[B_xT[kc], B_consts], writes=[B_PSD[pi]])
                            A("act", lambda h, half=half, xi=xi, pi=pi: h.activation(out=xtok[xi][:, half * 512:(half + 1) * 512], in_=PSD[pi][:, :], func=AF.Copy),
                              reads=[B_PSD[pi]], writes=[B_xtok[xi]])
                        A("sp", lambda h, c=c, xi=xi: h.dma_start(out=out_d[t0 + c * 128:t0 + (c + 1) * 128, :], in_=xtok[xi][:]),
                          reads=[B_xtok[xi]], writes=[B_out], dsem=D_xtok[xi])
            add_step(None, s_store)

        A("sp", lambda h: h.dma_start(out=smalls[:], in_=smalls_d[0]), writes=[B_smalls], dsem=D_sm)
        for job in ([] if skip_conv else conv_jobs(0)):
            do_conv_job(job, smalls, B_smalls)

        for l in range(NL):
            nxt_jobs = conv_jobs(l + 1) if l + 1 < NL else []
            per_tile = (len(nxt_jobs) + NT - 1) // NT if nxt_jobs else 0
            for t in range(NT):
                i0 = len(steps)
                tile_steps(l, t, l == 0, l == NL - 1)
                if nxt_jobs:
                    jl = nxt_jobs[t * per_tile:(t + 1) * per_tile]
                    tile = steps[i0:]
                    del steps[i0:]
                    every = max(1, (len(tile) - 8) // max(1, len(jl)))
                    ji = 0
                    for si_, st_ in enumerate(tile):
                        steps.append(st_)
                        if si_ >= 6 and (si_ - 6) % every == 0 and ji < len(jl):
                            steps.append(("conv", [jl[ji]], l + 1))
                            ji += 1
                    if ji < len(jl):
                        steps.append(("conv", jl[ji:], l + 1))

        wsteps = [i for i, s_ in enumerate(steps) if s_[0] is not None and s_[0] != "conv"]
        wpos = {i: k for k, i in enumerate(wsteps)}
        issued = [0]

        def issue_loads(upto):
            while issued[0] < len(wsteps) and issued[0] <= upto:
                k = issued[0]
                gi = steps[wsteps[k]][0]
                b = BLOCKS[gi % NBLK]
                width = b["kcb"] * b["ncb"]
                s = k % NSLOT
                A("sp", lambda h, gi=gi, s=s, width=width: h.dma_start(out=wsl[s][:, 0:width], in_=wscr[gi][:, 0:width]),
                  reads=[B_wscr[gi]], writes=[B_wsl[s]], dsem=D_wsl[s])
                if b["bias"]:
                    nb_ = b["ncb"]
                    A("sp", lambda h, gi=gi, s=s, nb_=nb_: h.dma_start(out=wsl[s][0:1, BIASOFF:BIASOFF + nb_], in_=wscr[gi][0:1, BIASOFF:BIASOFF + nb_]),
                      reads=[B_wscr[gi]], writes=[B_wsl[s]], dsem=D_wsl[s])
                issued[0] += 1

        smalls2 = sb("smalls2", [P, NSM], F32)
        B_smalls2 = Buf("smalls2")
        D_sm2 = newdsem("sm2")
        conv_layer_loaded = [-1]
        if max_steps is not None:
            steps = steps[:max_steps]
            wsteps = [i for i, s_ in enumerate(steps) if s_[0] is not None and s_[0] != "conv"]
            wpos = {i: k for k, i in enumerate(wsteps)}
        for i, s_ in enumerate(steps):
            if s_[0] == "conv":
                _, jl, ln = s_
                if conv_layer_loaded[0] != ln:
                    A("sp", lambda h, ln=ln: h.dma_start(out=smalls2[:], in_=smalls_d[ln]), writes=[B_smalls2], dsem=D_sm2)
                    conv_layer_loaded[0] = ln
                for job in jl:
                    do_conv_job(job, smalls2, B_smalls2)
                continue
            if s_[0] is None:
                s_[1](None, None)
            else:
                k = wpos[i]
                issue_loads(k + NSLOT - 1)
                slot = k % NSLOT
                s_[1](wsl[slot], B_wsl[slot])

        fin = [(d.sem, d.count) for d in D_xtok + [D_xs] if d.count > 0]
        S_.wait_only("sp", fin)
        S_.emit()
    return nc


def _smalls(inp, NLW):
    sm = np.zeros((NLW, P, NSM), np.float32)

    def colmajor(v):
        return np.ascontiguousarray(v.reshape(-1, P).T)

    for l in range(NLW):
        b = inp["b_in"][l]
        sm[l, :, SM["bqk"]:SM["bqk"] + 16] = colmajor(b[0:2048])
        for h in range(4):
            sm[l, :, SM["brqk"] + h * 4:SM["brqk"] + h * 4 + 2] = colmajor(b[RQ + h * 256:RQ + (h + 1) * 256])
            sm[l, :, SM["brqk"] + h * 4 + 2:SM["brqk"] + h * 4 + 4] = colmajor(b[RK + h * 256:RK + (h + 1) * 256])
        sm[l, :, SM["bg"]:SM["bg"] + 8] = colmajor(b[GM:GM + 1024])
        sm[l, :, SM["bg"] + 8:SM["bg"] + 16] = colmajor(b[GR:GR + 1024])
        cw = inp["conv_w"][l]
        for c in range(16):
            sm[l, :, SM["cw"] + c * 4:SM["cw"] + c * 4 + 4] = cw[:, c * P:(c + 1) * P].T
        sm[l, :, SM["cb"]:SM["cb"] + 16] = colmajor(inp["conv_b"][l])
        sm[l, :, SM["bf1"]:SM["bf1"] + 32] = colmajor(inp["b_ff1"][l])
        sm[l, :, SM["bf2"]:SM["bf2"] + 8] = colmajor(inp["b_ff2"][l])
        sm[l, :, SM["g1"]:SM["g1"] + 8] = colmajor(inp["norm1_g"][l])
        sm[l, :, SM["gmn"]:SM["gmn"] + 8] = colmajor(inp["m_norm_g"][l])
        sm[l, :, SM["grn"]:SM["grn"] + 16] = colmajor(inp["r_norm_g"][l])
        sm[l, :, SM["g2"]:SM["g2"] + 8] = colmajor(inp["norm2_g"][l])
        sm[l, :, SM["g3"]:SM["g3"] + 8] = colmajor(inp["norm3_g"][l])
        sm[l, :, SM["bgate"]:SM["bgate"] + 8] = b[MI:MI + 8][None, :]
        sm[l, :, SM["fg"]:SM["fg"] + 8] = colmajor(inp["final_g"])
    return sm


def _consts():
    c = np.zeros((P, NCST), np.float64)
    idx = np.arange(P)
    c[:, C_ID:C_ID + P] = np.eye(P)
    tri = (idx[:, None] <= idx[None, :]).astype(np.float64)
    c[:, C_TRIU:C_TRIU + P] = tri
    c[:, C_ONES:C_ONES + P] = 1.0
    c[:, C_MASKM:C_MASKM + P] = tri / 16.0
    for h in range(4):
        g = 1.0 - 2.0 ** (-5.0 - h)
        c[:, C_MASKR + h * P:C_MASKR + (h + 1) * P] = tri * (g ** (-(idx[:, None] + 1.0))) / 16.0
        c[:, C_RSC + h] = g ** (idx + 1.0)
        c[:, C_KDEC + h] = g ** (127.0 - idx) / 16.0
    return c.astype(np.float32)


def _rope_tables(S):
    pos = np.arange(S, dtype=np.float32)
    inv_freq = (10000.0 ** (-np.arange(0, 256, 2, dtype=np.float32) / np.float32(256))).astype(np.float32)
    ang = (pos[None, :] * inv_freq[:, None]).astype(np.float32)
    return np.cos(ang).astype(np.float32), np.sin(ang).astype(np.float32)


_WNAMES = ("w_in", "w_bm", "w_br", "w_out", "w_ff1", "w_ff2", "w_pe_gate", "w_pe")


def run_model(inp, NL, TT=256, n_cores=8, **bkw):
    x = np.asarray(inp["x"], np.float32)
    p = np.asarray(inp["p"], np.float32)
    B, S, _ = x.shape
    NLW = inp["w_in"].shape[0]
    nc = build(NL, S, TT=TT, NLW=NLW, **bkw)
    sm = _smalls(inp, NLW)
    cst = _consts()
    cos_t, sin_t = _rope_tables(S)
    shared = {k: np.ascontiguousarray(np.asarray(inp[k], np.float32)) for k in _WNAMES}
    shared["b_in"] = np.ascontiguousarray(np.asarray(inp["b_in"], np.float32))
    shared.update(smalls=sm, consts=cst, cos_t=cos_t, sin_t=sin_t)
    in_maps = []
    place = list(range(B))
    zeros = None
    if n_cores == 8 and B == 4:
        place = [0, 1, 4, 5]
        zeros = {k: np.zeros_like(v) for k, v in shared.items()}
        zeros["x"] = np.zeros_like(x[0])
        zeros["p"] = np.zeros_like(np.ascontiguousarray(p[:, 0]))
    for c in range(n_cores):
        if c in place:
            b = place.index(c)
            m = dict(shared)
            m["x"] = np.ascontiguousarray(x[b])
            m["p"] = np.ascontiguousarray(p[:, b])
        elif zeros is not None:
            m = dict(zeros)
        else:
            b = c % B
            m = dict(shared)
            m["x"] = np.ascontiguousarray(x[b])
            m["p"] = np.ascontiguousarray(p[:, b])
        in_maps.append(m)
    res = run_bass_kernel_spmd(nc, in_maps, core_ids=list(range(n_cores)))
    out = np.stack([res.results[place[b]]["out"] for b in range(B)], axis=0)
    return out.astype(np.float32)


def kernel(**inputs):
    return run_model(inputs, NL=4, TT=512, n_cores=8)
```
